# Optimizing a Trainium2 kernel written in Bass

```python
import math
import jax, jax.numpy as jnp
from jax import lax
import numpy as np

D_MODEL = 1024
BATCH = 4
SEQ = 4096
DEPTH = 1

HEAD_DIM = 64
A_HEADS = 4
A_VDIM = 2 * HEAD_DIM
A_WIDTH = A_HEADS * A_VDIM
B_HEADS = 8
B_WIDTH = B_HEADS * HEAD_DIM
B_BLOCK = 256
B_TOPK = 3
Q_BLOCK = 128
MOBA_Q_BLOCK = 32
D_FF = 2816
CONV_W = 3
ROPE_THETA = 10000.0
EPS = 1e-6
GATE_WIDTH = 2 * D_MODEL
IN_COLS = 3 * A_WIDTH + 3 * B_WIDTH + GATE_WIDTH
SPLITS = (A_WIDTH, 2 * A_WIDTH, 3 * A_WIDTH,
          3 * A_WIDTH + B_WIDTH, 3 * A_WIDTH + 2 * B_WIDTH, 3 * A_WIDTH + 3 * B_WIDTH)

kernel_name = "hybrid_diffattn_moba_convffn"


def rms_norm(x, g):
    xf = x.astype(jnp.float32)
    y = xf * lax.rsqrt(jnp.mean(xf * xf, axis=-1, keepdims=True) + EPS)
    return (y * g.astype(jnp.float32)).astype(x.dtype)


def rope_tables(seq):
    inv = 1.0 / (ROPE_THETA ** (jnp.arange(0, HEAD_DIM, 2, dtype=jnp.float32) / HEAD_DIM))
    ang = jnp.arange(seq, dtype=jnp.float32)[:, None] * inv[None, :]
    return jnp.cos(ang), jnp.sin(ang)


def apply_rope(x, cos, sin):
    x1, x2 = jnp.split(x, 2, axis=-1)
    c = cos.astype(x.dtype)
    s = sin.astype(x.dtype)
    return jnp.concatenate([x1 * c - x2 * s, x2 * c + x1 * s], axis=-1)


def diff_attention(q, k, v, lam):
    Bsz, H, _, S, _ = q.shape
    VD = v.shape[-1]
    scale = HEAD_DIM ** -0.5
    kpos = jnp.arange(S)

    def one_block(i):
        start = i * Q_BLOCK
        qb = lax.dynamic_slice_in_dim(q, start, Q_BLOCK, axis=3)
        s = jnp.einsum('bhcqd,bhckd->bhcqk', qb, k).astype(jnp.float32) * scale
        qpos = start + jnp.arange(Q_BLOCK)
        s = jnp.where(kpos[None, :] <= qpos[:, None], s, -jnp.inf)
        p = jax.nn.softmax(s, axis=-1)
        a = p[:, :, 0] - lam * p[:, :, 1]
        return jnp.einsum('bhqk,bhkv->bhqv', a.astype(v.dtype), v)

    out = lax.map(one_block, jnp.arange(S // Q_BLOCK))
    return jnp.moveaxis(out, 0, 2).reshape(Bsz, H, S, VD)


def moba_attention(q, k, v):
    Bsz, H, S, D = q.shape
    nb = -(-S // B_BLOCK)
    pad = nb * B_BLOCK - S
    kp = jnp.pad(k, ((0, 0), (0, 0), (0, pad), (0, 0)))
    vp = jnp.pad(v, ((0, 0), (0, 0), (0, pad), (0, 0)))
    kblk = kp.reshape(Bsz, H, nb, B_BLOCK, D)
    vblk = vp.reshape(Bsz, H, nb, B_BLOCK, D)
    kmean = jnp.mean(kblk.astype(jnp.float32), axis=3)
    n_sel = min(B_TOPK, max(nb - 1, 1))
    blk_ids = jnp.arange(nb)
    b_idx = jnp.arange(Bsz)[:, None, None, None]
    h_idx = jnp.arange(H)[None, :, None, None]
    scale = D ** -0.5

    def one_block(i):
        start = i * MOBA_Q_BLOCK
        qb = lax.dynamic_slice_in_dim(q, start, MOBA_Q_BLOCK, axis=2)
        qpos = start + jnp.arange(MOBA_Q_BLOCK)
        own = start // B_BLOCK
        g = jnp.einsum('bhqd,bhnd->bhqn', qb.astype(jnp.float32), kmean)
        g = jnp.where(blk_ids[None, :] < own, g, -jnp.inf)
        _, sel = lax.top_k(g, n_sel)
        valid = jnp.arange(n_sel) < own
        kg = kblk[b_idx, h_idx, sel]
        vg = vblk[b_idx, h_idx, sel]
        s_sel = jnp.einsum('bhqd,bhqnkd->bhqnk', qb, kg).astype(jnp.float32) * scale
        s_sel = jnp.where(valid[None, None, None, :, None], s_sel, -jnp.inf)
        s_sel = s_sel.reshape(Bsz, H, MOBA_Q_BLOCK, n_sel * B_BLOCK)
        k_own = lax.dynamic_slice_in_dim(kp, own * B_BLOCK, B_BLOCK, axis=2)
        v_own = lax.dynamic_slice_in_dim(vp, own * B_BLOCK, B_BLOCK, axis=2)
        s_own = jnp.einsum('bhqd,bhkd->bhqk', qb, k_own).astype(jnp.float32) * scale
        kpos_own = own * B_BLOCK + jnp.arange(B_BLOCK)
        s_own = jnp.where(kpos_own[None, :] <= qpos[:, None], s_own, -jnp.inf)
        p = jax.nn.softmax(jnp.concatenate([s_sel, s_own], axis=-1), axis=-1).astype(v.dtype)
        p_sel = p[..., :n_sel * B_BLOCK].reshape(Bsz, H, MOBA_Q_BLOCK, n_sel, B_BLOCK)
        p_own = p[..., n_sel * B_BLOCK:]
        return (jnp.einsum('bhqnk,bhqnkd->bhqd', p_sel, vg)
                + jnp.einsum('bhqk,bhkd->bhqd', p_own, v_own))

    out = lax.map(one_block, jnp.arange(S // MOBA_Q_BLOCK))
    return jnp.moveaxis(out, 0, 2).reshape(Bsz, H, S, D)


def causal_dwconv(u, w, b):
    S = u.shape[1]
    up = jnp.pad(u, ((0, 0), (CONV_W - 1, 0), (0, 0)))
    y = b
    for j in range(CONV_W):
        y = y + up[:, j:j + S] * w[j]
    return y


def setup_inputs(seed: int = 0) -> dict:
    key = jax.random.key(seed)
    ks = jax.random.split(key, 24)
    n = jax.random.normal
    f32 = jnp.float32
    L, D = DEPTH, D_MODEL
    return {
        "x": n(ks[0], (BATCH, SEQ, D), f32),
        "norm1_g": 1.0 + 0.02 * n(ks[1], (L, D), f32),
        "w_in": n(ks[2], (L, D, IN_COLS), f32) * D ** -0.5,
        "b_gate": 0.02 * n(ks[3], (L, GATE_WIDTH), f32),
        "qn_a": 1.0 + 0.02 * n(ks[4], (L, HEAD_DIM), f32),
        "kn_a": 1.0 + 0.02 * n(ks[5], (L, HEAD_DIM), f32),
        "lam_q1": 0.1 * n(ks[6], (L, HEAD_DIM), f32),
        "lam_k1": 0.1 * n(ks[7], (L, HEAD_DIM), f32),
        "lam_q2": 0.1 * n(ks[8], (L, HEAD_DIM), f32),
        "lam_k2": 0.1 * n(ks[9], (L, HEAD_DIM), f32),
        "subln_g": 1.0 + 0.02 * n(ks[10], (L, A_VDIM), f32),
        "qn_b": 1.0 + 0.02 * n(ks[11], (L, HEAD_DIM), f32),
        "kn_b": 1.0 + 0.02 * n(ks[12], (L, HEAD_DIM), f32),
        "w_a_proj": n(ks[13], (L, A_WIDTH, D), f32) * A_WIDTH ** -0.5,
        "w_b_proj": n(ks[14], (L, B_WIDTH, D), f32) * B_WIDTH ** -0.5,
        "w_out": n(ks[15], (L, D, D), f32) * D ** -0.5,
        "norm2_g": 1.0 + 0.02 * n(ks[16], (L, D), f32),
        "w_up": n(ks[17], (L, D, 2 * D_FF), f32) * D ** -0.5,
        "conv_w": n(ks[18], (L, CONV_W, D_FF), f32) * CONV_W ** -0.5,
        "conv_b": 0.02 * n(ks[19], (L, D_FF), f32),
        "w_down": n(ks[20], (L, D_FF, D), f32) * D_FF ** -0.5,
    }


def reference(x, norm1_g, w_in, b_gate, qn_a, kn_a, lam_q1, lam_k1, lam_q2, lam_k2, subln_g,
              qn_b, kn_b, w_a_proj, w_b_proj, w_out, norm2_g, w_up, conv_w, conv_b, w_down):
    Bsz, S, _ = x.shape
    cos, sin = rope_tables(S)
    for l in range(DEPTH):
        lam_init = 0.8 - 0.6 * math.exp(-0.3 * l)
        h = rms_norm(x, norm1_g[l])
        proj = h @ w_in[l]
        qa, ka, va, qm, km, vm, gates = jnp.split(proj, SPLITS, axis=-1)
        ga, gb = jnp.split(gates + b_gate[l], 2, axis=-1)

        qa = qa.reshape(Bsz, S, A_HEADS, 2, HEAD_DIM).transpose(0, 2, 3, 1, 4)
        ka = ka.reshape(Bsz, S, A_HEADS, 2, HEAD_DIM).transpose(0, 2, 3, 1, 4)
        qa = apply_rope(rms_norm(qa, qn_a[l]), cos, sin)
        ka = apply_rope(rms_norm(ka, kn_a[l]), cos, sin)
        va = va.reshape(Bsz, S, A_HEADS, A_VDIM).transpose(0, 2, 1, 3)
        lam = (jnp.exp(jnp.sum(lam_q1[l].astype(jnp.float32) * lam_k1[l].astype(jnp.float32)))
               - jnp.exp(jnp.sum(lam_q2[l].astype(jnp.float32) * lam_k2[l].astype(jnp.float32)))
               + lam_init)
        oa = diff_attention(qa, ka, va, lam)
        oa = rms_norm(oa, subln_g[l]) * (1.0 - lam_init)
        oa = oa.transpose(0, 2, 1, 3).reshape(Bsz, S, A_WIDTH)

        qm = qm.reshape(Bsz, S, B_HEADS, HEAD_DIM).transpose(0, 2, 1, 3)
        km = km.reshape(Bsz, S, B_HEADS, HEAD_DIM).transpose(0, 2, 1, 3)
        vm = vm.reshape(Bsz, S, B_HEADS, HEAD_DIM).transpose(0, 2, 1, 3)
        qm = apply_rope(rms_norm(qm, qn_b[l]), cos, sin)
        km = apply_rope(rms_norm(km, kn_b[l]), cos, sin)
        ob = moba_attention(qm, km, vm).transpose(0, 2, 1, 3).reshape(Bsz, S, B_WIDTH)

        merged = jax.nn.sigmoid(ga) * (oa @ w_a_proj[l]) + jax.nn.sigmoid(gb) * (ob @ w_b_proj[l])
        x = x + merged @ w_out[l]

        h2 = rms_norm(x, norm2_g[l])
        u, g = jnp.split(h2 @ w_up[l], 2, axis=-1)
        u = causal_dwconv(u, conv_w[l], conv_b[l])
        x = x + (jax.nn.gelu(u) * g) @ w_down[l]
    return x
```

```python
import numpy as np
import ml_dtypes
from contextlib import ExitStack
import concourse.bass as bass
import concourse.mybir as mybir
from concourse.bass_utils import run_bass_kernel_spmd

F32 = mybir.dt.float32
BF16 = mybir.dt.bfloat16
AF = mybir.ActivationFunctionType
ALU = mybir.AluOpType
AX = mybir.AxisListType

S = 4096
D = 1024
NG = 8
GW = 512
DFF = 2816
NFB = 22
EPS = 1e-6
BIGM = 30000.0
BIGB = 32768.0
ENGS = ["sync", "scalar", "vector", "gpsimd", "tensor"]
DEBUG = False
STOP = None
MAXG = 8
NOV = False
NOK = False
NOPREF = False
SKIP0 = False
S0JOBS = 13
STQ = 'gpsimd'
NOPIPE = False
USE_SQRT = False


def I(name, *args, **kw):
    return (name, args, kw)


class Op:
    __slots__ = ("eng", "fn", "deps", "dma", "sem", "val", "inc", "waits")


class Prog:
    def __init__(self, nc, es):
        self.nc = nc
        self.es = es
        self.ops = {e: [] for e in ENGS}
        self.lastw = {}
        self.readers = {}
        self.pending = {e: None for e in ENGS}
        self.final_dma = []

    def op(self, eng, fn, reads=(), writes=(), dma=False):
        o = Op()
        o.eng, o.fn, o.dma = eng, fn, dma
        o.inc = dma
        deps = set()
        for r in reads:
            w = self.lastw.get(r)
            if w is not None:
                deps.add(w)
            if r.startswith("ps") or r.startswith("ST"):
                for rd in self.readers.get(r, ()):
                    if rd.eng != eng:
                        deps.add(rd)
        for w_ in writes:
            w = self.lastw.get(w_)
            if w is not None:
                deps.add(w)
            for rd in self.readers.get(w_, ()):
                deps.add(rd)
        if self.pending[eng] is not None:
            deps.update(self.pending[eng])
            self.pending[eng] = None
        o.deps = [d for d in deps if not (d.eng == "tensor" and eng == "tensor" and not d.dma and not dma)]
        for r in reads:
            self.readers.setdefault(r, []).append(o)
        for w_ in writes:
            self.lastw[w_] = o
            self.readers[w_] = []
        self.ops[eng].append(o)
        return o

    def barrier(self):
        deps = []
        for e in ENGS:
            lastc = None
            for o in reversed(self.ops[e]):
                if not o.dma:
                    lastc = o
                    break
            if lastc is not None:
                deps.append(lastc)
            cnt = 0
            for o in reversed(self.ops[e]):
                if o.dma:
                    deps.append(o)
                    cnt += 1
                    if cnt >= 8:
                        break
        for e in ENGS:
            self.pending[e] = list(deps)
        self.lastw = {}
        self.readers = {}

    def finalize(self):
        nc, es = self.nc, self.es
        for e in ENGS:
            for o in self.ops[e]:
                for d in o.deps:
                    d.inc = True
        self.csem = {}
        self.dsem = {}
        for e in ENGS:
            self.csem[e] = [es.enter_context(nc.semaphore("c_%s_%d" % (e, i))) for i in range(2)]
            self.dsem[e] = [es.enter_context(nc.semaphore("d_%s_%d" % (e, i))) for i in range(8)]
        LIM = 30000
        for e in ENGS:
            cc = 0
            dcount = [0] * 8
            di = 0
            for o in self.ops[e]:
                o.waits = []
                if o.dma:
                    k = di % 8
                    di += 1
                    dcount[k] += 1
                    o.sem = self.dsem[e][k]
                    o.val = 16 * dcount[k]
                    if dcount[k] > 1:
                        o.waits.append((o.sem, o.val - 16))
                elif o.inc:
                    si = cc // LIM
                    o.sem = self.csem[e][si]
                    o.val = cc % LIM + 1
                    cc += 1
            assert cc < 2 * LIM, (e, cc)
            self.final_dma.append((e, [(self.dsem[e][k], 16 * dcount[k]) for k in range(8) if dcount[k] > 0]))
        for e in ENGS:
            for o in self.ops[e]:
                for d in o.deps:
                    o.waits.append((d.sem, d.val))

    def emit(self):
        nc = self.nc
        fin = dict(self.final_dma)
        with nc.Block() as block:
            for eng in ENGS:
                def body(e, eng=eng):
                    waited = {}
                    for o in self.ops[eng]:
                        for (sem, val) in o.waits:
                            if waited.get(id(sem), 0) < val:
                                e.wait_ge(sem, val)
                                waited[id(sem)] = val
                        inst = getattr(e, o.fn[0])(*o.fn[1], **o.fn[2])
                        if o.inc:
                            inst.then_inc(o.sem, 16 if o.dma else 1)
                    for (sem, val) in fin[eng]:
                        if waited.get(id(sem), 0) < val:
                            e.wait_ge(sem, val)
                getattr(block, eng)(body)


NQ = 2056
GROUPS = [[0, 3, 4, 7], [1, 2, 5, 6]]
SLOTS = [(0, 512, 8), (512, 512, 16), (1024, 512, 24), (1536, 512, 32)]
MINI = (2048, 8, 28)
CHUNKS = SLOTS + [MINI]


def _wlayout():
    off = {}
    cur = [0]

    def add(name, n):
        off[name] = (cur[0], n)
        cur[0] += n
    add("wk", 8 * 512)
    add("wv", 8 * 512)
    add("wq", 8 * 512)
    for hp in range(2):
        add("wkm%d" % hp, 8 * 256)
        add("wvm%d" % hp, 8 * 256)
        add("wqm%d" % hp, 8 * 256)
    for p in range(4):
        add("wga%d" % p, 8 * 256)
        add("wgb%d" % p, 8 * 256)
    add("wa", 4 * 1024)
    add("wb", 4 * 1024)
    add("wo", 8 * 1024)
    for nm in ("wupAu", "wupAg", "wupBu", "wupBg"):
        add(nm, 8 * 1408)
    for d in range(8):
        add("wd%d" % d, NFB * 128)
    return off, cur[0]


WOFF, WTOT = _wlayout()


def build_program():
    nc = bass.Bass("TRN2", target_bir_lowering=False)

    def din(name, shape, dt=F32):
        return nc.dram_tensor(name, list(shape), dt, kind="ExternalInput").ap()

    xT = din("xT", [D, S])
    xq = din("xq", [D, NQ])
    wpack = din("wpack", [128, WTOT])

    def wsrc(name):
        o, n = WOFF[name]
        return wpack[:, o:o + n]
    g1_d = din("g1", [128, 8])
    g2_d = din("g2", [128, 8])
    bg_d = din("bg", [128, 16])
    cw_d = din("cw", [128, NFB * 3])
    cb_d = din("cb", [128, NFB])
    gn_d = din("gn", [128, 4])
    gsub_d = din("gsub", [128, 1])
    lam_d = din("lamv", [128, 4 * 64])
    cosk_d = din("cosk", [128, S])
    sink_d = din("sink", [128, S])
    cosq_d = din("cosq", [128, NQ])
    sinq_d = din("sinq", [128, NQ])
    cmat_d = din("cmat", [128, 6 * 128], BF16)
    amask_d = din("amask", [128, 2 * 8 * GW], BF16)
    hmask_d = din("hmask", [128, 32 * 8], BF16)
    gmask_d = din("gmask", [128, 17 * 16])
    valid_d = din("valid", [128, 17 * 16])
    hflag_d = din("hflag", [128, 4])
    oh_d = din("oh16", [16, S], BF16)

    skind = "ExternalOutput" if DEBUG else "Internal"
    outT = nc.dram_tensor("outT", [D, 2048], F32, kind="ExternalOutput").ap()
    hT_d = nc.dram_tensor("hT_d", [D, S], BF16, kind=skind).ap()
    hq_d = nc.dram_tensor("hq_d", [D, NQ], BF16, kind=skind).ap()
    yT_d = nc.dram_tensor("yT_d", [512, NQ], F32, kind=skind).ap()
    obT_d = nc.dram_tensor("obT_d", [512, NQ], BF16, kind=skind).ap()
    x1_d = nc.dram_tensor("x1_d", [D, NQ], F32, kind=skind).ap()

    def cv(ap):
        return ap.rearrange("(c p) t -> p c t", p=128)

    xT_v, xq_v, hT_v, hq_v, yT_v, obT_v, x1_v, outT_v = (cv(a) for a in (xT, xq, hT_d, hq_d, yT_d, obT_d, x1_d, outT))

    with ExitStack() as es:
        P = Prog(nc, es)
        uid = [0]

        def sb(st, name, shape, dt):
            uid[0] += 1
            return st.enter_context(nc.sbuf_tensor("sb%d_%s" % (uid[0], name), list(shape), dt))

        cmat = sb(es, "cmat", [128, 6, 128], BF16)
        ident, perm, bd64, on1024, on128, ones = (cmat[:, i, :] for i in range(6))
        g1 = sb(es, "g1", [128, 8], F32)
        g2 = sb(es, "g2", [128, 8], F32)
        bg = sb(es, "bg", [128, 16], F32)
        cw = sb(es, "cw", [128, NFB, 3], F32)
        cb = sb(es, "cb", [128, NFB], F32)
        gn = sb(es, "gn", [128, 4], F32)
        gsub = sb(es, "gsub", [128, 1], F32)
        lamv = sb(es, "lamv", [128, 4, 64], F32)
        lamt = sb(es, "lamt", [128, 2, 64], F32)
        lams = sb(es, "lams", [128, 2], F32)
        neglam = sb(es, "neglam", [128, 1], F32)
        epsc = sb(es, "epsc", [128, 1], F32)
        hflag = sb(es, "hflag", [128, 4], F32)
        pp = [es.enter_context(nc.psum_tensor("pp%d" % i, [128, 2 * GW], F32)) for i in range(4)]
        pk = ["ps%d" % i for i in range(8)]

        STP = [0, 1, 3]
        STK = [["ps0", "ps1"], ["ps2", "ps3"], ["ps6", "ps7"]]

        def B(i, c0=0, c1=GW, p0=0, p1=128):
            off = (i % 2) * GW
            return pp[i // 2][p0:p1, off + c0:off + c1]

        def dma(eng, out, in_, reads, writes):
            return P.op(eng, I("dma_start", out=out, in_=in_), reads, writes, dma=True)

        def mm(out, lhsT, rhs, start, stop, reads, writes):
            return P.op("tensor", I("matmul", out, lhsT, rhs, start=start, stop=stop), reads, writes)

        dma("sync", cmat[:].rearrange("p a b -> p (a b)"), cmat_d, [], ["cmat"])
        dma("sync", g1[:], g1_d, [], ["g1"])
        dma("sync", g2[:], g2_d, [], ["g2"])
        dma("sync", bg[:], bg_d, [], ["bg"])
        dma("sync", cw[:].rearrange("p a b -> p (a b)"), cw_d, [], ["cw"])
        dma("sync", cb[:], cb_d, [], ["cb"])
        dma("sync", gn[:], gn_d, [], ["gn"])
        dma("sync", gsub[:], gsub_d, [], ["gsub"])
        dma("sync", hflag[:], hflag_d, [], ["hflag"])
        dma("sync", lamv[:].rearrange("p a b -> p (a b)"), lam_d, [], ["lamv"])
        P.op("vector", I("memset", epsc[:], EPS), [], ["epsc"])
        P.op("vector", I("tensor_tensor", out=lamt[:, 0, :], in0=lamv[:, 0, :], in1=lamv[:, 1, :], op=ALU.mult), ["lamv"], ["lamt0"])
        P.op("vector", I("tensor_tensor", out=lamt[:, 1, :], in0=lamv[:, 2, :], in1=lamv[:, 3, :], op=ALU.mult), ["lamv"], ["lamt1"])
        P.op("vector", I("reduce_sum", out=lams[:], in_=lamt[:], axis=AX.X), ["lamt0", "lamt1"], ["lams"])
        P.op("scalar", I("activation", out=lams[:], in_=lams[:], func=AF.Exp), ["lams"], ["lams"])
        P.op("vector", I("scalar_tensor_tensor", out=neglam[:], in0=lams[:, 1:2], scalar=-0.2, in1=lams[:, 0:1],
                         op0=ALU.add, op1=ALU.subtract), ["lams"], ["neglam"])
        P.op("vector", I("tensor_scalar", out=gsub[:], in0=gsub[:], scalar1=0.8, scalar2=None, op0=ALU.mult), ["gsub"], ["gsub"])

        def rstd_from(bank, n, lnr, rstd, key):
            if USE_SQRT:
                P.op("scalar", I("activation", out=lnr[:, 0:n], in_=B(bank, 0, n), func=AF.Sqrt, bias=epsc[:]), [pk[bank], "epsc"], [key + "ln"])
                P.op("vector", I("reciprocal", out=rstd[:, 0:n], in_=lnr[:, 0:n]), [key + "ln"], [key])
                return
            P.op("scalar", I("activation", out=lnr[:, 0:n], in_=B(bank, 0, n), func=AF.Ln, bias=epsc[:]), [pk[bank], "epsc"], [key + "ln"])
            P.op("scalar", I("activation", out=rstd[:, 0:n], in_=lnr[:, 0:n], func=AF.Exp, scale=-0.5), [key + "ln"], [key])

        def rms_chunk(tag, src, skey, n, sq, lnr, rstd, bank):
            P.op("scalar", I("activation", out=sq[:, :, 0:n], in_=src[:, :, 0:n], func=AF.Square), [skey], [tag + "sq"])
            for c in range(8):
                mm(B(bank, 0, n), on1024, sq[:, c, 0:n], c == 0, c == 7, [tag + "sq", "cmat"], [pk[bank]])
            rstd_from(bank, n, lnr, rstd, tag + "rstd")

        class NR:
            def __init__(self, st, name):
                self.bufs = []
                for i in range(2):
                    self.bufs.append(dict(
                        sq=sb(st, name + "sq%d" % i, [128, GW], BF16), pbf=sb(st, name + "pbf%d" % i, [128, GW], BF16),
                        t1=sb(st, name + "t1%d" % i, [128, GW], F32), t2=sb(st, name + "t2%d" % i, [128, GW], F32),
                        ln=sb(st, name + "ln%d" % i, [128, GW], F32), rs=sb(st, name + "rs%d" % i, [128, GW], F32)))
                self.k = 0
                self.pending = None

            def flush(self):
                if self.pending is not None:
                    f = self.pending
                    self.pending = None
                    f()

            def block(self, bank, n, gcol, cos_ap, sin_ap, outs, ckey="cos", skey="sin"):
                i = self.k % 2
                self.k += 1
                bf = self.bufs[i]
                t = "nr%d" % i
                pm, psw = (2, 3) if i == 0 else (4, 5)
                pa = B(bank, 0, n)
                P.op("scalar", I("activation", out=bf["sq"][:, 0:n], in_=pa, func=AF.Square), [pk[bank]], [t + "sq"])
                P.op("scalar", I("activation", out=bf["pbf"][:, 0:n], in_=pa, func=AF.Copy, scale=gcol), [pk[bank], "gn"], [t + "pbf"])
                P.op("vector", I("scalar_tensor_tensor", out=bf["t1"][:, 0:n], in0=pa, scalar=gcol, in1=cos_ap, op0=ALU.mult, op1=ALU.mult),
                     [pk[bank], "gn", ckey, t + "pbf", t + "sq"], [t + "t1"])
                prev = self.pending

                def phaseB():
                    mm(B(pm, 0, n), bd64, bf["sq"][:, 0:n], True, True, [t + "sq", "cmat"], [pk[pm]])
                    mm(B(psw, 0, n), perm, bf["pbf"][:, 0:n], True, True, [t + "pbf", "cmat"], [pk[psw]])
                    rstd_from(pm, n, bf["ln"], bf["rs"], t + "rs")
                    P.op("vector", I("tensor_tensor", out=bf["t2"][:, 0:n], in0=B(psw, 0, n), in1=sin_ap, op=ALU.mult), [pk[psw], skey], [t + "t2"])
                    P.op("vector", I("tensor_tensor", out=bf["t1"][:, 0:n], in0=bf["t1"][:, 0:n], in1=bf["t2"][:, 0:n], op=ALU.add),
                         [t + "t1", t + "t2"], [t + "t1"])
                    for (oap, lo, hi, okey) in outs:
                        P.op("vector", I("tensor_tensor", out=oap, in0=bf["t1"][lo:hi, 0:n], in1=bf["rs"][lo:hi, 0:n], op=ALU.mult),
                             [t + "t1", t + "rs"], [okey])
                self.pending = phaseB
                if prev is not None:
                    prev()
                if NOPIPE:
                    self.flush()

        def load_amasks(st):
            amask = sb(st, "amask", [128, 2, 8, GW], BF16)
            hmask = sb(st, "hmask", [128, 32, 8], BF16)
            dma("sync", amask[:].rearrange("p a b c -> p (a b c)"), amask_d, [], ["amask"])
            dma("sync", hmask[:].rearrange("p a b -> p (a b)"), hmask_d, [], ["hmask"])
            return amask, hmask

        stab = ExitStack()
        coskT = sb(stab, "coskT", [128, S], F32)
        sinkT = sb(stab, "sinkT", [128, S], F32)

        with ExitStack() as so:
            KT = [sb(so, "KTd%d" % h, [128, S], BF16) for h in range(4)]
            Vd = sb(so, "Vd", [128, 32, 512], BF16)
            with ExitStack() as st:
                wk = sb(st, "wk", [128, 8, 512], BF16)
                wv = sb(st, "wv", [128, 8, 512], BF16)
                dma("gpsimd", wk[:].rearrange("p c n -> p (c n)"), wsrc("wk"), [], ["wk"])
                dma("gpsimd", wv[:].rearrange("p c n -> p (c n)"), wsrc("wv"), [], ["wv"])
                xg = [sb(st, "s0_xg%d" % i, [128, 8, GW], F32) for i in range(2)]
                sq = sb(st, "s0_sq", [128, 8, GW], BF16)
                lnr = sb(st, "s0_ln", [128, GW], F32)
                rstd2 = [sb(st, "s0_rstd%d" % i, [128, GW], F32) for i in range(2)]
                hT = [sb(st, "s0_hT%d" % i, [128, 8, GW], BF16) for i in range(2)]
                nr = NR(st, "nra")
                jobs = []
                for g in range(NG):
                    jobs.append(("g", g, xT_v, hT_v, g * GW, GW, "hT_d%d" % g))
                    if g < len(CHUNKS):
                        c0_, n_, _ = CHUNKS[g]
                        jobs.append(("o", g, xq_v, hq_v, c0_, n_, "hq_d%d" % g))

                def s0_load(i):
                    _, _, src, _, c0, n, _k = jobs[i]
                    dma("sync", xg[i % 2][:, :, 0:n], src[:, :, c0:c0 + n], [], ["s0xg%d" % (i % 2)])
                def s0_rms(i):
                    kind, g, src, dst, c0, n, okey = jobs[i]
                    b = i % 2
                    P.op("scalar", I("activation", out=sq[:, :, 0:n], in_=xg[b][:, :, 0:n], func=AF.Square), ["s0xg%d" % b], ["s0sq"])
                    for c in range(8):
                        mm(B(7, 0, n), on1024, sq[:, c, 0:n], c == 0, c == 7, ["s0sq", "cmat"], [pk[7]])
                    rstd_from(7, n, lnr, rstd2[b], "s0rstd%d" % b)
                    for c in range(8):
                        P.op("vector", I("scalar_tensor_tensor", out=hT[b][:, c, 0:n], in0=xg[b][:, c, 0:n], scalar=g1[:, c:c + 1],
                                         in1=rstd2[b][:, 0:n], op0=ALU.mult, op1=ALU.mult), ["s0xg%d" % b, "g1", "s0rstd%d" % b], ["s0hT%d_%d" % (b, c)])
                    dma(STQ, dst[:, :, c0:c0 + n], hT[b][:, :, 0:n], ["s0hT%d_%d" % (b, c) for c in range(8)], [okey])

                def s0_kv(i):
                    kind, g, src, dst, c0, n, okey = jobs[i]
                    b = i % 2
                    if kind != "g":
                        return
                    for h in range(4):
                        bank = h % 2
                        for c in range(8):
                            mm(B(bank), wk[:, c, 128 * h:128 * h + 128], hT[b][:, c, :], c == 0, c == 7, ["wk", "s0hT%d_%d" % (b, c)], [pk[bank]])
                        nr.block(bank, GW, gn[:, 1:2], coskT[:, c0:c0 + GW], sinkT[:, c0:c0 + GW],
                                 [(KT[h][:, c0:c0 + GW], 0, 128, "KT%d_%d" % (h, g))], ckey="cosk", skey="sink")
                        for c in range(8):
                            mm(B(6), hT[b][:, c, 128 * h:128 * h + 128], wv[:, c, :], c == 0, c == 7, ["wv", "s0hT%d_%d" % (b, c)], [pk[6]])
                        P.op("scalar", I("activation", out=Vd[:, 4 * g + h, :], in_=B(6), func=AF.Copy), [pk[6]], ["V_%d" % (4 * g + h)])
                s0_load(0)
                if len(jobs) > 1:
                    s0_load(1)
                dma("sync", coskT[:], cosk_d, [], ["cosk"])
                dma("sync", sinkT[:], sink_d, [], ["sink"])
                s0_rms(0)
                for i in range(len(jobs)):
                    if i + 1 < len(jobs):
                        s0_rms(i + 1)
                    if i + 2 < len(jobs):
                        s0_load(i + 2)
                    s0_kv(i)
                nr.flush()
            P.barrier()
            QT = [[sb(so, "QTd%d_%d" % (h, cc), [128, NQ], BF16) for cc in range(2)] for h in range(4)]
            for h in range(4):
                P.op("gpsimd", I("memset", QT[h][0][64:128, :], 0.0), [], ["QTz%d_0" % h])
                P.op("gpsimd", I("memset", QT[h][1][0:64, :], 0.0), [], ["QTz%d_1" % h])
            with ExitStack() as st:
                cosT = sb(st, "cosq", [128, NQ], F32)
                sinT = sb(st, "sinq", [128, NQ], F32)
                dma("sync", cosT[:], cosq_d, [], ["cos"])
                dma("sync", sinT[:], sinq_d, [], ["sin"])
                wq = sb(st, "wq", [128, 8, 512], BF16)
                dma("gpsimd", wq[:].rearrange("p c n -> p (c n)"), wsrc("wq"), [], ["wq"])
                hTg = [sb(st, "hqg%d" % i, [128, 8, GW], BF16) for i in range(2)]
                nr = NR(st, "nrq")
                dma("sync", hTg[0][:, :, 0:GW], hq_v[:, :, 0:GW], [], ["hTg0"])
                for ci, (c0, n, _) in enumerate(CHUNKS):
                    b = ci % 2
                    if ci + 1 < len(CHUNKS):
                        c0n, nn, _ = CHUNKS[ci + 1]
                        dma("sync", hTg[1 - b][:, :, 0:nn], hq_v[:, :, c0n:c0n + nn], [], ["hTg%d" % (1 - b)])
                    for h in range(4):
                        bank = h % 2
                        for c in range(8):
                            mm(B(bank, 0, n), wq[:, c, 128 * h:128 * h + 128], hTg[b][:, c, 0:n], c == 0, c == 7, ["wq", "hTg%d" % b], [pk[bank]])
                        nr.block(bank, n, gn[:, 0:1], cosT[:, c0:c0 + n], sinT[:, c0:c0 + n],
                                 [(QT[h][0][0:64, c0:c0 + n], 0, 64, "QT%d_%d_0" % (h, ci)),
                                  (QT[h][1][64:128, c0:c0 + n], 64, 128, "QT%d_%d_1" % (h, ci))])
                nr.flush()
            P.barrier()
            if STOP == "qa":
                P.finalize()
                P.emit()
                return nc
            with ExitStack() as st:
                amask, hmask = load_amasks(st)
                Pt2 = [sb(st, "Pt2_%d" % i, [128, 2 * GW], BF16) for i in range(3)]
                ep = [dict((nm, sb(st, "ae_%s%d" % (nm, i), [128, GW], F32)) for nm in ("l0", "l1", "o0", "o1")) for i in range(2)]
                dacc = [sb(st, "dacc%d" % i, [128, 2 * GW], F32) for i in range(2)]
                daccb = sb(st, "daccb", [128, 2 * GW], BF16)
                hi_ = 0
                for ci, (c0, n, nkt) in enumerate(CHUNKS):
                    regular = n == GW
                    for h in range(4):
                        if regular:
                            batches = [[(j, 0), (j, 1)] for j in range(nkt)]
                        else:
                            batches = [[(j, cc) for j in range(nkt) for cc in (0, 1)]]
                        nb = len(batches)

                        def emit_S(bi):
                            st_ = bi % 3
                            units = batches[bi]
                            for ui, (j, cc) in enumerate(units):
                                off = ui * n
                                lo, hi = 64 * cc, 64 * cc + 64
                                if regular:
                                    masked = j >= nkt - 8
                                    mk = amask[:, ci % 2, j - (nkt - 8), :] if masked else None
                                else:
                                    masked = True
                                    mk = hmask[:, j, :]
                                oap = pp[STP[st_]][:, off:off + n]
                                mm(oap, KT[h][:, 128 * j:128 * j + 128], QT[h][cc][:, c0:c0 + n], True, not masked,
                                   ["KT%d_%d" % (h, j // 4), "QT%d_%d_%d" % (h, ci, cc), "QTz%d_%d" % (h, cc)], STK[st_])
                                if masked:
                                    mm(oap, ident, mk, False, True, ["cmat", "amask", "hmask"], STK[st_])
                            w = len(units) * n
                            P.op("scalar", I("activation", out=Pt2[st_][:, 0:w], in_=pp[STP[st_]][:, 0:w], func=AF.Exp, scale=0.125),
                                 STK[st_], ["Pt2_%d" % st_])

                        def emit_PV(bi):
                            st_ = bi % 3
                            units = batches[bi]
                            for ui, (j, cc) in enumerate(units):
                                off = ui * n
                                mm(B(4 + cc, 0, n), Vd[:, j, 128 * h:128 * h + 128], Pt2[st_][:, off:off + n], j == 0, j == nkt - 1,
                                   ["V_%d" % j, "Pt2_%d" % st_], [pk[4 + cc]])
                                if not regular:
                                    mm(B(6 + cc, 0, n), ones, Pt2[st_][:, off:off + n], j == 0, j == nkt - 1, ["cmat", "Pt2_%d" % st_], [pk[6 + cc]])
                            if regular:
                                ac = dacc[hi_ % 2]
                                akey = "dacc%d" % (hi_ % 2)
                                for (eng_, lo_, hi_c) in (("vector", 0, 672), ("gpsimd", 672, 2 * GW)):
                                    ak2 = akey + eng_
                                    if units[0][0] == 0:
                                        P.op(eng_, I("tensor_copy", out=ac[:, lo_:hi_c], in_=Pt2[st_][:, lo_:hi_c]), ["Pt2_%d" % st_], [ak2])
                                    else:
                                        P.op(eng_, I("tensor_tensor", out=ac[:, lo_:hi_c], in0=ac[:, lo_:hi_c], in1=Pt2[st_][:, lo_:hi_c], op=ALU.add),
                                             ["Pt2_%d" % st_, ak2], [ak2])
                        for bi in range(min(2, nb)):
                            emit_S(bi)
                        for bi in range(nb):
                            if bi + 2 < nb:
                                emit_S(bi + 2)
                            emit_PV(bi)
                        e_ = ep[hi_ % 2]
                        ek = "ae%d" % (hi_ % 2)
                        if regular:
                            P.op("vector", I("tensor_copy", out=daccb[:], in_=dacc[hi_ % 2][:]),
                                 ["dacc%dvector" % (hi_ % 2), "dacc%dgpsimd" % (hi_ % 2)], ["daccb"])
                            mm(B(6, 0, n), ones, daccb[:, 0:GW], True, True, ["cmat", "daccb"], [pk[6]])
                            mm(B(7, 0, n), ones, daccb[:, GW:2 * GW], True, True, ["cmat", "daccb"], [pk[7]])
                        hi_ += 1
                        P.op("scalar", I("activation", out=e_["l0"][:, 0:n], in_=B(6, 0, n), func=AF.Ln), [pk[6]], [ek + "l0"])
                        P.op("scalar", I("activation", out=e_["l1"][:, 0:n], in_=B(7, 0, n), func=AF.Ln), [pk[7]], [ek + "l1"])
                        P.op("vector", I("tensor_copy", out=e_["o0"][:, 0:n], in_=B(4, 0, n)), [pk[4]], [ek + "o0"])
                        P.op("vector", I("tensor_copy", out=e_["o1"][:, 0:n], in_=B(5, 0, n)), [pk[5]], [ek + "o1"])
                        P.op("scalar", I("activation", out=e_["l0"][:, 0:n], in_=e_["l0"][:, 0:n], func=AF.Exp, scale=-1.0), [ek + "l0"], [ek + "l0"])
                        P.op("scalar", I("activation", out=e_["l1"][:, 0:n], in_=e_["l1"][:, 0:n], func=AF.Exp, scale=-1.0), [ek + "l1"], [ek + "l1"])
                        P.op("vector", I("tensor_tensor", out=e_["o0"][:, 0:n], in0=e_["o0"][:, 0:n], in1=e_["l0"][:, 0:n], op=ALU.mult),
                             [ek + "o0", ek + "l0"], [ek + "o0"])
                        P.op("vector", I("tensor_tensor", out=e_["o1"][:, 0:n], in0=e_["o1"][:, 0:n], in1=e_["l1"][:, 0:n], op=ALU.mult),
                             [ek + "o1", ek + "l1"], [ek + "o1"])
                        P.op("vector", I("scalar_tensor_tensor", out=e_["o0"][:, 0:n], in0=e_["o1"][:, 0:n], scalar=neglam[:], in1=e_["o0"][:, 0:n],
                                         op0=ALU.mult, op1=ALU.add), [ek + "o0", ek + "o1", "neglam"], [ek + "o0"])
                        dma(STQ, yT_v[:, h, c0:c0 + n], e_["o0"][:, 0:n], [ek + "o0"], ["yT_d%d" % ci])
        P.barrier()

        if STOP == "2a":
            P.finalize()
            P.emit()
            return nc
        for hp in range(2):
            with ExitStack() as so:
                KT = [sb(so, "KTm%d" % h, [80, S], BF16) for h in range(4)]
                Vm = sb(so, "Vm", [128, 32, 4, 128], BF16)
                QT = [sb(so, "QTm%d" % h, [80, NQ], BF16) for h in range(4)]
                kms = sb(so, "kms", [64, 4, 16], F32)
                kmT = sb(so, "kmT", [64, 4, 16], BF16)
                with ExitStack() as st:
                    wk = sb(st, "wkm", [128, 8, 256], BF16)
                    wv = sb(st, "wvm", [128, 8, 256], BF16)
                    dma("gpsimd", wk[:].rearrange("p c n -> p (c n)"), wsrc("wkm%d" % hp), [], ["wk"])
                    dma("gpsimd", wv[:].rearrange("p c n -> p (c n)"), wsrc("wvm%d" % hp), [], ["wv"])
                    for h in range(4):
                        dma("sync", KT[h][64:80, :], oh_d, [], ["KToh%d" % h])
                    P.op("gpsimd", I("memset", Vm[:, :, :, 64:128], 1.0), [], ["Vones"])
                    hTg = [sb(st, "hTgm%d" % i, [128, 8, GW], BF16) for i in range(2)]
                    nr = NR(st, "nrkm")
                    dma("sync", hTg[0][:], hT_v[:, :, 0:GW], [], ["hTg0"])
                    cosT = sb(st, "cosqm", [128, NQ], F32)
                    sinT = sb(st, "sinqm", [128, NQ], F32)
                    dma("sync", cosT[:], cosq_d, [], ["cos"])
                    dma("sync", sinT[:], sinq_d, [], ["sin"])
                    wq = sb(st, "wqm", [128, 8, 256], BF16)
                    dma("gpsimd", wq[:].rearrange("p c n -> p (c n)"), wsrc("wqm%d" % hp), [], ["wq"])
                    gmask = sb(st, "gmask", [128, 17, 16], F32)
                    valid = sb(st, "valid", [128, 17, 16], F32)
                    dma("sync", gmask[:].rearrange("p a b -> p (a b)"), gmask_d, [], ["gmask"])
                    dma("sync", valid[:].rearrange("p a b -> p (a b)"), valid_d, [], ["valid"])
                    for g in range(NG):
                        b = g % 2
                        c0 = g * GW
                        if g + 1 < NG:
                            dma("sync", hTg[1 - b][:], hT_v[:, :, c0 + GW:c0 + 2 * GW], [], ["hTg%d" % (1 - b)])
                        for bl in range(2):
                            bank = bl
                            for c in range(8):
                                mm(B(bank), wk[:, c, 128 * bl:128 * bl + 128], hTg[b][:, c, :], c == 0, c == 7, ["wk", "hTg%d" % b], [pk[bank]])
                            nr.block(bank, GW, gn[:, 3:4], coskT[:, c0:c0 + GW], sinkT[:, c0:c0 + GW],
                                     [(KT[2 * bl][0:64, c0:c0 + GW], 0, 64, "KT%d_%d" % (2 * bl, g)),
                                      (KT[2 * bl + 1][0:64, c0:c0 + GW], 64, 128, "KT%d_%d" % (2 * bl + 1, g))], ckey="cosk", skey="sink")
                            if g > 0:
                                for h_ in (2 * bl, 2 * bl + 1):
                                    P.op("vector", I("reduce_sum", out=kms[:, h_, 2 * (g - 1):2 * g],
                                                     in_=KT[h_][0:64, c0 - GW:c0].rearrange("p (n k) -> p n k", k=256), axis=AX.X),
                                         ["KT%d_%d" % (h_, g - 1)], ["kms%d_%d" % (h_, g - 1)])
                        for j in range(4):
                            bank = 6 + j % 2
                            for c in range(8):
                                mm(B(bank, 0, 256), hTg[b][:, c, 128 * j:128 * j + 128], wv[:, c, :], c == 0, c == 7, ["wv", "hTg%d" % b], [pk[bank]])
                            if j == 0:
                                nr.flush()
                            P.op("scalar", I("activation", out=Vm[:, 4 * g + j, :, 0:64], in_=B(bank, 0, 256).rearrange("p (h d) -> p h d", h=4),
                                             func=AF.Copy), [pk[bank]], ["V_%d" % (4 * g + j)])
                    nr.flush()
                    for h in range(4):
                        P.op("vector", I("reduce_sum", out=kms[:, h, 2 * (NG - 1):2 * NG],
                                         in_=KT[h][0:64, (NG - 1) * GW:NG * GW].rearrange("p (n k) -> p n k", k=256), axis=AX.X),
                             ["KT%d_%d" % (h, NG - 1)], ["kms%d_%d" % (h, NG - 1)])
                        P.op("vector", I("tensor_scalar", out=kmT[:, h, :], in0=kms[:, h, :], scalar1=1.0 / 256.0, scalar2=None, op0=ALU.mult),
                             ["kms%d_%d" % (h, g_) for g_ in range(NG)], ["kmT%d" % h])
                    gb = sb(st, "gb", [128, 16, 16], F32)
                    sel = sb(st, "sel", [128, 16, 16], F32)
                    mx = sb(st, "mx", [128, 16, 8], F32)
                    MBs = [sb(st, "MB%d" % i, [128, 4, 4, 80], BF16) for i in range(2)]
                    P.op("gpsimd", I("memset", MBs[0][:], 0.0), [], ["MB0"])
                    P.op("gpsimd", I("memset", MBs[1][:], 0.0), [], ["MB1"])
                    def gate(ci):
                        c0, n, _ = CHUNKS[ci]
                        gbk = 6 + ci % 2
                        MB = MBs[ci % 2]
                        mbk = "MB%d" % (ci % 2)
                        regular = n == GW
                        nt = 4 if regular else 1
                        rows = 128 if regular else 8
                        qt0 = 4 * ci if regular else 16
                        for h in range(4):
                            for t in range(nt):
                                wcol = 128 if regular else 8
                                mm(B(gbk, (h * nt + t) * 16, (h * nt + t) * 16 + 16, 0, rows), QT[h][0:64, c0 + wcol * t:c0 + wcol * t + wcol], kmT[:, h, :],
                                   True, True, ["QT%d_%d" % (h, ci), "kmT%d" % h], [pk[gbk]])
                        nn_ = 4 * nt
                        gbv = gb[0:rows, 0:nn_, :].rearrange("p (h t) n -> p h t n", h=4)
                        selv = sel[0:rows, 0:nn_, :].rearrange("p (h t) n -> p h t n", h=4)
                        P.op("vector", I("tensor_tensor", out=gbv, in0=B(gbk, 0, nn_ * 16, 0, rows).rearrange("p (h t n) -> p h t n", h=4, t=nt),
                                         in1=gmask[0:rows, qt0:qt0 + nt, :].unsqueeze(1).to_broadcast([rows, 4, nt, 16]), op=ALU.add),
                             [pk[gbk], "gmask"], ["gb"])
                        for i in range(nn_):
                            P.op("vector", I("max", out=mx[0:rows, i, :], in_=gb[0:rows, i, :]), ["gb"], ["mx"])
                        P.op("vector", I("tensor_tensor", out=sel[0:rows, 0:nn_, :], in0=gb[0:rows, 0:nn_, :],
                                         in1=mx[0:rows, 0:nn_, 2:3].to_broadcast([rows, nn_, 16]), op=ALU.is_ge), ["gb", "mx"], ["sel"])
                        P.op("vector", I("scalar_tensor_tensor", out=MB[0:rows, :, 0:nt, 64:80], in0=selv, scalar=1.0,
                                         in1=valid[0:rows, qt0:qt0 + nt, :].unsqueeze(1).to_broadcast([rows, 4, nt, 16]),
                                         op0=ALU.subtract, op1=ALU.mult), ["sel", "valid", mbk], [mbk])
                        for h in range(4):
                            tbk = 2 + h
                            for t in range(nt):
                                wcol = 128 if regular else 8
                                mm(B(tbk, wcol * t, wcol * t + wcol, 0, 80), MB[0:rows, h, t, :], ident[0:rows, 0:rows], True, True, [mbk, "cmat"], [pk[tbk]])
                            P.op("vector", I("tensor_copy", out=QT[h][64:80, c0:c0 + n], in_=B(tbk, 0, n, 64, 80)), [pk[tbk]], ["QTa%d_%d" % (h, ci)])
                    dma("sync", hTg[0][:, :, 0:GW], hq_v[:, :, 0:GW], [], ["hTg0"])
                    for ci, (c0, n, _) in enumerate(CHUNKS):
                        b = ci % 2
                        if ci + 1 < len(CHUNKS):
                            c0n, nn, _ = CHUNKS[ci + 1]
                            dma("sync", hTg[1 - b][:, :, 0:nn], hq_v[:, :, c0n:c0n + nn], [], ["hTg%d" % (1 - b)])
                        for bl in range(2):
                            bank = bl
                            for c in range(8):
                                mm(B(bank, 0, n), wq[:, c, 128 * bl:128 * bl + 128], hTg[b][:, c, 0:n], c == 0, c == 7, ["wq", "hTg%d" % b], [pk[bank]])
                            nr.block(bank, n, gn[:, 2:3], cosT[:, c0:c0 + n], sinT[:, c0:c0 + n],
                                     [(QT[2 * bl][0:64, c0:c0 + n], 0, 64, "QT%d_%d" % (2 * bl, ci)),
                                      (QT[2 * bl + 1][0:64, c0:c0 + n], 64, 128, "QT%d_%d" % (2 * bl + 1, ci))])
                    nr.flush()
                    for ci in range(len(CHUNKS)):
                        gate(ci)
                P.barrier()
                with ExitStack() as st:
                    amask, hmask = load_amasks(st)
                    Pt2 = [sb(st, "Pt2m_%d" % i, [128, 2 * GW], BF16) for i in range(3)]
                    rl = [sb(st, "m_rl%d" % i, [64, GW], F32) for i in range(2)]
                    obt = [sb(st, "m_ob%d" % i, [64, GW], BF16) for i in range(2)]
                    hi_ = 0
                    for ci, (c0, n, nkt) in enumerate(CHUNKS):
                        regular = n == GW
                        for h in range(4):
                            hh = 4 * hp + h
                            if regular:
                                batches = [[j, j + 1] for j in range(0, nkt, 2)]
                            else:
                                batches = [list(range(nkt))]
                            nb = len(batches)
                            od = 4 + hi_ % 2

                            def emit_S(bi):
                                st_ = bi % 3
                                units = batches[bi]
                                for ui, j in enumerate(units):
                                    off = ui * n
                                    if regular:
                                        masked = j >= nkt - 8
                                        mk = amask[:, ci % 2, j - (nkt - 8), :] if masked else None
                                    else:
                                        masked = True
                                        mk = hmask[:, j, :]
                                    oap = pp[STP[st_]][:, off:off + n]
                                    mm(oap, KT[h][0:80, 128 * j:128 * j + 128], QT[h][0:80, c0:c0 + n], True, not masked,
                                       ["KT%d_%d" % (h, j // 4), "KToh%d" % h, "QT%d_%d" % (h, ci), "QTa%d_%d" % (h, ci)], STK[st_])
                                    if masked:
                                        mm(oap, ident, mk, False, True, ["cmat", "amask", "hmask"], STK[st_])
                                w = len(units) * n
                                P.op("scalar", I("activation", out=Pt2[st_][:, 0:w], in_=pp[STP[st_]][:, 0:w], func=AF.Exp, scale=0.125),
                                     STK[st_], ["Pt2_%d" % st_])

                            def emit_PV(bi):
                                st_ = bi % 3
                                units = batches[bi]
                                for ui, j in enumerate(units):
                                    off = ui * n
                                    mm(B(od, 0, n), Vm[:, j, h, :], Pt2[st_][:, off:off + n], j == 0, j == nkt - 1,
                                       ["V_%d" % j, "Vones", "Pt2_%d" % st_], [pk[od]])
                            for bi in range(min(2, nb)):
                                emit_S(bi)
                            for bi in range(nb):
                                if bi + 2 < nb:
                                    emit_S(bi + 2)
                                emit_PV(bi)
                            r_ = rl[hi_ % 2]
                            o_ = obt[hi_ % 2]
                            ek = "me%d" % (hi_ % 2)
                            hi_ += 1
                            P.op("vector", I("reciprocal", out=r_[:, 0:n], in_=B(od, 0, n, 64, 128)), [pk[od]], [ek + "r"])
                            P.op("vector", I("tensor_tensor", out=o_[:, 0:n], in0=B(od, 0, n, 0, 64), in1=r_[:, 0:n], op=ALU.mult), [pk[od], ek + "r"], [ek + "o"])
                            dma(STQ, obT_d[64 * hh:64 * hh + 64, c0:c0 + n], o_[:, 0:n], [ek + "o"], ["obT_d%d" % ci])
            P.barrier()

        if STOP == "2b":
            P.finalize()
            P.emit()
            return nc
        stab.close()
        sw = es.enter_context(ExitStack())
        wupA = sb(sw, "wupA", [128, 2, 8, 1408], BF16)
        with ExitStack() as st:
            wg = sb(st, "wg", [128, 4, 2, 8, 256], BF16)
            wa = sb(st, "wa", [128, 4, D], BF16)
            wb = sb(st, "wb", [128, 4, D], BF16)
            wo = sb(st, "wo", [128, 8, D], BF16)
            def wg_piece(p_):
                dma("gpsimd", wg[:, p_, 0, :, :].rearrange("p c n -> p (c n)"), wsrc("wga%d" % p_), [], ["wga%d" % p_])
                dma("gpsimd", wg[:, p_, 1, :, :].rearrange("p c n -> p (c n)"), wsrc("wgb%d" % p_), [], ["wgb%d" % p_])
            wg_piece(0)
            dma("gpsimd", wa[:].rearrange("p c n -> p (c n)"), wsrc("wa"), [], ["wa"])
            dma("gpsimd", wb[:].rearrange("p c n -> p (c n)"), wsrc("wb"), [], ["wb"])
            for p_ in range(1, 4):
                wg_piece(p_)
            dma("gpsimd", wo[:].rearrange("p c n -> p (c n)"), wsrc("wo"), [], ["wo"])
            dma("gpsimd", wupA[:, 0, :, :].rearrange("p c n -> p (c n)"), wsrc("wupAu"), [], ["wupAu"])
            dma("gpsimd", wupA[:, 1, :, :].rearrange("p c n -> p (c n)"), wsrc("wupAg"), [], ["wupAg"])
            hTg = [sb(st, "t_hT%d" % i, [128, 8, GW], BF16) for i in range(2)]
            yg1 = sb(st, "t_y", [128, 4, GW], F32)
            yg = [yg1, yg1]
            obg = [sb(st, "t_ob%d" % i, [128, 4, GW], BF16) for i in range(2)]
            xg1 = sb(st, "t_xg", [128, 8, GW], F32)
            xg = [xg1, xg1]
            ysq = sb(st, "t_ysq", [128, 4, GW], BF16)
            lnr = [sb(st, "t_ln%d" % i, [128, GW], F32) for i in range(2)]
            rs = [sb(st, "t_rs%d" % i, [128, GW], F32) for i in range(2)]
            oag = sb(st, "t_oa", [128, 4, GW], BF16)
            sga = sb(st, "t_sga", [128, GW], F32)
            sgb = sb(st, "t_sgb", [128, GW], F32)
            m1 = sb(st, "t_m1", [128, GW], F32)
            m2 = sb(st, "t_m2", [128, GW], F32)
            mg = sb(st, "t_mg", [128, 8, GW], BF16)
            x1t = [sb(st, "t_x1%d" % i, [128, GW], F32) for i in range(2)]

            def t_load(ci):
                c0, n, _ = CHUNKS[ci]
                b = ci % 2
                dma("sync", hTg[b][:, :, 0:n], hq_v[:, :, c0:c0 + n], [], ["hTg%d" % b])
                dma("sync", obg[b][:, :, 0:n], obT_v[:, :, c0:c0 + n], [], ["obg%d" % b])

            def t_load_y(ci):
                c0, n, _ = CHUNKS[ci]
                dma("sync", yg1[:, :, 0:n], yT_v[:, :, c0:c0 + n], [], ["yg"])

            def t_load_x(ci):
                c0, n, _ = CHUNKS[ci]
                dma("sync", xg1[:, :, 0:n], xq_v[:, :, c0:c0 + n], [], ["xgT"])
            t_load(0)
            t_load_y(0)
            t_load_x(0)
            for ci, (c0, n, _) in enumerate(CHUNKS):
                b = ci % 2
                if ci + 1 < len(CHUNKS):
                    t_load(ci + 1)
                P.op("scalar", I("activation", out=ysq[:, :, 0:n], in_=yg[b][:, :, 0:n], func=AF.Square), ["yg"], ["ysq"])
                for h in range(4):
                    bank = 6 + h % 2
                    mm(B(bank, 0, n), on128, ysq[:, h, 0:n], True, True, ["ysq", "cmat"], [pk[bank]])
                    rstd_from(bank, n, lnr[h % 2], rs[h % 2], "trs%d" % (h % 2))
                    P.op("vector", I("scalar_tensor_tensor", out=oag[:, h, 0:n], in0=yg[b][:, h, 0:n], scalar=gsub[:], in1=rs[h % 2][:, 0:n],
                                     op0=ALU.mult, op1=ALU.mult), ["yg", "gsub", "trs%d" % (h % 2)], ["oag%d" % h])
                if ci + 1 < len(CHUNKS):
                    t_load_y(ci + 1)
                for d in range(8):
                    for c in range(8):
                        mm(B(2, 0, n), wg[:, d // 2, 0, c, 128 * (d % 2):128 * (d % 2) + 128], hTg[b][:, c, 0:n], c == 0, c == 7, ["wga%d" % (d // 2), "hTg%d" % b], [pk[2]])
                    for c in range(8):
                        mm(B(3, 0, n), wg[:, d // 2, 1, c, 128 * (d % 2):128 * (d % 2) + 128], hTg[b][:, c, 0:n], c == 0, c == 7, ["wgb%d" % (d // 2), "hTg%d" % b], [pk[3]])
                    for c in range(4):
                        mm(B(0, 0, n), wa[:, c, 128 * d:128 * d + 128], oag[:, c, 0:n], c == 0, c == 3, ["wa", "oag%d" % c], [pk[0]])
                    for c in range(4):
                        mm(B(1, 0, n), wb[:, c, 128 * d:128 * d + 128], obg[b][:, c, 0:n], c == 0, c == 3, ["wb", "obg%d" % b], [pk[1]])
                    P.op("scalar", I("activation", out=sga[:, 0:n], in_=B(2, 0, n), func=AF.Sigmoid, bias=bg[:, d:d + 1]), [pk[2], "bg"], ["sga"])
                    P.op("scalar", I("activation", out=sgb[:, 0:n], in_=B(3, 0, n), func=AF.Sigmoid, bias=bg[:, 8 + d:9 + d]), [pk[3], "bg"], ["sgb"])
                    P.op("vector", I("tensor_tensor", out=m1[:, 0:n], in0=B(0, 0, n), in1=sga[:, 0:n], op=ALU.mult), [pk[0], "sga"], ["m1"])
                    P.op("vector", I("tensor_tensor", out=m2[:, 0:n], in0=B(1, 0, n), in1=sgb[:, 0:n], op=ALU.mult), [pk[1], "sgb"], ["m2"])
                    P.op("gpsimd", I("tensor_tensor", out=mg[:, d, 0:n], in0=m1[:, 0:n], in1=m2[:, 0:n], op=ALU.add), ["m1", "m2"], ["mg%d" % d])
                for d in range(8):
                    bank = 4 + d % 2
                    for c in range(8):
                        mm(B(bank, 0, n), wo[:, c, 128 * d:128 * d + 128], mg[:, c, 0:n], c == 0, c == 7, ["wo", "mg%d" % c], [pk[bank]])
                    xo = x1t[d % 2]
                    P.op("vector", I("tensor_tensor", out=xo[:, 0:n], in0=B(bank, 0, n), in1=xg[b][:, d, 0:n], op=ALU.add),
                         [pk[bank], "xgT"], ["x1t%d" % (d % 2)])
                    dma(STQ, x1_v[:, d, c0:c0 + n], xo[:, 0:n], ["x1t%d" % (d % 2)], ["x1_d%d" % ci])
                if ci + 1 < len(CHUNKS):
                    t_load_x(ci + 1)
        P.barrier()

        if STOP == "3a":
            P.finalize()
            P.emit()
            return nc
        with ExitStack() as st:
            wupB = sb(st, "wupB", [128, 2, 8, 1408], BF16)
            dma("gpsimd", wupB[:, 0, :, :].rearrange("p c n -> p (c n)"), wsrc("wupBu"), [], ["wupBu"])
            dma("gpsimd", wupB[:, 1, :, :].rearrange("p c n -> p (c n)"), wsrc("wupBg"), [], ["wupBg"])

            def wu(fb, c):
                W = wupA if fb < 11 else wupB
                return W[:, 0, c, 128 * (fb % 11):128 * (fb % 11) + 128], ("wupAu" if fb < 11 else "wupBu")

            def wgt(fb, c):
                W = wupA if fb < 11 else wupB
                return W[:, 1, c, 128 * (fb % 11):128 * (fb % 11) + 128], ("wupAg" if fb < 11 else "wupBg")
            wd = [sb(st, "wd%d" % i, [128, NFB, 128], BF16) for i in range(3)]
            x1g = [sb(st, "f_x1g%d" % i, [128, 8, GW], F32) for i in range(2)]
            sq = sb(st, "f_sq", [128, 8, GW], BF16)
            lnr = sb(st, "f_ln", [128, GW], F32)
            rstd = sb(st, "f_rstd", [128, GW], F32)
            h2 = sb(st, "f_h2", [128, 8, GW], BF16)
            U = [sb(st, "f_U%d" % i, [128, GW + 2], F32) for i in range(2)]
            Uh = sb(st, "f_Uh", [128, NFB, 8], F32)
            cc_ = [sb(st, "f_c%d" % i, [128, GW], F32) for i in range(2)]
            ge = [sb(st, "f_ge%d" % i, [128, GW], F32) for i in range(2)]
            mT = sb(st, "f_mT", [128, NFB, GW], BF16)
            ot = [sb(st, "f_ot%d" % i, [128, GW], F32) for i in range(2)]
            order = [4, 0, 1, 2, 3]

            def f_load(oi):
                ci = order[oi]
                c0, n, _ = CHUNKS[ci]
                dma("sync", x1g[oi % 2][:, :, 0:n], x1_v[:, :, c0:c0 + n], [], ["x1g%d" % (oi % 2)])
            f_load(0)

            def f_rms_parts(oi_):
                ci_ = order[oi_]
                _, n_, _ = CHUNKS[ci_]
                b_ = oi_ % 2

                def p0():
                    P.op("scalar", I("activation", out=sq[:, :, 0:n_], in_=x1g[b_][:, :, 0:n_], func=AF.Square), ["x1g%d" % b_], ["fsq"])

                def p1():
                    for c in range(8):
                        mm(B(0, 0, n_), on1024, sq[:, c, 0:n_], c == 0, c == 7, ["fsq", "cmat"], [pk[0]])
                    rstd_from(0, n_, lnr, rstd, "frstd")

                def p2():
                    for c in range(8):
                        P.op("vector", I("scalar_tensor_tensor", out=h2[:, c, 0:n_], in0=x1g[b_][:, c, 0:n_], scalar=g2[:, c:c + 1],
                                         in1=rstd[:, 0:n_], op0=ALU.mult, op1=ALU.mult), ["x1g%d" % b_, "g2", "frstd"], ["h2"])
                return [p0, p1, p2]
            wdi = 0
            for oi, ci in enumerate(order):
                c0, n, _ = CHUNKS[ci]
                b = oi % 2
                if oi + 1 < len(order):
                    f_load(oi + 1)
                if oi == 0:
                    for f_ in f_rms_parts(0):
                        f_()
                if ci == 4:
                    for fb in range(NFB):
                        bank = 1 + fb % 2
                        for c in range(8):
                            mm(B(bank, 0, n), wu(fb, c)[0], h2[:, c, 0:n], c == 0, c == 7, [wu(fb, c)[1], "h2"], [pk[bank]])
                        P.op("scalar", I("activation", out=Uh[:, fb, :], in_=B(bank, 0, n), func=AF.Copy), [pk[bank]], ["Uh"])
                    if oi + 1 < len(order):
                        for f_ in f_rms_parts(oi + 1):
                            f_()
                    continue
                s = ci
                wl = []
                for d in range(3):
                    wl.append(dma("gpsimd", wd[(wdi + d) % 3][:].rearrange("p c n -> p (c n)"), wsrc("wd%d" % d), [], ["wd%d" % ((wdi + d) % 3)]))
                for fb in range(NFB):
                    bu, bgk = 1 + fb % 2, 3 + fb % 2
                    Ut, uk = U[fb % 2], "U%d" % (fb % 2)
                    ct, ck = cc_[fb % 2], "c%d" % (fb % 2)
                    gt, gk = ge[fb % 2], "ge%d" % (fb % 2)
                    for c in range(8):
                        mm(B(bu), wu(fb, c)[0], h2[:, c, :], c == 0, c == 7, [wu(fb, c)[1], "h2"], [pk[bu]])
                    for c in range(8):
                        mm(B(bgk), wgt(fb, c)[0], h2[:, c, :], c == 0, c == 7, [wgt(fb, c)[1], "h2"], [pk[bgk]])
                    P.op("scalar", I("activation", out=Ut[:, 2:GW + 2], in_=B(bu), func=AF.Copy), [pk[bu]], [uk])
                    P.op("vector", I("tensor_scalar", out=Ut[:, 0:2], in0=Uh[:, fb, 2 * s:2 * s + 2], scalar1=hflag[:, s:s + 1], scalar2=None, op0=ALU.mult),
                         ["Uh", "hflag"], [uk])
                    P.op("vector", I("tensor_scalar", out=ct[:], in0=Ut[:, 0:GW], scalar1=cw[:, fb, 0:1], scalar2=cb[:, fb:fb + 1],
                                     op0=ALU.mult, op1=ALU.add), [uk, "cw", "cb"], [ck])
                    P.op("vector", I("scalar_tensor_tensor", out=ct[:], in0=Ut[:, 1:GW + 1], scalar=cw[:, fb, 1:2], in1=ct[:],
                                     op0=ALU.mult, op1=ALU.add), [uk, "cw", ck], [ck])
                    P.op("vector", I("scalar_tensor_tensor", out=ct[:], in0=Ut[:, 2:GW + 2], scalar=cw[:, fb, 2:3], in1=ct[:],
                                     op0=ALU.mult, op1=ALU.add), [uk, "cw", ck], [ck])
                    P.op("scalar", I("activation", out=gt[:], in_=ct[:], func=AF.Gelu_apprx_tanh), [ck], [gk])
                    P.op("vector", I("tensor_tensor", out=mT[:, fb, :], in0=B(bgk), in1=gt[:], op=ALU.mult), [pk[bgk], gk], ["mT%d" % fb])
                nxt = f_rms_parts(oi + 1) if oi + 1 < len(order) else []
                if nxt:
                    nxt[0]()
                for d in range(8):
                    wt = wd[wdi % 3]
                    wkey = "wd%d" % (wdi % 3)
                    bank = 5 + d % 2
                    for fb in range(NFB):
                        mm(B(bank), wt[:, fb, :], mT[:, fb, :], fb == 0, fb == NFB - 1, [wkey, "mT%d" % fb], [pk[bank]])
                    if nxt and d == 0:
                        nxt[1]()
                    if nxt and d == 1:
                        nxt[2]()
                    o_ = ot[d % 2]
                    P.op("vector", I("tensor_tensor", out=o_[:], in0=B(bank), in1=x1g[b][:, d, :], op=ALU.add),
                         [pk[bank], "x1g%d" % b], ["ot%d" % (d % 2)])
                    dma(STQ, outT_v[:, d, c0:c0 + GW], o_[:], ["ot%d" % (d % 2)], ["out%d_%d" % (s, d)])
                    if d + 3 < 8:
                        dma("gpsimd", wd[wdi % 3][:].rearrange("p c n -> p (c n)"), wsrc("wd%d" % (d + 3)), [], [wkey])
                    wdi += 1

        P.finalize()
        P.emit()
    return nc


def own_positions(r):
    pos = np.zeros(NQ, np.int64)
    flag = np.ones(4, np.float32)
    for s in range(4):
        g = GROUPS[r][s]
        pos[512 * s:512 * s + 512] = 512 * g + np.arange(512)
        for i in range(2):
            p = 512 * g - 2 + i
            if p < 0:
                p = i
                flag[s] = 0.0
            pos[2048 + 2 * s + i] = p
    return pos, flag


def _consts(r):
    bf = ml_dtypes.bfloat16
    inv = (1.0 / (np.float32(10000.0) ** (np.arange(0, 64, 2, dtype=np.float32) / np.float32(64)))).astype(np.float32)
    ang = (np.arange(S, dtype=np.float32)[:, None] * inv[None, :]).astype(np.float32)
    cos = np.cos(ang).astype(np.float32)
    sin = np.sin(ang).astype(np.float32)
    p = np.arange(128)
    sgn = np.where((p % 64) < 32, -1.0, 1.0).astype(np.float32)
    cosk = np.ascontiguousarray(cos[:, p % 32].T)
    sink = np.ascontiguousarray((sin[:, p % 32] * sgn[None, :]).T)
    pos, flag = own_positions(r)
    cosq = np.ascontiguousarray(cosk[:, pos])
    sinq = np.ascontiguousarray(sink[:, pos])
    ident = np.eye(128, dtype=np.float32)
    sw = (p // 64) * 64 + ((p % 64) + 32) % 64
    perm = np.zeros((128, 128), np.float32)
    perm[sw, p] = 1.0
    bd64 = np.zeros((128, 128), np.float32)
    bd64[:64, :64] = 1.0 / 64
    bd64[64:, 64:] = 1.0 / 64
    cmat = np.concatenate([ident, perm, bd64, np.full((128, 128), 1.0 / 1024, np.float32),
                           np.full((128, 128), 1.0 / 128, np.float32), np.ones((128, 128), np.float32)], axis=1).astype(bf)
    k = np.arange(128)[:, None]
    am = np.zeros((4, 8, 128, GW), np.float32)
    for s, (c0, n, nkt) in enumerate(SLOTS):
        qpos = pos[c0:c0 + n][None, :]
        for jj in range(8):
            j = nkt - 8 + jj
            am[s, jj] = np.where(128 * j + k <= qpos, 0.0, -BIGM)
    assert np.array_equal(am[0], am[2]) and np.array_equal(am[1], am[3])
    amask = np.ascontiguousarray(am[0:2].transpose(2, 0, 1, 3).reshape(128, 2 * 8 * GW)).astype(bf)
    hpos = pos[2048:2056][None, :]
    hm = np.stack([np.where(128 * j + k <= hpos, 0.0, -BIGM) for j in range(32)], axis=1)
    hmask = np.ascontiguousarray(hm.reshape(128, 32 * 8)).astype(bf)
    gmask = np.zeros((128, 17, 16), np.float32)
    valid = np.zeros((128, 17, 16), np.float32)
    nidx = np.arange(16)[None, :]
    for qt in range(16):
        own = (pos[128 * qt:128 * qt + 128] // 256)[:, None]
        valid[:, qt, :] = (nidx < own)
    own = (pos[2048:2056] // 256)[:, None]
    valid[0:8, 16, :] = (nidx < own)
    gmask = np.where(valid > 0, 0.0, -1e30).astype(np.float32)
    oh = np.zeros((16, S), np.float32)
    for n_ in range(16):
        oh[n_, 256 * n_:256 * n_ + 256] = BIGB
    hflag = np.tile(flag[None, :], (128, 1)).astype(np.float32)
    return dict(cosk=cosk, sink=sink, cosq=cosq, sinq=sinq, cmat=cmat, amask=amask, hmask=hmask,
                gmask=np.ascontiguousarray(gmask.reshape(128, 17 * 16)), valid=np.ascontiguousarray(valid.reshape(128, 17 * 16)),
                hflag=hflag, oh16=oh.astype(bf)), pos


_NC_CACHE = {}


def kernel(x, norm1_g, w_in, b_gate, qn_a, kn_a, lam_q1, lam_k1, lam_q2, lam_k2, subln_g,
           qn_b, kn_b, w_a_proj, w_b_proj, w_out, norm2_g, w_up, conv_w, conv_b, w_down):
    f = np.float32
    x = np.asarray(x, f)
    Bn = x.shape[0]
    col = lambda v: np.ascontiguousarray(np.asarray(v, f).reshape(-1, 128).T)
    gn = np.stack([np.tile(np.asarray(v, f).reshape(64), 2) for v in (qn_a, kn_a, qn_b, kn_b)], axis=1)
    lamv = np.concatenate([np.tile(np.asarray(v, f).reshape(1, 64), (128, 1)) for v in (lam_q1, lam_k1, lam_q2, lam_k2)], axis=1)
    cwl = np.ascontiguousarray(np.asarray(conv_w, f)[0].T.reshape(NFB, 128, 3).transpose(1, 0, 2).reshape(128, NFB * 3))
    def pc(w, a, b_):
        w = np.asarray(w, f)
        C = w.shape[0] // 128
        return w[:, a:b_].reshape(C, 128, b_ - a).transpose(1, 0, 2).reshape(128, C * (b_ - a))
    Win, Wup, Wdn = np.asarray(w_in, f)[0], np.asarray(w_up, f)[0], np.asarray(w_down, f)[0]
    pieces = {"wk": pc(Win, 512, 1024), "wv": pc(Win, 1024, 1536), "wq": pc(Win, 0, 512),
              "wa": pc(np.asarray(w_a_proj, f)[0], 0, D), "wb": pc(np.asarray(w_b_proj, f)[0], 0, D), "wo": pc(np.asarray(w_out, f)[0], 0, D),
              "wupAu": pc(Wup, 0, 1408), "wupAg": pc(Wup, DFF, DFF + 1408), "wupBu": pc(Wup, 1408, DFF), "wupBg": pc(Wup, DFF + 1408, 2 * DFF)}
    for hp in range(2):
        pieces["wkm%d" % hp] = pc(Win, 2048 + 256 * hp, 2048 + 256 * hp + 256)
        pieces["wvm%d" % hp] = pc(Win, 2560 + 256 * hp, 2560 + 256 * hp + 256)
        pieces["wqm%d" % hp] = pc(Win, 1536 + 256 * hp, 1536 + 256 * hp + 256)
    for p_ in range(4):
        pieces["wga%d" % p_] = pc(Win, 3072 + 256 * p_, 3072 + 256 * p_ + 256)
        pieces["wgb%d" % p_] = pc(Win, 4096 + 256 * p_, 4096 + 256 * p_ + 256)
    for d_ in range(8):
        pieces["wd%d" % d_] = pc(Wdn, 128 * d_, 128 * d_ + 128)
    wpack = np.empty((128, WTOT), f)
    for nm, (o_, n_) in WOFF.items():
        assert pieces[nm].shape == (128, n_), (nm, pieces[nm].shape, n_)
        wpack[:, o_:o_ + n_] = pieces[nm]
    shared = {
        "wpack": wpack,
        "g1": col(norm1_g), "g2": col(norm2_g), "bg": col(b_gate),
        "cw": cwl, "cb": col(conv_b),
        "gn": np.ascontiguousarray(gn.astype(f)), "gsub": np.ascontiguousarray(np.asarray(subln_g, f).reshape(128, 1)),
        "lamv": np.ascontiguousarray(lamv.astype(f)),
    }
    cst = [_consts(r) for r in range(2)]
    in_maps = []
    for b in range(Bn):
        xTb = np.ascontiguousarray(x[b].T)
        for r in range(2):
            m = dict(shared)
            m.update(cst[r][0])
            m["xT"] = xTb
            m["xq"] = np.ascontiguousarray(xTb[:, cst[r][1]])
            in_maps.append(m)
    if "nc" not in _NC_CACHE:
        _NC_CACHE["nc"] = build_program()
    nc = _NC_CACHE["nc"]
    res = run_bass_kernel_spmd(nc, in_maps, core_ids=list(range(2 * Bn)))
    out = np.empty((Bn, S, D), f)
    for b in range(Bn):
        for r in range(2):
            o = np.asarray(res.results[2 * b + r]["outT"], f)
            out[b, cst[r][1][0:2048], :] = o.T
    if DEBUG:
        kernel.debug = res.results
    return out
```

```python
import numpy as np
import ml_dtypes
from contextlib import ExitStack
import concourse.bass as bass
import concourse.mybir as mybir
from concourse.bass_utils import run_bass_kernel_spmd

F32 = mybir.dt.float32
BF16 = mybir.dt.bfloat16
AF = mybir.ActivationFunctionType
ALU = mybir.AluOpType
AX = mybir.AxisListType

S = 4096
D = 1024
NG = 8
GW = 512
DFF = 2816
NFB = 22
EPS = 1e-6
BIGM = 30000.0
BIGB = 32768.0
ENGS = ["sync", "scalar", "vector", "gpsimd", "tensor"]
DEBUG = False
STOP = None
MAXG = 8
NOV = False
NOK = False
NOPREF = False
SKIP0 = False
S0JOBS = 13
STQ = 'gpsimd'
NOPIPE = False
USE_SQRT = False


def I(name, *args, **kw):
    return (name, args, kw)


class Op:
    __slots__ = ("eng", "fn", "deps", "dma", "sem", "val", "inc", "waits")


class Prog:
    def __init__(self, nc, es):
        self.nc = nc
        self.es = es
        self.ops = {e: [] for e in ENGS}
        self.lastw = {}
        self.readers = {}
        self.pending = {e: None for e in ENGS}
        self.final_dma = []

    def op(self, eng, fn, reads=(), writes=(), dma=False):
        o = Op()
        o.eng, o.fn, o.dma = eng, fn, dma
        o.inc = dma
        deps = set()
        for r in reads:
            w = self.lastw.get(r)
            if w is not None:
                deps.add(w)
            if r.startswith("ps") or r.startswith("ST"):
                for rd in self.readers.get(r, ()):
                    if rd.eng != eng:
                        deps.add(rd)
        for w_ in writes:
            w = self.lastw.get(w_)
            if w is not None:
                deps.add(w)
            for rd in self.readers.get(w_, ()):
                deps.add(rd)
        if self.pending[eng] is not None:
            deps.update(self.pending[eng])
            self.pending[eng] = None
        o.deps = [d for d in deps if not (d.eng == "tensor" and eng == "tensor" and not d.dma and not dma)]
        for r in reads:
            self.readers.setdefault(r, []).append(o)
        for w_ in writes:
            self.lastw[w_] = o
            self.readers[w_] = []
        self.ops[eng].append(o)
        return o

    def barrier(self):
        deps = []
        for e in ENGS:
            lastc = None
            for o in reversed(self.ops[e]):
                if not o.dma:
                    lastc = o
                    break
            if lastc is not None:
                deps.append(lastc)
            cnt = 0
            for o in reversed(self.ops[e]):
                if o.dma:
                    deps.append(o)
                    cnt += 1
                    if cnt >= 8:
                        break
        for e in ENGS:
            self.pending[e] = list(deps)
        self.lastw = {}
        self.readers = {}

    def finalize(self):
        nc, es = self.nc, self.es
        for e in ENGS:
            for o in self.ops[e]:
                for d in o.deps:
                    d.inc = True
        self.csem = {}
        self.dsem = {}
        for e in ENGS:
            self.csem[e] = [es.enter_context(nc.semaphore("c_%s_%d" % (e, i))) for i in range(2)]
            self.dsem[e] = [es.enter_context(nc.semaphore("d_%s_%d" % (e, i))) for i in range(8)]
        LIM = 30000
        for e in ENGS:
            cc = 0
            dcount = [0] * 8
            di = 0
            for o in self.ops[e]:
                o.waits = []
                if o.dma:
                    k = di % 8
                    di += 1
                    dcount[k] += 1
                    o.sem = self.dsem[e][k]
                    o.val = 16 * dcount[k]
                    if dcount[k] > 1:
                        o.waits.append((o.sem, o.val - 16))
                elif o.inc:
                    si = cc // LIM
                    o.sem = self.csem[e][si]
                    o.val = cc % LIM + 1
                    cc += 1
            assert cc < 2 * LIM, (e, cc)
            self.final_dma.append((e, [(self.dsem[e][k], 16 * dcount[k]) for k in range(8) if dcount[k] > 0]))
        for e in ENGS:
            for o in self.ops[e]:
                for d in o.deps:
                    o.waits.append((d.sem, d.val))

    def emit(self):
        nc = self.nc
        fin = dict(self.final_dma)
        with nc.Block() as block:
            for eng in ENGS:
                def body(e, eng=eng):
                    waited = {}
                    for o in self.ops[eng]:
                        for (sem, val) in o.waits:
                            if waited.get(id(sem), 0) < val:
                                e.wait_ge(sem, val)
                                waited[id(sem)] = val
                        inst = getattr(e, o.fn[0])(*o.fn[1], **o.fn[2])
                        if o.inc:
                            inst.then_inc(o.sem, 16 if o.dma else 1)
                    for (sem, val) in fin[eng]:
                        if waited.get(id(sem), 0) < val:
                            e.wait_ge(sem, val)
                getattr(block, eng)(body)


NQ = 2056
GROUPS = [[0, 3, 4, 7], [1, 2, 5, 6]]
SLOTS = [(0, 512, 8), (512, 512, 16), (1024, 512, 24), (1536, 512, 32)]
MINI = (2048, 8, 28)
CHUNKS = SLOTS + [MINI]


def _wlayout():
    off = {}
    cur = [0]

    def add(name, n):
        off[name] = (cur[0], n)
        cur[0] += n
    add("wk", 8 * 512)
    add("wv", 8 * 512)
    add("wq", 8 * 512)
    for hp in range(2):
        add("wkm%d" % hp, 8 * 256)
        add("wvm%d" % hp, 8 * 256)
        add("wqm%d" % hp, 8 * 256)
    for p in range(4):
        add("wga%d" % p, 8 * 256)
        add("wgb%d" % p, 8 * 256)
    add("wa", 4 * 1024)
    add("wb", 4 * 1024)
    add("wo", 8 * 1024)
    for nm in ("wupAu", "wupAg", "wupBu", "wupBg"):
        add(nm, 8 * 1408)
    for d in range(8):
        add("wd%d" % d, NFB * 128)
    return off, cur[0]


WOFF, WTOT = _wlayout()


def build_program():
    nc = bass.Bass("TRN2", target_bir_lowering=False)

    def din(name, shape, dt=F32):
        return nc.dram_tensor(name, list(shape), dt, kind="ExternalInput").ap()

    xT = din("xT", [D, S])
    xq = din("xq", [D, NQ])
    wpack = din("wpack", [128, WTOT])

    def wsrc(name):
        o, n = WOFF[name]
        return wpack[:, o:o + n]
    g1_d = din("g1", [128, 8])
    g2_d = din("g2", [128, 8])
    bg_d = din("bg", [128, 16])
    cw_d = din("cw", [128, NFB * 3])
    cb_d = din("cb", [128, NFB])
    gn_d = din("gn", [128, 4])
    gsub_d = din("gsub", [128, 1])
    lam_d = din("lamv", [128, 4 * 64])
    cosk_d = din("cosk", [128, S])
    sink_d = din("sink", [128, S])
    cosq_d = din("cosq", [128, NQ])
    sinq_d = din("sinq", [128, NQ])
    cmat_d = din("cmat", [128, 6 * 128], BF16)
    amask_d = din("amask", [128, 2 * 8 * GW], BF16)
    hmask_d = din("hmask", [128, 32 * 8], BF16)
    gmask_d = din("gmask", [128, 17 * 16])
    valid_d = din("valid", [128, 17 * 16])
    hflag_d = din("hflag", [128, 4])
    oh_d = din("oh16", [16, S], BF16)

    skind = "ExternalOutput" if DEBUG else "Internal"
    outT = nc.dram_tensor("outT", [D, 2048], F32, kind="ExternalOutput").ap()
    hT_d = nc.dram_tensor("hT_d", [D, S], BF16, kind=skind).ap()
    hq_d = nc.dram_tensor("hq_d", [D, NQ], BF16, kind=skind).ap()
    yT_d = nc.dram_tensor("yT_d", [512, NQ], F32, kind=skind).ap()
    obT_d = nc.dram_tensor("obT_d", [512, NQ], BF16, kind=skind).ap()
    x1_d = nc.dram_tensor("x1_d", [D, NQ], F32, kind=skind).ap()

    def cv(ap):
        return ap.rearrange("(c p) t -> p c t", p=128)

    xT_v, xq_v, hT_v, hq_v, yT_v, obT_v, x1_v, outT_v = (cv(a) for a in (xT, xq, hT_d, hq_d, yT_d, obT_d, x1_d, outT))

    with ExitStack() as es:
        P = Prog(nc, es)
        uid = [0]

        def sb(st, name, shape, dt):
            uid[0] += 1
            return st.enter_context(nc.sbuf_tensor("sb%d_%s" % (uid[0], name), list(shape), dt))

        cmat = sb(es, "cmat", [128, 6, 128], BF16)
        ident, perm, bd64, on1024, on128, ones = (cmat[:, i, :] for i in range(6))
        g1 = sb(es, "g1", [128, 8], F32)
        g2 = sb(es, "g2", [128, 8], F32)
        bg = sb(es, "bg", [128, 16], F32)
        cw = sb(es, "cw", [128, NFB, 3], F32)
        cb = sb(es, "cb", [128, NFB], F32)
        gn = sb(es, "gn", [128, 4], F32)
        gsub = sb(es, "gsub", [128, 1], F32)
        lamv = sb(es, "lamv", [128, 4, 64], F32)
        lamt = sb(es, "lamt", [128, 2, 64], F32)
        lams = sb(es, "lams", [128, 2], F32)
        neglam = sb(es, "neglam", [128, 1], F32)
        epsc = sb(es, "epsc", [128, 1], F32)
        hflag = sb(es, "hflag", [128, 4], F32)
        pp = [es.enter_context(nc.psum_tensor("pp%d" % i, [128, 2 * GW], F32)) for i in range(4)]
        pk = ["ps%d" % i for i in range(8)]

        STP = [0, 1, 3]
        STK = [["ps0", "ps1"], ["ps2", "ps3"], ["ps6", "ps7"]]

        def B(i, c0=0, c1=GW, p0=0, p1=128):
            off = (i % 2) * GW
            return pp[i // 2][p0:p1, off + c0:off + c1]

        def dma(eng, out, in_, reads, writes):
            return P.op(eng, I("dma_start", out=out, in_=in_), reads, writes, dma=True)

        def mm(out, lhsT, rhs, start, stop, reads, writes):
            return P.op("tensor", I("matmul", out, lhsT, rhs, start=start, stop=stop), reads, writes)

        dma("sync", cmat[:].rearrange("p a b -> p (a b)"), cmat_d, [], ["cmat"])
        dma("sync", g1[:], g1_d, [], ["g1"])
        dma("sync", g2[:], g2_d, [], ["g2"])
        dma("sync", bg[:], bg_d, [], ["bg"])
        dma("sync", cw[:].rearrange("p a b -> p (a b)"), cw_d, [], ["cw"])
        dma("sync", cb[:], cb_d, [], ["cb"])
        dma("sync", gn[:], gn_d, [], ["gn"])
        dma("sync", gsub[:], gsub_d, [], ["gsub"])
        dma("sync", hflag[:], hflag_d, [], ["hflag"])
        dma("sync", lamv[:].rearrange("p a b -> p (a b)"), lam_d, [], ["lamv"])
        P.op("vector", I("memset", epsc[:], EPS), [], ["epsc"])
        P.op("vector", I("tensor_tensor", out=lamt[:, 0, :], in0=lamv[:, 0, :], in1=lamv[:, 1, :], op=ALU.mult), ["lamv"], ["lamt0"])
        P.op("vector", I("tensor_tensor", out=lamt[:, 1, :], in0=lamv[:, 2, :], in1=lamv[:, 3, :], op=ALU.mult), ["lamv"], ["lamt1"])
        P.op("vector", I("reduce_sum", out=lams[:], in_=lamt[:], axis=AX.X), ["lamt0", "lamt1"], ["lams"])
        P.op("scalar", I("activation", out=lams[:], in_=lams[:], func=AF.Exp), ["lams"], ["lams"])
        P.op("vector", I("scalar_tensor_tensor", out=neglam[:], in0=lams[:, 1:2], scalar=-0.2, in1=lams[:, 0:1],
                         op0=ALU.add, op1=ALU.subtract), ["lams"], ["neglam"])
        P.op("vector", I("tensor_scalar", out=gsub[:], in0=gsub[:], scalar1=0.8, scalar2=None, op0=ALU.mult), ["gsub"], ["gsub"])

        def rstd_from(bank, n, lnr, rstd, key):
            if USE_SQRT:
                P.op("scalar", I("activation", out=lnr[:, 0:n], in_=B(bank, 0, n), func=AF.Sqrt, bias=epsc[:]), [pk[bank], "epsc"], [key + "ln"])
                P.op("vector", I("reciprocal", out=rstd[:, 0:n], in_=lnr[:, 0:n]), [key + "ln"], [key])
                return
            P.op("scalar", I("activation", out=lnr[:, 0:n], in_=B(bank, 0, n), func=AF.Ln, bias=epsc[:]), [pk[bank], "epsc"], [key + "ln"])
            P.op("scalar", I("activation", out=rstd[:, 0:n], in_=lnr[:, 0:n], func=AF.Exp, scale=-0.5), [key + "ln"], [key])

        def rms_chunk(tag, src, skey, n, sq, lnr, rstd, bank):
            P.op("scalar", I("activation", out=sq[:, :, 0:n], in_=src[:, :, 0:n], func=AF.Square), [skey], [tag + "sq"])
            for c in range(8):
                mm(B(bank, 0, n), on1024, sq[:, c, 0:n], c == 0, c == 7, [tag + "sq", "cmat"], [pk[bank]])
            rstd_from(bank, n, lnr, rstd, tag + "rstd")

        class NR:
            def __init__(self, st, name):
                self.bufs = []
                for i in range(2):
                    self.bufs.append(dict(
                        sq=sb(st, name + "sq%d" % i, [128, GW], BF16), pbf=sb(st, name + "pbf%d" % i, [128, GW], BF16),
                        t1=sb(st, name + "t1%d" % i, [128, GW], F32), t2=sb(st, name + "t2%d" % i, [128, GW], F32),
                        ln=sb(st, name + "ln%d" % i, [128, GW], F32), rs=sb(st, name + "rs%d" % i, [128, GW], F32)))
                self.k = 0
                self.pending = None

            def flush(self):
                if self.pending is not None:
                    f = self.pending
                    self.pending = None
                    f()

            def block(self, bank, n, gcol, cos_ap, sin_ap, outs, ckey="cos", skey="sin"):
                i = self.k % 2
                self.k += 1
                bf = self.bufs[i]
                t = "nr%d" % i
                pm, psw = (2, 3) if i == 0 else (4, 5)
                pa = B(bank, 0, n)
                P.op("scalar", I("activation", out=bf["sq"][:, 0:n], in_=pa, func=AF.Square), [pk[bank]], [t + "sq"])
                P.op("scalar", I("activation", out=bf["pbf"][:, 0:n], in_=pa, func=AF.Copy, scale=gcol), [pk[bank], "gn"], [t + "pbf"])
                P.op("vector", I("scalar_tensor_tensor", out=bf["t1"][:, 0:n], in0=pa, scalar=gcol, in1=cos_ap, op0=ALU.mult, op1=ALU.mult),
                     [pk[bank], "gn", ckey, t + "pbf", t + "sq"], [t + "t1"])
                prev = self.pending

                def phaseB():
                    mm(B(pm, 0, n), bd64, bf["sq"][:, 0:n], True, True, [t + "sq", "cmat"], [pk[pm]])
                    mm(B(psw, 0, n), perm, bf["pbf"][:, 0:n], True, True, [t + "pbf", "cmat"], [pk[psw]])
                    rstd_from(pm, n, bf["ln"], bf["rs"], t + "rs")
                    P.op("vector", I("tensor_tensor", out=bf["t2"][:, 0:n], in0=B(psw, 0, n), in1=sin_ap, op=ALU.mult), [pk[psw], skey], [t + "t2"])
                    P.op("vector", I("tensor_tensor", out=bf["t1"][:, 0:n], in0=bf["t1"][:, 0:n], in1=bf["t2"][:, 0:n], op=ALU.add),
                         [t + "t1", t + "t2"], [t + "t1"])
                    for (oap, lo, hi, okey) in outs:
                        P.op("vector", I("tensor_tensor", out=oap, in0=bf["t1"][lo:hi, 0:n], in1=bf["rs"][lo:hi, 0:n], op=ALU.mult),
                             [t + "t1", t + "rs"], [okey])
                self.pending = phaseB
                if prev is not None:
                    prev()
                if NOPIPE:
                    self.flush()

        def load_amasks(st):
            amask = sb(st, "amask", [128, 2, 8, GW], BF16)
            hmask = sb(st, "hmask", [128, 32, 8], BF16)
            dma("sync", amask[:].rearrange("p a b c -> p (a b c)"), amask_d, [], ["amask"])
            dma("sync", hmask[:].rearrange("p a b -> p (a b)"), hmask_d, [], ["hmask"])
            return amask, hmask

        stab = ExitStack()
        coskT = sb(stab, "coskT", [128, S], F32)
        sinkT = sb(stab, "sinkT", [128, S], F32)

        with ExitStack() as so:
            KT = [sb(so, "KTd%d" % h, [128, S], BF16) for h in range(4)]
            Vd = sb(so, "Vd", [128, 32, 512], BF16)
            with ExitStack() as st:
                wk = sb(st, "wk", [128, 8, 512], BF16)
                wv = sb(st, "wv", [128, 8, 512], BF16)
                dma("gpsimd", wk[:].rearrange("p c n -> p (c n)"), wsrc("wk"), [], ["wk"])
                dma("gpsimd", wv[:].rearrange("p c n -> p (c n)"), wsrc("wv"), [], ["wv"])
                xg = [sb(st, "s0_xg%d" % i, [128, 8, GW], F32) for i in range(2)]
                sq = sb(st, "s0_sq", [128, 8, GW], BF16)
                lnr = sb(st, "s0_ln", [128, GW], F32)
                rstd2 = [sb(st, "s0_rstd%d" % i, [128, GW], F32) for i in range(2)]
                hT = [sb(st, "s0_hT%d" % i, [128, 8, GW], BF16) for i in range(2)]
                nr = NR(st, "nra")
                jobs = []
                for g in range(NG):
                    jobs.append(("g", g, xT_v, hT_v, g * GW, GW, "hT_d%d" % g))
                    if g < len(CHUNKS):
                        c0_, n_, _ = CHUNKS[g]
                        jobs.append(("o", g, xq_v, hq_v, c0_, n_, "hq_d%d" % g))

                def s0_load(i):
                    _, _, src, _, c0, n, _k = jobs[i]
                    dma("sync", xg[i % 2][:, :, 0:n], src[:, :, c0:c0 + n], [], ["s0xg%d" % (i % 2)])
                def s0_rms(i):
                    kind, g, src, dst, c0, n, okey = jobs[i]
                    b = i % 2
                    P.op("scalar", I("activation", out=sq[:, :, 0:n], in_=xg[b][:, :, 0:n], func=AF.Square), ["s0xg%d" % b], ["s0sq"])
                    for c in range(8):
                        mm(B(7, 0, n), on1024, sq[:, c, 0:n], c == 0, c == 7, ["s0sq", "cmat"], [pk[7]])
                    rstd_from(7, n, lnr, rstd2[b], "s0rstd%d" % b)
                    for c in range(8):
                        P.op("vector", I("scalar_tensor_tensor", out=hT[b][:, c, 0:n], in0=xg[b][:, c, 0:n], scalar=g1[:, c:c + 1],
                                         in1=rstd2[b][:, 0:n], op0=ALU.mult, op1=ALU.mult), ["s0xg%d" % b, "g1", "s0rstd%d" % b], ["s0hT%d_%d" % (b, c)])
                    dma(STQ, dst[:, :, c0:c0 + n], hT[b][:, :, 0:n], ["s0hT%d_%d" % (b, c) for c in range(8)], [okey])

                def s0_kv(i):
                    kind, g, src, dst, c0, n, okey = jobs[i]
                    b = i % 2
                    if kind != "g":
                        return
                    for h in range(4):
                        bank = h % 2
                        for c in range(8):
                            mm(B(bank), wk[:, c, 128 * h:128 * h + 128], hT[b][:, c, :], c == 0, c == 7, ["wk", "s0hT%d_%d" % (b, c)], [pk[bank]])
                        nr.block(bank, GW, gn[:, 1:2], coskT[:, c0:c0 + GW], sinkT[:, c0:c0 + GW],
                                 [(KT[h][:, c0:c0 + GW], 0, 128, "KT%d_%d" % (h, g))], ckey="cosk", skey="sink")
                        for c in range(8):
                            mm(B(6), hT[b][:, c, 128 * h:128 * h + 128], wv[:, c, :], c == 0, c == 7, ["wv", "s0hT%d_%d" % (b, c)], [pk[6]])
                        P.op("scalar", I("activation", out=Vd[:, 4 * g + h, :], in_=B(6), func=AF.Copy), [pk[6]], ["V_%d" % (4 * g + h)])
                s0_load(0)
                if len(jobs) > 1:
                    s0_load(1)
                dma("sync", coskT[:], cosk_d, [], ["cosk"])
                dma("sync", sinkT[:], sink_d, [], ["sink"])
                s0_rms(0)
                for i in range(len(jobs)):
                    if i + 1 < len(jobs):
                        s0_rms(i + 1)
                    if i + 2 < len(jobs):
                        s0_load(i + 2)
                    s0_kv(i)
                nr.flush()
            P.barrier()
            QT = [[sb(so, "QTd%d_%d" % (h, cc), [128, NQ], BF16) for cc in range(2)] for h in range(4)]
            for h in range(4):
                P.op("gpsimd", I("memset", QT[h][0][64:128, :], 0.0), [], ["QTz%d_0" % h])
                P.op("gpsimd", I("memset", QT[h][1][0:64, :], 0.0), [], ["QTz%d_1" % h])
            with ExitStack() as st:
                cosT = sb(st, "cosq", [128, NQ], F32)
                sinT = sb(st, "sinq", [128, NQ], F32)
                dma("sync", cosT[:], cosq_d, [], ["cos"])
                dma("sync", sinT[:], sinq_d, [], ["sin"])
                wq = sb(st, "wq", [128, 8, 512], BF16)
                dma("gpsimd", wq[:].rearrange("p c n -> p (c n)"), wsrc("wq"), [], ["wq"])
                hTg = [sb(st, "hqg%d" % i, [128, 8, GW], BF16) for i in range(2)]
                nr = NR(st, "nrq")
                dma("sync", hTg[0][:, :, 0:GW], hq_v[:, :, 0:GW], [], ["hTg0"])
                for ci, (c0, n, _) in enumerate(CHUNKS):
                    b = ci % 2
                    if ci + 1 < len(CHUNKS):
                        c0n, nn, _ = CHUNKS[ci + 1]
                        dma("sync", hTg[1 - b][:, :, 0:nn], hq_v[:, :, c0n:c0n + nn], [], ["hTg%d" % (1 - b)])
                    for h in range(4):
                        bank = h % 2
                        for c in range(8):
                            mm(B(bank, 0, n), wq[:, c, 128 * h:128 * h + 128], hTg[b][:, c, 0:n], c == 0, c == 7, ["wq", "hTg%d" % b], [pk[bank]])
                        nr.block(bank, n, gn[:, 0:1], cosT[:, c0:c0 + n], sinT[:, c0:c0 + n],
                                 [(QT[h][0][0:64, c0:c0 + n], 0, 64, "QT%d_%d_0" % (h, ci)),
                                  (QT[h][1][64:128, c0:c0 + n], 64, 128, "QT%d_%d_1" % (h, ci))])
                nr.flush()
            P.barrier()
            if STOP == "qa":
                P.finalize()
                P.emit()
                return nc
            with ExitStack() as st:
                amask, hmask = load_amasks(st)
                Pt2 = [sb(st, "Pt2_%d" % i, [128, 2 * GW], BF16) for i in range(3)]
                ep = [dict((nm, sb(st, "ae_%s%d" % (nm, i), [128, GW], F32)) for nm in ("l0", "l1", "o0", "o1")) for i in range(2)]
                dacc = [sb(st, "dacc%d" % i, [128, 2 * GW], F32) for i in range(2)]
                daccb = sb(st, "daccb", [128, 2 * GW], BF16)
                hi_ = 0
                ep_pending = [None]
                for ci, (c0, n, nkt) in enumerate(CHUNKS):
                    regular = n == GW
                    for h in range(4):
                        if regular:
                            batches = [[(j, 0), (j, 1)] for j in range(nkt)]
                        else:
                            batches = [[(j, cc) for j in range(nkt) for cc in (0, 1)]]
                        nb = len(batches)

                        def emit_S(bi):
                            st_ = bi % 3
                            units = batches[bi]
                            for ui, (j, cc) in enumerate(units):
                                off = ui * n
                                lo, hi = 64 * cc, 64 * cc + 64
                                if regular:
                                    masked = j >= nkt - 8
                                    mk = amask[:, ci % 2, j - (nkt - 8), :] if masked else None
                                else:
                                    masked = True
                                    mk = hmask[:, j, :]
                                oap = pp[STP[st_]][:, off:off + n]
                                mm(oap, KT[h][:, 128 * j:128 * j + 128], QT[h][cc][:, c0:c0 + n], True, not masked,
                                   ["KT%d_%d" % (h, j // 4), "QT%d_%d_%d" % (h, ci, cc), "QTz%d_%d" % (h, cc)], STK[st_])
                                if masked:
                                    mm(oap, ident, mk, False, True, ["cmat", "amask", "hmask"], STK[st_])
                            w = len(units) * n
                            P.op("scalar", I("activation", out=Pt2[st_][:, 0:w], in_=pp[STP[st_]][:, 0:w], func=AF.Exp, scale=0.125),
                                 STK[st_], ["Pt2_%d" % st_])

                        def emit_PV(bi):
                            st_ = bi % 3
                            units = batches[bi]
                            for ui, (j, cc) in enumerate(units):
                                off = ui * n
                                mm(B(4 + cc, 0, n), Vd[:, j, 128 * h:128 * h + 128], Pt2[st_][:, off:off + n], j == 0, j == nkt - 1,
                                   ["V_%d" % j, "Pt2_%d" % st_], [pk[4 + cc]])
                                if not regular:
                                    mm(B(6 + cc, 0, n), ones, Pt2[st_][:, off:off + n], j == 0, j == nkt - 1, ["cmat", "Pt2_%d" % st_], [pk[6 + cc]])
                            if regular:
                                ac = dacc[hi_ % 2]
                                akey = "dacc%d" % (hi_ % 2)
                                for (eng_, lo_, hi_c) in (("vector", 0, 800), ("gpsimd", 800, 2 * GW)):
                                    ak2 = akey + eng_
                                    if units[0][0] == 0:
                                        P.op(eng_, I("tensor_copy", out=ac[:, lo_:hi_c], in_=Pt2[st_][:, lo_:hi_c]), ["Pt2_%d" % st_], [ak2])
                                    else:
                                        P.op(eng_, I("tensor_tensor", out=ac[:, lo_:hi_c], in0=ac[:, lo_:hi_c], in1=Pt2[st_][:, lo_:hi_c], op=ALU.add),
                                             ["Pt2_%d" % st_, ak2], [ak2])
                        for bi in range(min(2, nb)):
                            emit_S(bi)
                        if ep_pending[0] is not None:
                            ep_pending[0]()
                            ep_pending[0] = None
                        for bi in range(nb):
                            if bi + 2 < nb:
                                emit_S(bi + 2)
                            emit_PV(bi)
                        e_ = ep[hi_ % 2]
                        ek = "ae%d" % (hi_ % 2)
                        if regular:
                            P.op("vector", I("tensor_copy", out=daccb[:], in_=dacc[hi_ % 2][:]),
                                 ["dacc%dvector" % (hi_ % 2), "dacc%dgpsimd" % (hi_ % 2)], ["daccb"])
                            mm(B(6, 0, n), ones, daccb[:, 0:GW], True, True, ["cmat", "daccb"], [pk[6]])
                            mm(B(7, 0, n), ones, daccb[:, GW:2 * GW], True, True, ["cmat", "daccb"], [pk[7]])
                        hi_ += 1
                        P.op("scalar", I("activation", out=e_["l0"][:, 0:n], in_=B(6, 0, n), func=AF.Ln), [pk[6]], [ek + "l0"])
                        P.op("scalar", I("activation", out=e_["l1"][:, 0:n], in_=B(7, 0, n), func=AF.Ln), [pk[7]], [ek + "l1"])
                        P.op("vector", I("tensor_copy", out=e_["o0"][:, 0:n], in_=B(4, 0, n)), [pk[4]], [ek + "o0"])
                        P.op("vector", I("tensor_copy", out=e_["o1"][:, 0:n], in_=B(5, 0, n)), [pk[5]], [ek + "o1"])
                        def ep_part2(e_=e_, ek=ek, n=n, h=h, c0=c0, ci=ci):
                            P.op("scalar", I("activation", out=e_["l0"][:, 0:n], in_=e_["l0"][:, 0:n], func=AF.Exp, scale=-1.0), [ek + "l0"], [ek + "l0"])
                            P.op("scalar", I("activation", out=e_["l1"][:, 0:n], in_=e_["l1"][:, 0:n], func=AF.Exp, scale=-1.0), [ek + "l1"], [ek + "l1"])
                            P.op("vector", I("tensor_tensor", out=e_["o0"][:, 0:n], in0=e_["o0"][:, 0:n], in1=e_["l0"][:, 0:n], op=ALU.mult),
                                 [ek + "o0", ek + "l0"], [ek + "o0"])
                            P.op("vector", I("tensor_tensor", out=e_["o1"][:, 0:n], in0=e_["o1"][:, 0:n], in1=e_["l1"][:, 0:n], op=ALU.mult),
                                 [ek + "o1", ek + "l1"], [ek + "o1"])
                            P.op("vector", I("scalar_tensor_tensor", out=e_["o0"][:, 0:n], in0=e_["o1"][:, 0:n], scalar=neglam[:], in1=e_["o0"][:, 0:n],
                                             op0=ALU.mult, op1=ALU.add), [ek + "o0", ek + "o1", "neglam"], [ek + "o0"])
                            dma(STQ, yT_v[:, h, c0:c0 + n], e_["o0"][:, 0:n], [ek + "o0"], ["yT_d%d" % ci])
                        ep_pending[0] = ep_part2
                if ep_pending[0] is not None:
                    ep_pending[0]()
                    ep_pending[0] = None
        P.barrier()

        if STOP == "2a":
            P.finalize()
            P.emit()
            return nc
        for hp in range(2):
            with ExitStack() as so:
                KT = [sb(so, "KTm%d" % h, [80, S], BF16) for h in range(4)]
                Vm = sb(so, "Vm", [128, 32, 4, 128], BF16)
                QT = [sb(so, "QTm%d" % h, [80, NQ], BF16) for h in range(4)]
                kms = sb(so, "kms", [64, 4, 16], F32)
                kmT = sb(so, "kmT", [64, 4, 16], BF16)
                with ExitStack() as st:
                    wk = sb(st, "wkm", [128, 8, 256], BF16)
                    wv = sb(st, "wvm", [128, 8, 256], BF16)
                    dma("gpsimd", wk[:].rearrange("p c n -> p (c n)"), wsrc("wkm%d" % hp), [], ["wk"])
                    dma("gpsimd", wv[:].rearrange("p c n -> p (c n)"), wsrc("wvm%d" % hp), [], ["wv"])
                    for h in range(4):
                        dma("sync", KT[h][64:80, :], oh_d, [], ["KToh%d" % h])
                    P.op("gpsimd", I("memset", Vm[:, :, :, 64:128], 1.0), [], ["Vones"])
                    hTg = [sb(st, "hTgm%d" % i, [128, 8, GW], BF16) for i in range(2)]
                    nr = NR(st, "nrkm")
                    dma("sync", hTg[0][:], hT_v[:, :, 0:GW], [], ["hTg0"])
                    cosT = sb(st, "cosqm", [128, NQ], F32)
                    sinT = sb(st, "sinqm", [128, NQ], F32)
                    dma("sync", cosT[:], cosq_d, [], ["cos"])
                    dma("sync", sinT[:], sinq_d, [], ["sin"])
                    wq = sb(st, "wqm", [128, 8, 256], BF16)
                    dma("gpsimd", wq[:].rearrange("p c n -> p (c n)"), wsrc("wqm%d" % hp), [], ["wq"])
                    gmask = sb(st, "gmask", [128, 17, 16], F32)
                    valid = sb(st, "valid", [128, 17, 16], F32)
                    dma("sync", gmask[:].rearrange("p a b -> p (a b)"), gmask_d, [], ["gmask"])
                    dma("sync", valid[:].rearrange("p a b -> p (a b)"), valid_d, [], ["valid"])
                    for g in range(NG):
                        b = g % 2
                        c0 = g * GW
                        if g + 1 < NG:
                            dma("sync", hTg[1 - b][:], hT_v[:, :, c0 + GW:c0 + 2 * GW], [], ["hTg%d" % (1 - b)])
                        for bl in range(2):
                            bank = bl
                            for c in range(8):
                                mm(B(bank), wk[:, c, 128 * bl:128 * bl + 128], hTg[b][:, c, :], c == 0, c == 7, ["wk", "hTg%d" % b], [pk[bank]])
                            nr.block(bank, GW, gn[:, 3:4], coskT[:, c0:c0 + GW], sinkT[:, c0:c0 + GW],
                                     [(KT[2 * bl][0:64, c0:c0 + GW], 0, 64, "KT%d_%d" % (2 * bl, g)),
                                      (KT[2 * bl + 1][0:64, c0:c0 + GW], 64, 128, "KT%d_%d" % (2 * bl + 1, g))], ckey="cosk", skey="sink")
                            if g > 0:
                                for h_ in (2 * bl, 2 * bl + 1):
                                    P.op("vector", I("reduce_sum", out=kms[:, h_, 2 * (g - 1):2 * g],
                                                     in_=KT[h_][0:64, c0 - GW:c0].rearrange("p (n k) -> p n k", k=256), axis=AX.X),
                                         ["KT%d_%d" % (h_, g - 1)], ["kms%d_%d" % (h_, g - 1)])
                        for j in range(4):
                            bank = 6 + j % 2
                            for c in range(8):
                                mm(B(bank, 0, 256), hTg[b][:, c, 128 * j:128 * j + 128], wv[:, c, :], c == 0, c == 7, ["wv", "hTg%d" % b], [pk[bank]])
                            if j == 0:
                                nr.flush()
                            P.op("scalar", I("activation", out=Vm[:, 4 * g + j, :, 0:64], in_=B(bank, 0, 256).rearrange("p (h d) -> p h d", h=4),
                                             func=AF.Copy), [pk[bank]], ["V_%d" % (4 * g + j)])
                    nr.flush()
                    for h in range(4):
                        P.op("vector", I("reduce_sum", out=kms[:, h, 2 * (NG - 1):2 * NG],
                                         in_=KT[h][0:64, (NG - 1) * GW:NG * GW].rearrange("p (n k) -> p n k", k=256), axis=AX.X),
                             ["KT%d_%d" % (h, NG - 1)], ["kms%d_%d" % (h, NG - 1)])
                        P.op("vector", I("tensor_scalar", out=kmT[:, h, :], in0=kms[:, h, :], scalar1=1.0 / 256.0, scalar2=None, op0=ALU.mult),
                             ["kms%d_%d" % (h, g_) for g_ in range(NG)], ["kmT%d" % h])
                    gb = sb(st, "gb", [128, 16, 16], F32)
                    sel = sb(st, "sel", [128, 16, 16], F32)
                    mx = sb(st, "mx", [128, 16, 8], F32)
                    MBs = [sb(st, "MB%d" % i, [128, 4, 4, 80], BF16) for i in range(2)]
                    P.op("gpsimd", I("memset", MBs[0][:], 0.0), [], ["MB0"])
                    P.op("gpsimd", I("memset", MBs[1][:], 0.0), [], ["MB1"])
                    def gate(ci):
                        c0, n, _ = CHUNKS[ci]
                        gbk = 6 + ci % 2
                        MB = MBs[ci % 2]
                        mbk = "MB%d" % (ci % 2)
                        regular = n == GW
                        nt = 4 if regular else 1
                        rows = 128 if regular else 8
                        qt0 = 4 * ci if regular else 16
                        for h in range(4):
                            for t in range(nt):
                                wcol = 128 if regular else 8
                                mm(B(gbk, (h * nt + t) * 16, (h * nt + t) * 16 + 16, 0, rows), QT[h][0:64, c0 + wcol * t:c0 + wcol * t + wcol], kmT[:, h, :],
                                   True, True, ["QT%d_%d" % (h, ci), "kmT%d" % h], [pk[gbk]])
                        nn_ = 4 * nt
                        gbv = gb[0:rows, 0:nn_, :].rearrange("p (h t) n -> p h t n", h=4)
                        selv = sel[0:rows, 0:nn_, :].rearrange("p (h t) n -> p h t n", h=4)
                        P.op("vector", I("tensor_tensor", out=gbv, in0=B(gbk, 0, nn_ * 16, 0, rows).rearrange("p (h t n) -> p h t n", h=4, t=nt),
                                         in1=gmask[0:rows, qt0:qt0 + nt, :].unsqueeze(1).to_broadcast([rows, 4, nt, 16]), op=ALU.add),
                             [pk[gbk], "gmask"], ["gb"])
                        for i in range(nn_):
                            P.op("vector", I("max", out=mx[0:rows, i, :], in_=gb[0:rows, i, :]), ["gb"], ["mx"])
                        P.op("vector", I("tensor_tensor", out=sel[0:rows, 0:nn_, :], in0=gb[0:rows, 0:nn_, :],
                                         in1=mx[0:rows, 0:nn_, 2:3].to_broadcast([rows, nn_, 16]), op=ALU.is_ge), ["gb", "mx"], ["sel"])
                        P.op("vector", I("scalar_tensor_tensor", out=MB[0:rows, :, 0:nt, 64:80], in0=selv, scalar=1.0,
                                         in1=valid[0:rows, qt0:qt0 + nt, :].unsqueeze(1).to_broadcast([rows, 4, nt, 16]),
                                         op0=ALU.subtract, op1=ALU.mult), ["sel", "valid", mbk], [mbk])
                        for h in range(4):
                            tbk = 2 + h
                            for t in range(nt):
                                wcol = 128 if regular else 8
                                mm(B(tbk, wcol * t, wcol * t + wcol, 0, 80), MB[0:rows, h, t, :], ident[0:rows, 0:rows], True, True, [mbk, "cmat"], [pk[tbk]])
                            P.op("vector", I("tensor_copy", out=QT[h][64:80, c0:c0 + n], in_=B(tbk, 0, n, 64, 80)), [pk[tbk]], ["QTa%d_%d" % (h, ci)])
                    dma("sync", hTg[0][:, :, 0:GW], hq_v[:, :, 0:GW], [], ["hTg0"])
                    for ci, (c0, n, _) in enumerate(CHUNKS):
                        b = ci % 2
                        if ci + 1 < len(CHUNKS):
                            c0n, nn, _ = CHUNKS[ci + 1]
                            dma("sync", hTg[1 - b][:, :, 0:nn], hq_v[:, :, c0n:c0n + nn], [], ["hTg%d" % (1 - b)])
                        for bl in range(2):
                            bank = bl
                            for c in range(8):
                                mm(B(bank, 0, n), wq[:, c, 128 * bl:128 * bl + 128], hTg[b][:, c, 0:n], c == 0, c == 7, ["wq", "hTg%d" % b], [pk[bank]])
                            nr.block(bank, n, gn[:, 2:3], cosT[:, c0:c0 + n], sinT[:, c0:c0 + n],
                                     [(QT[2 * bl][0:64, c0:c0 + n], 0, 64, "QT%d_%d" % (2 * bl, ci)),
                                      (QT[2 * bl + 1][0:64, c0:c0 + n], 64, 128, "QT%d_%d" % (2 * bl + 1, ci))])
                    nr.flush()
                    for ci in range(len(CHUNKS)):
                        gate(ci)
                P.barrier()
                with ExitStack() as st:
                    amask, hmask = load_amasks(st)
                    Pt2 = [sb(st, "Pt2m_%d" % i, [128, 2 * GW], BF16) for i in range(3)]
                    rl = [sb(st, "m_rl%d" % i, [64, GW], F32) for i in range(2)]
                    obt = [sb(st, "m_ob%d" % i, [64, GW], BF16) for i in range(2)]
                    hi_ = 0
                    for ci, (c0, n, nkt) in enumerate(CHUNKS):
                        regular = n == GW
                        for h in range(4):
                            hh = 4 * hp + h
                            if regular:
                                batches = [[j, j + 1] for j in range(0, nkt, 2)]
                            else:
                                batches = [list(range(nkt))]
                            nb = len(batches)
                            od = 4 + hi_ % 2

                            def emit_S(bi):
                                st_ = bi % 3
                                units = batches[bi]
                                for ui, j in enumerate(units):
                                    off = ui * n
                                    if regular:
                                        masked = j >= nkt - 8
                                        mk = amask[:, ci % 2, j - (nkt - 8), :] if masked else None
                                    else:
                                        masked = True
                                        mk = hmask[:, j, :]
                                    oap = pp[STP[st_]][:, off:off + n]
                                    mm(oap, KT[h][0:80, 128 * j:128 * j + 128], QT[h][0:80, c0:c0 + n], True, not masked,
                                       ["KT%d_%d" % (h, j // 4), "KToh%d" % h, "QT%d_%d" % (h, ci), "QTa%d_%d" % (h, ci)], STK[st_])
                                    if masked:
                                        mm(oap, ident, mk, False, True, ["cmat", "amask", "hmask"], STK[st_])
                                w = len(units) * n
                                P.op("scalar", I("activation", out=Pt2[st_][:, 0:w], in_=pp[STP[st_]][:, 0:w], func=AF.Exp, scale=0.125),
                                     STK[st_], ["Pt2_%d" % st_])

                            def emit_PV(bi):
                                st_ = bi % 3
                                units = batches[bi]
                                for ui, j in enumerate(units):
                                    off = ui * n
                                    mm(B(od, 0, n), Vm[:, j, h, :], Pt2[st_][:, off:off + n], j == 0, j == nkt - 1,
                                       ["V_%d" % j, "Vones", "Pt2_%d" % st_], [pk[od]])
                            for bi in range(min(2, nb)):
                                emit_S(bi)
                            for bi in range(nb):
                                if bi + 2 < nb:
                                    emit_S(bi + 2)
                                emit_PV(bi)
                            r_ = rl[hi_ % 2]
                            o_ = obt[hi_ % 2]
                            ek = "me%d" % (hi_ % 2)
                            hi_ += 1
                            P.op("vector", I("reciprocal", out=r_[:, 0:n], in_=B(od, 0, n, 64, 128)), [pk[od]], [ek + "r"])
                            P.op("vector", I("tensor_tensor", out=o_[:, 0:n], in0=B(od, 0, n, 0, 64), in1=r_[:, 0:n], op=ALU.mult), [pk[od], ek + "r"], [ek + "o"])
                            dma(STQ, obT_d[64 * hh:64 * hh + 64, c0:c0 + n], o_[:, 0:n], [ek + "o"], ["obT_d%d" % ci])
            P.barrier()

        if STOP == "2b":
            P.finalize()
            P.emit()
            return nc
        stab.close()
        sw = es.enter_context(ExitStack())
        wupA = sb(sw, "wupA", [128, 2, 8, 1408], BF16)
        with ExitStack() as st:
            wg = sb(st, "wg", [128, 4, 2, 8, 256], BF16)
            wa = sb(st, "wa", [128, 4, D], BF16)
            wb = sb(st, "wb", [128, 4, D], BF16)
            wo = sb(st, "wo", [128, 8, D], BF16)
            def wg_piece(p_):
                dma("gpsimd", wg[:, p_, 0, :, :].rearrange("p c n -> p (c n)"), wsrc("wga%d" % p_), [], ["wga%d" % p_])
                dma("gpsimd", wg[:, p_, 1, :, :].rearrange("p c n -> p (c n)"), wsrc("wgb%d" % p_), [], ["wgb%d" % p_])
            wg_piece(0)
            dma("gpsimd", wa[:].rearrange("p c n -> p (c n)"), wsrc("wa"), [], ["wa"])
            dma("gpsimd", wb[:].rearrange("p c n -> p (c n)"), wsrc("wb"), [], ["wb"])
            for p_ in range(1, 4):
                wg_piece(p_)
            dma("gpsimd", wo[:].rearrange("p c n -> p (c n)"), wsrc("wo"), [], ["wo"])
            dma("gpsimd", wupA[:, 0, :, :].rearrange("p c n -> p (c n)"), wsrc("wupAu"), [], ["wupAu"])
            dma("gpsimd", wupA[:, 1, :, :].rearrange("p c n -> p (c n)"), wsrc("wupAg"), [], ["wupAg"])
            hTg = [sb(st, "t_hT%d" % i, [128, 8, GW], BF16) for i in range(2)]
            yg1 = sb(st, "t_y", [128, 4, GW], F32)
            yg = [yg1, yg1]
            obg = [sb(st, "t_ob%d" % i, [128, 4, GW], BF16) for i in range(2)]
            xg1 = sb(st, "t_xg", [128, 8, GW], F32)
            xg = [xg1, xg1]
            ysq = sb(st, "t_ysq", [128, 4, GW], BF16)
            lnr = [sb(st, "t_ln%d" % i, [128, GW], F32) for i in range(2)]
            rs = [sb(st, "t_rs%d" % i, [128, GW], F32) for i in range(2)]
            oag = sb(st, "t_oa", [128, 4, GW], BF16)
            sga = sb(st, "t_sga", [128, GW], F32)
            sgb = sb(st, "t_sgb", [128, GW], F32)
            m1 = sb(st, "t_m1", [128, GW], F32)
            m2 = sb(st, "t_m2", [128, GW], F32)
            mg = sb(st, "t_mg", [128, 8, GW], BF16)
            x1t = [sb(st, "t_x1%d" % i, [128, GW], F32) for i in range(2)]

            def t_load(ci):
                c0, n, _ = CHUNKS[ci]
                b = ci % 2
                dma("sync", hTg[b][:, :, 0:n], hq_v[:, :, c0:c0 + n], [], ["hTg%d" % b])
                dma("sync", obg[b][:, :, 0:n], obT_v[:, :, c0:c0 + n], [], ["obg%d" % b])

            def t_load_y(ci):
                c0, n, _ = CHUNKS[ci]
                dma("sync", yg1[:, :, 0:n], yT_v[:, :, c0:c0 + n], [], ["yg"])

            def t_load_x(ci):
                c0, n, _ = CHUNKS[ci]
                dma("sync", xg1[:, :, 0:n], xq_v[:, :, c0:c0 + n], [], ["xgT"])
            t_load(0)
            t_load_y(0)
            t_load_x(0)
            for ci, (c0, n, _) in enumerate(CHUNKS):
                b = ci % 2
                if ci + 1 < len(CHUNKS):
                    t_load(ci + 1)
                P.op("scalar", I("activation", out=ysq[:, :, 0:n], in_=yg[b][:, :, 0:n], func=AF.Square), ["yg"], ["ysq"])
                for h in range(4):
                    bank = 6 + h % 2
                    mm(B(bank, 0, n), on128, ysq[:, h, 0:n], True, True, ["ysq", "cmat"], [pk[bank]])
                    rstd_from(bank, n, lnr[h % 2], rs[h % 2], "trs%d" % (h % 2))
                    P.op("vector", I("scalar_tensor_tensor", out=oag[:, h, 0:n], in0=yg[b][:, h, 0:n], scalar=gsub[:], in1=rs[h % 2][:, 0:n],
                                     op0=ALU.mult, op1=ALU.mult), ["yg", "gsub", "trs%d" % (h % 2)], ["oag%d" % h])
                if ci + 1 < len(CHUNKS):
                    t_load_y(ci + 1)
                for d in range(8):
                    for c in range(8):
                        mm(B(2, 0, n), wg[:, d // 2, 0, c, 128 * (d % 2):128 * (d % 2) + 128], hTg[b][:, c, 0:n], c == 0, c == 7, ["wga%d" % (d // 2), "hTg%d" % b], [pk[2]])
                    for c in range(8):
                        mm(B(3, 0, n), wg[:, d // 2, 1, c, 128 * (d % 2):128 * (d % 2) + 128], hTg[b][:, c, 0:n], c == 0, c == 7, ["wgb%d" % (d // 2), "hTg%d" % b], [pk[3]])
                    for c in range(4):
                        mm(B(0, 0, n), wa[:, c, 128 * d:128 * d + 128], oag[:, c, 0:n], c == 0, c == 3, ["wa", "oag%d" % c], [pk[0]])
                    for c in range(4):
                        mm(B(1, 0, n), wb[:, c, 128 * d:128 * d + 128], obg[b][:, c, 0:n], c == 0, c == 3, ["wb", "obg%d" % b], [pk[1]])
                    P.op("scalar", I("activation", out=sga[:, 0:n], in_=B(2, 0, n), func=AF.Sigmoid, bias=bg[:, d:d + 1]), [pk[2], "bg"], ["sga"])
                    P.op("scalar", I("activation", out=sgb[:, 0:n], in_=B(3, 0, n), func=AF.Sigmoid, bias=bg[:, 8 + d:9 + d]), [pk[3], "bg"], ["sgb"])
                    P.op("vector", I("tensor_tensor", out=m1[:, 0:n], in0=B(0, 0, n), in1=sga[:, 0:n], op=ALU.mult), [pk[0], "sga"], ["m1"])
                    P.op("vector", I("tensor_tensor", out=m2[:, 0:n], in0=B(1, 0, n), in1=sgb[:, 0:n], op=ALU.mult), [pk[1], "sgb"], ["m2"])
                    P.op("gpsimd", I("tensor_tensor", out=mg[:, d, 0:n], in0=m1[:, 0:n], in1=m2[:, 0:n], op=ALU.add), ["m1", "m2"], ["mg%d" % d])
                for d in range(8):
                    bank = 4 + d % 2
                    for c in range(8):
                        mm(B(bank, 0, n), wo[:, c, 128 * d:128 * d + 128], mg[:, c, 0:n], c == 0, c == 7, ["wo", "mg%d" % c], [pk[bank]])
                    xo = x1t[d % 2]
                    P.op("vector", I("tensor_tensor", out=xo[:, 0:n], in0=B(bank, 0, n), in1=xg[b][:, d, 0:n], op=ALU.add),
                         [pk[bank], "xgT"], ["x1t%d" % (d % 2)])
                    dma(STQ, x1_v[:, d, c0:c0 + n], xo[:, 0:n], ["x1t%d" % (d % 2)], ["x1_d%d" % ci])
                if ci + 1 < len(CHUNKS):
                    t_load_x(ci + 1)
        P.barrier()

        if STOP == "3a":
            P.finalize()
            P.emit()
            return nc
        with ExitStack() as st:
            wupB = sb(st, "wupB", [128, 2, 8, 1408], BF16)
            dma("gpsimd", wupB[:, 0, :, :].rearrange("p c n -> p (c n)"), wsrc("wupBu"), [], ["wupBu"])
            dma("gpsimd", wupB[:, 1, :, :].rearrange("p c n -> p (c n)"), wsrc("wupBg"), [], ["wupBg"])

            def wu(fb, c):
                W = wupA if fb < 11 else wupB
                return W[:, 0, c, 128 * (fb % 11):128 * (fb % 11) + 128], ("wupAu" if fb < 11 else "wupBu")

            def wgt(fb, c):
                W = wupA if fb < 11 else wupB
                return W[:, 1, c, 128 * (fb % 11):128 * (fb % 11) + 128], ("wupAg" if fb < 11 else "wupBg")
            wd = [sb(st, "wd%d" % i, [128, NFB, 128], BF16) for i in range(3)]
            x1g = [sb(st, "f_x1g%d" % i, [128, 8, GW], F32) for i in range(2)]
            sq = sb(st, "f_sq", [128, 8, GW], BF16)
            lnr = sb(st, "f_ln", [128, GW], F32)
            rstd = sb(st, "f_rstd", [128, GW], F32)
            h2 = sb(st, "f_h2", [128, 8, GW], BF16)
            U = [sb(st, "f_U%d" % i, [128, GW + 2], F32) for i in range(2)]
            Uh = sb(st, "f_Uh", [128, NFB, 8], F32)
            cc_ = [sb(st, "f_c%d" % i, [128, GW], F32) for i in range(2)]
            ge = [sb(st, "f_ge%d" % i, [128, GW], F32) for i in range(2)]
            mT = sb(st, "f_mT", [128, NFB, GW], BF16)
            ot = [sb(st, "f_ot%d" % i, [128, GW], F32) for i in range(2)]
            order = [4, 0, 1, 2, 3]

            def f_load(oi):
                ci = order[oi]
                c0, n, _ = CHUNKS[ci]
                dma("sync", x1g[oi % 2][:, :, 0:n], x1_v[:, :, c0:c0 + n], [], ["x1g%d" % (oi % 2)])
            f_load(0)

            def f_rms_parts(oi_):
                ci_ = order[oi_]
                _, n_, _ = CHUNKS[ci_]
                b_ = oi_ % 2

                def p0():
                    P.op("scalar", I("activation", out=sq[:, :, 0:n_], in_=x1g[b_][:, :, 0:n_], func=AF.Square), ["x1g%d" % b_], ["fsq"])

                def p1():
                    for c in range(8):
                        mm(B(0, 0, n_), on1024, sq[:, c, 0:n_], c == 0, c == 7, ["fsq", "cmat"], [pk[0]])
                    rstd_from(0, n_, lnr, rstd, "frstd")

                def p2():
                    for c in range(8):
                        P.op("vector", I("scalar_tensor_tensor", out=h2[:, c, 0:n_], in0=x1g[b_][:, c, 0:n_], scalar=g2[:, c:c + 1],
                                         in1=rstd[:, 0:n_], op0=ALU.mult, op1=ALU.mult), ["x1g%d" % b_, "g2", "frstd"], ["h2"])
                return [p0, p1, p2]
            wdi = 0
            for oi, ci in enumerate(order):
                c0, n, _ = CHUNKS[ci]
                b = oi % 2
                if oi + 1 < len(order):
                    f_load(oi + 1)
                if oi == 0:
                    for f_ in f_rms_parts(0):
                        f_()
                if ci == 4:
                    for fb in range(NFB):
                        bank = 1 + fb % 2
                        for c in range(8):
                            mm(B(bank, 0, n), wu(fb, c)[0], h2[:, c, 0:n], c == 0, c == 7, [wu(fb, c)[1], "h2"], [pk[bank]])
                        P.op("scalar", I("activation", out=Uh[:, fb, :], in_=B(bank, 0, n), func=AF.Copy), [pk[bank]], ["Uh"])
                    if oi + 1 < len(order):
                        for f_ in f_rms_parts(oi + 1):
                            f_()
                    continue
                s = ci
                wl = []
                for d in range(3):
                    wl.append(dma("gpsimd", wd[(wdi + d) % 3][:].rearrange("p c n -> p (c n)"), wsrc("wd%d" % d), [], ["wd%d" % ((wdi + d) % 3)]))
                for fb in range(NFB):
                    bu, bgk = 1 + fb % 2, 3 + fb % 2
                    Ut, uk = U[fb % 2], "U%d" % (fb % 2)
                    ct, ck = cc_[fb % 2], "c%d" % (fb % 2)
                    gt, gk = ge[fb % 2], "ge%d" % (fb % 2)
                    for c in range(8):
                        mm(B(bu), wu(fb, c)[0], h2[:, c, :], c == 0, c == 7, [wu(fb, c)[1], "h2"], [pk[bu]])
                    for c in range(8):
                        mm(B(bgk), wgt(fb, c)[0], h2[:, c, :], c == 0, c == 7, [wgt(fb, c)[1], "h2"], [pk[bgk]])
                    P.op("scalar", I("activation", out=Ut[:, 2:GW + 2], in_=B(bu), func=AF.Copy), [pk[bu]], [uk])
                    P.op("vector", I("tensor_scalar", out=Ut[:, 0:2], in0=Uh[:, fb, 2 * s:2 * s + 2], scalar1=hflag[:, s:s + 1], scalar2=None, op0=ALU.mult),
                         ["Uh", "hflag"], [uk])
                    P.op("vector", I("tensor_scalar", out=ct[:], in0=Ut[:, 0:GW], scalar1=cw[:, fb, 0:1], scalar2=cb[:, fb:fb + 1],
                                     op0=ALU.mult, op1=ALU.add), [uk, "cw", "cb"], [ck])
                    P.op("vector", I("scalar_tensor_tensor", out=ct[:], in0=Ut[:, 1:GW + 1], scalar=cw[:, fb, 1:2], in1=ct[:],
                                     op0=ALU.mult, op1=ALU.add), [uk, "cw", ck], [ck])
                    P.op("vector", I("scalar_tensor_tensor", out=ct[:], in0=Ut[:, 2:GW + 2], scalar=cw[:, fb, 2:3], in1=ct[:],
                                     op0=ALU.mult, op1=ALU.add), [uk, "cw", ck], [ck])
                    P.op("scalar", I("activation", out=gt[:], in_=ct[:], func=AF.Gelu_apprx_tanh), [ck], [gk])
                    P.op("vector", I("tensor_tensor", out=mT[:, fb, :], in0=B(bgk), in1=gt[:], op=ALU.mult), [pk[bgk], gk], ["mT%d" % fb])
                nxt = f_rms_parts(oi + 1) if oi + 1 < len(order) else []
                if nxt:
                    nxt[0]()
                for d in range(8):
                    wt = wd[wdi % 3]
                    wkey = "wd%d" % (wdi % 3)
                    bank = 5 + d % 2
                    for fb in range(NFB):
                        mm(B(bank), wt[:, fb, :], mT[:, fb, :], fb == 0, fb == NFB - 1, [wkey, "mT%d" % fb], [pk[bank]])
                    if nxt and d == 0:
                        nxt[1]()
                    if nxt and d == 1:
                        nxt[2]()
                    o_ = ot[d % 2]
                    P.op("vector", I("tensor_tensor", out=o_[:], in0=B(bank), in1=x1g[b][:, d, :], op=ALU.add),
                         [pk[bank], "x1g%d" % b], ["ot%d" % (d % 2)])
                    dma(STQ, outT_v[:, d, c0:c0 + GW], o_[:], ["ot%d" % (d % 2)], ["out%d_%d" % (s, d)])
                    if d + 3 < 8:
                        dma("gpsimd", wd[wdi % 3][:].rearrange("p c n -> p (c n)"), wsrc("wd%d" % (d + 3)), [], [wkey])
                    wdi += 1

        P.finalize()
        P.emit()
    return nc


def own_positions(r):
    pos = np.zeros(NQ, np.int64)
    flag = np.ones(4, np.float32)
    for s in range(4):
        g = GROUPS[r][s]
        pos[512 * s:512 * s + 512] = 512 * g + np.arange(512)
        for i in range(2):
            p = 512 * g - 2 + i
            if p < 0:
                p = i
                flag[s] = 0.0
            pos[2048 + 2 * s + i] = p
    return pos, flag


def _consts(r):
    bf = ml_dtypes.bfloat16
    inv = (1.0 / (np.float32(10000.0) ** (np.arange(0, 64, 2, dtype=np.float32) / np.float32(64)))).astype(np.float32)
    ang = (np.arange(S, dtype=np.float32)[:, None] * inv[None, :]).astype(np.float32)
    cos = np.cos(ang).astype(np.float32)
    sin = np.sin(ang).astype(np.float32)
    p = np.arange(128)
    sgn = np.where((p % 64) < 32, -1.0, 1.0).astype(np.float32)
    cosk = np.ascontiguousarray(cos[:, p % 32].T)
    sink = np.ascontiguousarray((sin[:, p % 32] * sgn[None, :]).T)
    pos, flag = own_positions(r)
    cosq = np.ascontiguousarray(cosk[:, pos])
    sinq = np.ascontiguousarray(sink[:, pos])
    ident = np.eye(128, dtype=np.float32)
    sw = (p // 64) * 64 + ((p % 64) + 32) % 64
    perm = np.zeros((128, 128), np.float32)
    perm[sw, p] = 1.0
    bd64 = np.zeros((128, 128), np.float32)
    bd64[:64, :64] = 1.0 / 64
    bd64[64:, 64:] = 1.0 / 64
    cmat = np.concatenate([ident, perm, bd64, np.full((128, 128), 1.0 / 1024, np.float32),
                           np.full((128, 128), 1.0 / 128, np.float32), np.ones((128, 128), np.float32)], axis=1).astype(bf)
    k = np.arange(128)[:, None]
    am = np.zeros((4, 8, 128, GW), np.float32)
    for s, (c0, n, nkt) in enumerate(SLOTS):
        qpos = pos[c0:c0 + n][None, :]
        for jj in range(8):
            j = nkt - 8 + jj
            am[s, jj] = np.where(128 * j + k <= qpos, 0.0, -BIGM)
    assert np.array_equal(am[0], am[2]) and np.array_equal(am[1], am[3])
    amask = np.ascontiguousarray(am[0:2].transpose(2, 0, 1, 3).reshape(128, 2 * 8 * GW)).astype(bf)
    hpos = pos[2048:2056][None, :]
    hm = np.stack([np.where(128 * j + k <= hpos, 0.0, -BIGM) for j in range(32)], axis=1)
    hmask = np.ascontiguousarray(hm.reshape(128, 32 * 8)).astype(bf)
    gmask = np.zeros((128, 17, 16), np.float32)
    valid = np.zeros((128, 17, 16), np.float32)
    nidx = np.arange(16)[None, :]
    for qt in range(16):
        own = (pos[128 * qt:128 * qt + 128] // 256)[:, None]
        valid[:, qt, :] = (nidx < own)
    own = (pos[2048:2056] // 256)[:, None]
    valid[0:8, 16, :] = (nidx < own)
    gmask = np.where(valid > 0, 0.0, -1e30).astype(np.float32)
    oh = np.zeros((16, S), np.float32)
    for n_ in range(16):
        oh[n_, 256 * n_:256 * n_ + 256] = BIGB
    hflag = np.tile(flag[None, :], (128, 1)).astype(np.float32)
    return dict(cosk=cosk, sink=sink, cosq=cosq, sinq=sinq, cmat=cmat, amask=amask, hmask=hmask,
                gmask=np.ascontiguousarray(gmask.reshape(128, 17 * 16)), valid=np.ascontiguousarray(valid.reshape(128, 17 * 16)),
                hflag=hflag, oh16=oh.astype(bf)), pos


_NC_CACHE = {}


def kernel(x, norm1_g, w_in, b_gate, qn_a, kn_a, lam_q1, lam_k1, lam_q2, lam_k2, subln_g,
           qn_b, kn_b, w_a_proj, w_b_proj, w_out, norm2_g, w_up, conv_w, conv_b, w_down):
    f = np.float32
    x = np.asarray(x, f)
    Bn = x.shape[0]
    col = lambda v: np.ascontiguousarray(np.asarray(v, f).reshape(-1, 128).T)
    gn = np.stack([np.tile(np.asarray(v, f).reshape(64), 2) for v in (qn_a, kn_a, qn_b, kn_b)], axis=1)
    lamv = np.concatenate([np.tile(np.asarray(v, f).reshape(1, 64), (128, 1)) for v in (lam_q1, lam_k1, lam_q2, lam_k2)], axis=1)
    cwl = np.ascontiguousarray(np.asarray(conv_w, f)[0].T.reshape(NFB, 128, 3).transpose(1, 0, 2).reshape(128, NFB * 3))
    def pc(w, a, b_):
        w = np.asarray(w, f)
        C = w.shape[0] // 128
        return w[:, a:b_].reshape(C, 128, b_ - a).transpose(1, 0, 2).reshape(128, C * (b_ - a))
    Win, Wup, Wdn = np.asarray(w_in, f)[0], np.asarray(w_up, f)[0], np.asarray(w_down, f)[0]
    pieces = {"wk": pc(Win, 512, 1024), "wv": pc(Win, 1024, 1536), "wq": pc(Win, 0, 512),
              "wa": pc(np.asarray(w_a_proj, f)[0], 0, D), "wb": pc(np.asarray(w_b_proj, f)[0], 0, D), "wo": pc(np.asarray(w_out, f)[0], 0, D),
              "wupAu": pc(Wup, 0, 1408), "wupAg": pc(Wup, DFF, DFF + 1408), "wupBu": pc(Wup, 1408, DFF), "wupBg": pc(Wup, DFF + 1408, 2 * DFF)}
    for hp in range(2):
        pieces["wkm%d" % hp] = pc(Win, 2048 + 256 * hp, 2048 + 256 * hp + 256)
        pieces["wvm%d" % hp] = pc(Win, 2560 + 256 * hp, 2560 + 256 * hp + 256)
        pieces["wqm%d" % hp] = pc(Win, 1536 + 256 * hp, 1536 + 256 * hp + 256)
    for p_ in range(4):
        pieces["wga%d" % p_] = pc(Win, 3072 + 256 * p_, 3072 + 256 * p_ + 256)
        pieces["wgb%d" % p_] = pc(Win, 4096 + 256 * p_, 4096 + 256 * p_ + 256)
    for d_ in range(8):
        pieces["wd%d" % d_] = pc(Wdn, 128 * d_, 128 * d_ + 128)
    wpack = np.empty((128, WTOT), f)
    for nm, (o_, n_) in WOFF.items():
        assert pieces[nm].shape == (128, n_), (nm, pieces[nm].shape, n_)
        wpack[:, o_:o_ + n_] = pieces[nm]
    shared = {
        "wpack": wpack,
        "g1": col(norm1_g), "g2": col(norm2_g), "bg": col(b_gate),
        "cw": cwl, "cb": col(conv_b),
        "gn": np.ascontiguousarray(gn.astype(f)), "gsub": np.ascontiguousarray(np.asarray(subln_g, f).reshape(128, 1)),
        "lamv": np.ascontiguousarray(lamv.astype(f)),
    }
    cst = [_consts(r) for r in range(2)]
    in_maps = []
    for b in range(Bn):
        xTb = np.ascontiguousarray(x[b].T)
        for r in range(2):
            m = dict(shared)
            m.update(cst[r][0])
            m["xT"] = xTb
            m["xq"] = np.ascontiguousarray(xTb[:, cst[r][1]])
            in_maps.append(m)
    if "nc" not in _NC_CACHE:
        _NC_CACHE["nc"] = build_program()
    nc = _NC_CACHE["nc"]
    res = run_bass_kernel_spmd(nc, in_maps, core_ids=list(range(2 * Bn)))
    out = np.empty((Bn, S, D), f)
    for b in range(Bn):
        for r in range(2):
            o = np.asarray(res.results[2 * b + r]["outT"], f)
            out[b, cst[r][1][0:2048], :] = o.T
    if DEBUG:
        kernel.debug = res.results
    return out
```

```python
import numpy as np
import ml_dtypes
from contextlib import ExitStack
import concourse.bass as bass
import concourse.mybir as mybir
from concourse.bass_utils import run_bass_kernel_spmd

F32 = mybir.dt.float32
BF16 = mybir.dt.bfloat16
AF = mybir.ActivationFunctionType
ALU = mybir.AluOpType
AX = mybir.AxisListType

S = 4096
D = 1024
NG = 8
GW = 512
DFF = 2816
NFB = 22
EPS = 1e-6
BIGM = 30000.0
BIGB = 32768.0
ENGS = ["sync", "scalar", "vector", "gpsimd", "tensor"]
DEBUG = False
STOP = None
MAXG = 8
NOV = False
NOK = False
NOPREF = False
SKIP0 = False
S0JOBS = 13
STQ = 'gpsimd'
NOPIPE = False
USE_SQRT = False


def I(name, *args, **kw):
    return (name, args, kw)


class Op:
    __slots__ = ("eng", "fn", "deps", "dma", "sem", "val", "inc", "waits")


class Prog:
    def __init__(self, nc, es):
        self.nc = nc
        self.es = es
        self.ops = {e: [] for e in ENGS}
        self.lastw = {}
        self.readers = {}
        self.pending = {e: None for e in ENGS}
        self.final_dma = []

    def op(self, eng, fn, reads=(), writes=(), dma=False):
        o = Op()
        o.eng, o.fn, o.dma = eng, fn, dma
        o.inc = dma
        deps = set()
        for r in reads:
            w = self.lastw.get(r)
            if w is not None:
                deps.add(w)
            if r.startswith("ps") or r.startswith("ST"):
                for rd in self.readers.get(r, ()):
                    if rd.eng != eng:
                        deps.add(rd)
        for w_ in writes:
            w = self.lastw.get(w_)
            if w is not None:
                deps.add(w)
            for rd in self.readers.get(w_, ()):
                deps.add(rd)
        if self.pending[eng] is not None:
            deps.update(self.pending[eng])
            self.pending[eng] = None
        o.deps = [d for d in deps if not (d.eng == "tensor" and eng == "tensor" and not d.dma and not dma)]
        for r in reads:
            self.readers.setdefault(r, []).append(o)
        for w_ in writes:
            self.lastw[w_] = o
            self.readers[w_] = []
        self.ops[eng].append(o)
        return o

    def barrier(self):
        deps = []
        for e in ENGS:
            lastc = None
            for o in reversed(self.ops[e]):
                if not o.dma:
                    lastc = o
                    break
            if lastc is not None:
                deps.append(lastc)
            cnt = 0
            for o in reversed(self.ops[e]):
                if o.dma:
                    deps.append(o)
                    cnt += 1
                    if cnt >= 8:
                        break
        for e in ENGS:
            self.pending[e] = list(deps)
        self.lastw = {}
        self.readers = {}

    def finalize(self):
        nc, es = self.nc, self.es
        for e in ENGS:
            for o in self.ops[e]:
                for d in o.deps:
                    d.inc = True
        self.csem = {}
        self.dsem = {}
        for e in ENGS:
            self.csem[e] = [es.enter_context(nc.semaphore("c_%s_%d" % (e, i))) for i in range(2)]
            self.dsem[e] = [es.enter_context(nc.semaphore("d_%s_%d" % (e, i))) for i in range(8)]
        LIM = 30000
        for e in ENGS:
            cc = 0
            dcount = [0] * 8
            di = 0
            for o in self.ops[e]:
                o.waits = []
                if o.dma:
                    k = di % 8
                    di += 1
                    dcount[k] += 1
                    o.sem = self.dsem[e][k]
                    o.val = 16 * dcount[k]
                    if dcount[k] > 1:
                        o.waits.append((o.sem, o.val - 16))
                elif o.inc:
                    si = cc // LIM
                    o.sem = self.csem[e][si]
                    o.val = cc % LIM + 1
                    cc += 1
            assert cc < 2 * LIM, (e, cc)
            self.final_dma.append((e, [(self.dsem[e][k], 16 * dcount[k]) for k in range(8) if dcount[k] > 0]))
        for e in ENGS:
            for o in self.ops[e]:
                for d in o.deps:
                    o.waits.append((d.sem, d.val))

    def emit(self):
        nc = self.nc
        fin = dict(self.final_dma)
        with nc.Block() as block:
            for eng in ENGS:
                def body(e, eng=eng):
                    waited = {}
                    for o in self.ops[eng]:
                        for (sem, val) in o.waits:
                            if waited.get(id(sem), 0) < val:
                                e.wait_ge(sem, val)
                                waited[id(sem)] = val
                        inst = getattr(e, o.fn[0])(*o.fn[1], **o.fn[2])
                        if o.inc:
                            inst.then_inc(o.sem, 16 if o.dma else 1)
                    for (sem, val) in fin[eng]:
                        if waited.get(id(sem), 0) < val:
                            e.wait_ge(sem, val)
                getattr(block, eng)(body)


NQ = 2056
GROUPS = [[0, 3, 4, 7], [1, 2, 5, 6]]
SLOTS = [(0, 512, 8), (512, 512, 16), (1024, 512, 24), (1536, 512, 32)]
MINI = (2048, 8, 28)
CHUNKS = SLOTS + [MINI]


def _wlayout():
    off = {}
    cur = [0]

    def add(name, n):
        off[name] = (cur[0], n)
        cur[0] += n
    add("wk", 8 * 512)
    add("wv", 8 * 512)
    add("wq", 8 * 512)
    for hp in range(2):
        add("wkm%d" % hp, 8 * 256)
        add("wvm%d" % hp, 8 * 256)
        add("wqm%d" % hp, 8 * 256)
    for p in range(4):
        add("wga%d" % p, 8 * 256)
        add("wgb%d" % p, 8 * 256)
    add("wa", 4 * 1024)
    add("wb", 4 * 1024)
    add("wo", 8 * 1024)
    for nm in ("wupAu", "wupAg", "wupBu", "wupBg"):
        add(nm, 8 * 1408)
    for d in range(8):
        add("wd%d" % d, NFB * 128)
    return off, cur[0]


WOFF, WTOT = _wlayout()


def build_program():
    nc = bass.Bass("TRN2", target_bir_lowering=False)

    def din(name, shape, dt=F32):
        return nc.dram_tensor(name, list(shape), dt, kind="ExternalInput").ap()

    xT = din("xT", [D, S])
    xq = din("xq", [D, NQ])
    wpack = din("wpack", [128, WTOT])

    def wsrc(name):
        o, n = WOFF[name]
        return wpack[:, o:o + n]
    g1_d = din("g1", [128, 8])
    g2_d = din("g2", [128, 8])
    bg_d = din("bg", [128, 16])
    cw_d = din("cw", [128, NFB * 3])
    cb_d = din("cb", [128, NFB])
    gn_d = din("gn", [128, 4])
    gsub_d = din("gsub", [128, 1])
    lam_d = din("lamv", [128, 4 * 64])
    cosk_d = din("cosk", [128, S])
    sink_d = din("sink", [128, S])
    cosq_d = din("cosq", [128, NQ])
    sinq_d = din("sinq", [128, NQ])
    cmat_d = din("cmat", [128, 6 * 128], BF16)
    amask_d = din("amask", [128, 2 * 8 * GW], BF16)
    hmask_d = din("hmask", [128, 32 * 8], BF16)
    gmask_d = din("gmask", [128, 17 * 16])
    valid_d = din("valid", [128, 17 * 16])
    hflag_d = din("hflag", [128, 4])
    oh_d = din("oh16", [16, S], BF16)

    skind = "ExternalOutput" if DEBUG else "Internal"
    outT = nc.dram_tensor("outT", [D, 2048], F32, kind="ExternalOutput").ap()
    hT_d = nc.dram_tensor("hT_d", [D, S], BF16, kind=skind).ap()
    hq_d = nc.dram_tensor("hq_d", [D, NQ], BF16, kind=skind).ap()
    yT_d = nc.dram_tensor("yT_d", [512, NQ], F32, kind=skind).ap()
    obT_d = nc.dram_tensor("obT_d", [512, NQ], BF16, kind=skind).ap()
    x1_d = nc.dram_tensor("x1_d", [D, NQ], F32, kind=skind).ap()

    def cv(ap):
        return ap.rearrange("(c p) t -> p c t", p=128)

    xT_v, xq_v, hT_v, hq_v, yT_v, obT_v, x1_v, outT_v = (cv(a) for a in (xT, xq, hT_d, hq_d, yT_d, obT_d, x1_d, outT))

    with ExitStack() as es:
        P = Prog(nc, es)
        uid = [0]

        def sb(st, name, shape, dt):
            uid[0] += 1
            return st.enter_context(nc.sbuf_tensor("sb%d_%s" % (uid[0], name), list(shape), dt))

        cmat = sb(es, "cmat", [128, 6, 128], BF16)
        ident, perm, bd64, on1024, on128, ones = (cmat[:, i, :] for i in range(6))
        g1 = sb(es, "g1", [128, 8], F32)
        g2 = sb(es, "g2", [128, 8], F32)
        bg = sb(es, "bg", [128, 16], F32)
        cw = sb(es, "cw", [128, NFB, 3], F32)
        cb = sb(es, "cb", [128, NFB], F32)
        gn = sb(es, "gn", [128, 4], F32)
        gsub = sb(es, "gsub", [128, 1], F32)
        lamv = sb(es, "lamv", [128, 4, 64], F32)
        lamt = sb(es, "lamt", [128, 2, 64], F32)
        lams = sb(es, "lams", [128, 2], F32)
        neglam = sb(es, "neglam", [128, 1], F32)
        epsc = sb(es, "epsc", [128, 1], F32)
        hflag = sb(es, "hflag", [128, 4], F32)
        pp = [es.enter_context(nc.psum_tensor("pp%d" % i, [128, 2 * GW], F32)) for i in range(4)]
        pk = ["ps%d" % i for i in range(8)]

        STP = [0, 1, 3]
        STK = [["ps0", "ps1"], ["ps2", "ps3"], ["ps6", "ps7"]]

        def B(i, c0=0, c1=GW, p0=0, p1=128):
            off = (i % 2) * GW
            return pp[i // 2][p0:p1, off + c0:off + c1]

        def dma(eng, out, in_, reads, writes):
            return P.op(eng, I("dma_start", out=out, in_=in_), reads, writes, dma=True)

        def mm(out, lhsT, rhs, start, stop, reads, writes):
            return P.op("tensor", I("matmul", out, lhsT, rhs, start=start, stop=stop), reads, writes)

        dma("sync", cmat[:].rearrange("p a b -> p (a b)"), cmat_d, [], ["cmat"])
        dma("sync", g1[:], g1_d, [], ["g1"])
        dma("sync", g2[:], g2_d, [], ["g2"])
        dma("sync", bg[:], bg_d, [], ["bg"])
        dma("sync", cw[:].rearrange("p a b -> p (a b)"), cw_d, [], ["cw"])
        dma("sync", cb[:], cb_d, [], ["cb"])
        dma("sync", gn[:], gn_d, [], ["gn"])
        dma("sync", gsub[:], gsub_d, [], ["gsub"])
        dma("sync", hflag[:], hflag_d, [], ["hflag"])
        dma("sync", lamv[:].rearrange("p a b -> p (a b)"), lam_d, [], ["lamv"])
        P.op("vector", I("memset", epsc[:], EPS), [], ["epsc"])
        P.op("vector", I("tensor_tensor", out=lamt[:, 0, :], in0=lamv[:, 0, :], in1=lamv[:, 1, :], op=ALU.mult), ["lamv"], ["lamt0"])
        P.op("vector", I("tensor_tensor", out=lamt[:, 1, :], in0=lamv[:, 2, :], in1=lamv[:, 3, :], op=ALU.mult), ["lamv"], ["lamt1"])
        P.op("vector", I("reduce_sum", out=lams[:], in_=lamt[:], axis=AX.X), ["lamt0", "lamt1"], ["lams"])
        P.op("scalar", I("activation", out=lams[:], in_=lams[:], func=AF.Exp), ["lams"], ["lams"])
        P.op("vector", I("scalar_tensor_tensor", out=neglam[:], in0=lams[:, 1:2], scalar=-0.2, in1=lams[:, 0:1],
                         op0=ALU.add, op1=ALU.subtract), ["lams"], ["neglam"])
        P.op("vector", I("tensor_scalar", out=gsub[:], in0=gsub[:], scalar1=0.8, scalar2=None, op0=ALU.mult), ["gsub"], ["gsub"])

        def rstd_from(bank, n, lnr, rstd, key):
            if USE_SQRT:
                P.op("scalar", I("activation", out=lnr[:, 0:n], in_=B(bank, 0, n), func=AF.Sqrt, bias=epsc[:]), [pk[bank], "epsc"], [key + "ln"])
                P.op("vector", I("reciprocal", out=rstd[:, 0:n], in_=lnr[:, 0:n]), [key + "ln"], [key])
                return
            P.op("scalar", I("activation", out=lnr[:, 0:n], in_=B(bank, 0, n), func=AF.Ln, bias=epsc[:]), [pk[bank], "epsc"], [key + "ln"])
            P.op("scalar", I("activation", out=rstd[:, 0:n], in_=lnr[:, 0:n], func=AF.Exp, scale=-0.5), [key + "ln"], [key])

        def rms_chunk(tag, src, skey, n, sq, lnr, rstd, bank):
            P.op("scalar", I("activation", out=sq[:, :, 0:n], in_=src[:, :, 0:n], func=AF.Square), [skey], [tag + "sq"])
            for c in range(8):
                mm(B(bank, 0, n), on1024, sq[:, c, 0:n], c == 0, c == 7, [tag + "sq", "cmat"], [pk[bank]])
            rstd_from(bank, n, lnr, rstd, tag + "rstd")

        class NR:
            def __init__(self, st, name):
                self.bufs = []
                for i in range(2):
                    self.bufs.append(dict(
                        sq=sb(st, name + "sq%d" % i, [128, GW], BF16), pbf=sb(st, name + "pbf%d" % i, [128, GW], BF16),
                        t1=sb(st, name + "t1%d" % i, [128, GW], F32), t2=sb(st, name + "t2%d" % i, [128, GW], F32),
                        ln=sb(st, name + "ln%d" % i, [128, GW], F32), rs=sb(st, name + "rs%d" % i, [128, GW], F32)))
                self.k = 0
                self.pending = None

            def flush(self):
                if self.pending is not None:
                    f = self.pending
                    self.pending = None
                    f()

            def block(self, bank, n, gcol, cos_ap, sin_ap, outs, ckey="cos", skey="sin", pool_out=False):
                i = self.k % 2
                self.k += 1
                bf = self.bufs[i]
                t = "nr%d" % i
                pm, psw = (2, 3) if i == 0 else (4, 5)
                pa = B(bank, 0, n)
                P.op("scalar", I("activation", out=bf["sq"][:, 0:n], in_=pa, func=AF.Square), [pk[bank]], [t + "sq"])
                P.op("scalar", I("activation", out=bf["pbf"][:, 0:n], in_=pa, func=AF.Copy, scale=gcol), [pk[bank], "gn"], [t + "pbf"])
                P.op("vector", I("scalar_tensor_tensor", out=bf["t1"][:, 0:n], in0=pa, scalar=gcol, in1=cos_ap, op0=ALU.mult, op1=ALU.mult),
                     [pk[bank], "gn", ckey, t + "pbf", t + "sq"], [t + "t1"])
                prev = self.pending

                def phaseB():
                    mm(B(pm, 0, n), bd64, bf["sq"][:, 0:n], True, True, [t + "sq", "cmat"], [pk[pm]])
                    mm(B(psw, 0, n), perm, bf["pbf"][:, 0:n], True, True, [t + "pbf", "cmat"], [pk[psw]])
                    rstd_from(pm, n, bf["ln"], bf["rs"], t + "rs")
                    P.op("vector", I("tensor_tensor", out=bf["t2"][:, 0:n], in0=B(psw, 0, n), in1=sin_ap, op=ALU.mult), [pk[psw], skey], [t + "t2"])
                    P.op("vector", I("tensor_tensor", out=bf["t1"][:, 0:n], in0=bf["t1"][:, 0:n], in1=bf["t2"][:, 0:n], op=ALU.add),
                         [t + "t1", t + "t2"], [t + "t1"])
                    for oi_, (oap, lo, hi, okey) in enumerate(outs):
                        eng_o = "gpsimd" if (len(outs) == 2 and oi_ == 0 and pool_out) else "vector"
                        P.op(eng_o, I("tensor_tensor", out=oap, in0=bf["t1"][lo:hi, 0:n], in1=bf["rs"][lo:hi, 0:n], op=ALU.mult),
                             [t + "t1", t + "rs"], [okey])
                self.pending = phaseB
                if prev is not None:
                    prev()
                if NOPIPE:
                    self.flush()

        def load_amasks(st):
            amask = sb(st, "amask", [128, 2, 8, GW], BF16)
            hmask = sb(st, "hmask", [128, 32, 8], BF16)
            dma("sync", amask[:].rearrange("p a b c -> p (a b c)"), amask_d, [], ["amask"])
            dma("sync", hmask[:].rearrange("p a b -> p (a b)"), hmask_d, [], ["hmask"])
            return amask, hmask

        stab = ExitStack()
        coskT = sb(stab, "coskT", [128, S], F32)
        sinkT = sb(stab, "sinkT", [128, S], F32)

        with ExitStack() as so:
            KT = [sb(so, "KTd%d" % h, [128, S], BF16) for h in range(4)]
            Vd = sb(so, "Vd", [128, 32, 512], BF16)
            with ExitStack() as st:
                wk = sb(st, "wk", [128, 8, 512], BF16)
                wv = sb(st, "wv", [128, 8, 512], BF16)
                dma("gpsimd", wk[:].rearrange("p c n -> p (c n)"), wsrc("wk"), [], ["wk"])
                dma("gpsimd", wv[:].rearrange("p c n -> p (c n)"), wsrc("wv"), [], ["wv"])
                xg = [sb(st, "s0_xg%d" % i, [128, 8, GW], F32) for i in range(2)]
                sq = sb(st, "s0_sq", [128, 8, GW], BF16)
                lnr = sb(st, "s0_ln", [128, GW], F32)
                rstd2 = [sb(st, "s0_rstd%d" % i, [128, GW], F32) for i in range(2)]
                hT = [sb(st, "s0_hT%d" % i, [128, 8, GW], BF16) for i in range(2)]
                nr = NR(st, "nra")
                jobs = []
                for g in range(NG):
                    jobs.append(("g", g, xT_v, hT_v, g * GW, GW, "hT_d%d" % g))
                    if g < len(CHUNKS):
                        c0_, n_, _ = CHUNKS[g]
                        jobs.append(("o", g, xq_v, hq_v, c0_, n_, "hq_d%d" % g))

                def s0_load(i):
                    _, _, src, _, c0, n, _k = jobs[i]
                    dma("sync", xg[i % 2][:, :, 0:n], src[:, :, c0:c0 + n], [], ["s0xg%d" % (i % 2)])
                def s0_rms(i):
                    kind, g, src, dst, c0, n, okey = jobs[i]
                    b = i % 2
                    P.op("scalar", I("activation", out=sq[:, :, 0:n], in_=xg[b][:, :, 0:n], func=AF.Square), ["s0xg%d" % b], ["s0sq"])
                    for c in range(8):
                        mm(B(7, 0, n), on1024, sq[:, c, 0:n], c == 0, c == 7, ["s0sq", "cmat"], [pk[7]])
                    rstd_from(7, n, lnr, rstd2[b], "s0rstd%d" % b)
                    for c in range(8):
                        P.op("vector", I("scalar_tensor_tensor", out=hT[b][:, c, 0:n], in0=xg[b][:, c, 0:n], scalar=g1[:, c:c + 1],
                                         in1=rstd2[b][:, 0:n], op0=ALU.mult, op1=ALU.mult), ["s0xg%d" % b, "g1", "s0rstd%d" % b], ["s0hT%d_%d" % (b, c)])
                    dma(STQ, dst[:, :, c0:c0 + n], hT[b][:, :, 0:n], ["s0hT%d_%d" % (b, c) for c in range(8)], [okey])

                def s0_kv(i):
                    kind, g, src, dst, c0, n, okey = jobs[i]
                    b = i % 2
                    if kind != "g":
                        return
                    for h in range(4):
                        bank = h % 2
                        for c in range(8):
                            mm(B(bank), wk[:, c, 128 * h:128 * h + 128], hT[b][:, c, :], c == 0, c == 7, ["wk", "s0hT%d_%d" % (b, c)], [pk[bank]])
                        nr.block(bank, GW, gn[:, 1:2], coskT[:, c0:c0 + GW], sinkT[:, c0:c0 + GW],
                                 [(KT[h][:, c0:c0 + GW], 0, 128, "KT%d_%d" % (h, g))], ckey="cosk", skey="sink")
                        for c in range(8):
                            mm(B(6), hT[b][:, c, 128 * h:128 * h + 128], wv[:, c, :], c == 0, c == 7, ["wv", "s0hT%d_%d" % (b, c)], [pk[6]])
                        P.op("scalar", I("activation", out=Vd[:, 4 * g + h, :], in_=B(6), func=AF.Copy), [pk[6]], ["V_%d" % (4 * g + h)])
                s0_load(0)
                if len(jobs) > 1:
                    s0_load(1)
                dma("sync", coskT[:], cosk_d, [], ["cosk"])
                dma("sync", sinkT[:], sink_d, [], ["sink"])
                s0_rms(0)
                for i in range(len(jobs)):
                    if i + 1 < len(jobs):
                        s0_rms(i + 1)
                    if i + 2 < len(jobs):
                        s0_load(i + 2)
                    s0_kv(i)
                nr.flush()
            P.barrier()
            QT = [[sb(so, "QTd%d_%d" % (h, cc), [128, NQ], BF16) for cc in range(2)] for h in range(4)]
            for h in range(4):
                P.op("gpsimd", I("memset", QT[h][0][64:128, :], 0.0), [], ["QTz%d_0" % h])
                P.op("gpsimd", I("memset", QT[h][1][0:64, :], 0.0), [], ["QTz%d_1" % h])
            with ExitStack() as st:
                cosT = sb(st, "cosq", [128, NQ], F32)
                sinT = sb(st, "sinq", [128, NQ], F32)
                dma("sync", cosT[:], cosq_d, [], ["cos"])
                dma("sync", sinT[:], sinq_d, [], ["sin"])
                wq = sb(st, "wq", [128, 8, 512], BF16)
                dma("gpsimd", wq[:].rearrange("p c n -> p (c n)"), wsrc("wq"), [], ["wq"])
                hTg = [sb(st, "hqg%d" % i, [128, 8, GW], BF16) for i in range(2)]
                nr = NR(st, "nrq")
                dma("sync", hTg[0][:, :, 0:GW], hq_v[:, :, 0:GW], [], ["hTg0"])
                for ci, (c0, n, _) in enumerate(CHUNKS):
                    b = ci % 2
                    if ci + 1 < len(CHUNKS):
                        c0n, nn, _ = CHUNKS[ci + 1]
                        dma("sync", hTg[1 - b][:, :, 0:nn], hq_v[:, :, c0n:c0n + nn], [], ["hTg%d" % (1 - b)])
                    for h in range(4):
                        bank = h % 2
                        for c in range(8):
                            mm(B(bank, 0, n), wq[:, c, 128 * h:128 * h + 128], hTg[b][:, c, 0:n], c == 0, c == 7, ["wq", "hTg%d" % b], [pk[bank]])
                        nr.block(bank, n, gn[:, 0:1], cosT[:, c0:c0 + n], sinT[:, c0:c0 + n],
                                 [(QT[h][0][0:64, c0:c0 + n], 0, 64, "QT%d_%d_0" % (h, ci)),
                                  (QT[h][1][64:128, c0:c0 + n], 64, 128, "QT%d_%d_1" % (h, ci))])
                nr.flush()
            P.barrier()
            if STOP == "qa":
                P.finalize()
                P.emit()
                return nc
            with ExitStack() as st:
                amask, hmask = load_amasks(st)
                Pt2 = [sb(st, "Pt2_%d" % i, [128, 2 * GW], BF16) for i in range(3)]
                ep = [dict((nm, sb(st, "ae_%s%d" % (nm, i), [128, GW], F32)) for nm in ("l0", "l1", "o0", "o1")) for i in range(2)]
                dacc = [sb(st, "dacc%d" % i, [128, 2 * GW], F32) for i in range(2)]
                daccb = sb(st, "daccb", [128, 2 * GW], BF16)
                hi_ = 0
                for ci, (c0, n, nkt) in enumerate(CHUNKS):
                    regular = n == GW
                    for h in range(4):
                        if regular:
                            batches = [[(j, 0), (j, 1)] for j in range(nkt)]
                        else:
                            batches = [[(j, cc) for j in range(nkt) for cc in (0, 1)]]
                        nb = len(batches)

                        def emit_S(bi):
                            st_ = bi % 3
                            units = batches[bi]
                            for ui, (j, cc) in enumerate(units):
                                off = ui * n
                                lo, hi = 64 * cc, 64 * cc + 64
                                if regular:
                                    masked = j >= nkt - 8
                                    mk = amask[:, ci % 2, j - (nkt - 8), :] if masked else None
                                else:
                                    masked = True
                                    mk = hmask[:, j, :]
                                oap = pp[STP[st_]][:, off:off + n]
                                mm(oap, KT[h][:, 128 * j:128 * j + 128], QT[h][cc][:, c0:c0 + n], True, not masked,
                                   ["KT%d_%d" % (h, j // 4), "QT%d_%d_%d" % (h, ci, cc), "QTz%d_%d" % (h, cc)], STK[st_])
                                if masked:
                                    mm(oap, ident, mk, False, True, ["cmat", "amask", "hmask"], STK[st_])
                            w = len(units) * n
                            P.op("scalar", I("activation", out=Pt2[st_][:, 0:w], in_=pp[STP[st_]][:, 0:w], func=AF.Exp, scale=0.125),
                                 STK[st_], ["Pt2_%d" % st_])

                        def emit_PV(bi):
                            st_ = bi % 3
                            units = batches[bi]
                            for ui, (j, cc) in enumerate(units):
                                off = ui * n
                                mm(B(4 + cc, 0, n), Vd[:, j, 128 * h:128 * h + 128], Pt2[st_][:, off:off + n], j == 0, j == nkt - 1,
                                   ["V_%d" % j, "Pt2_%d" % st_], [pk[4 + cc]])
                                if not regular:
                                    mm(B(6 + cc, 0, n), ones, Pt2[st_][:, off:off + n], j == 0, j == nkt - 1, ["cmat", "Pt2_%d" % st_], [pk[6 + cc]])
                            if regular:
                                ac = dacc[hi_ % 2]
                                akey = "dacc%d" % (hi_ % 2)
                                for (eng_, lo_, hi_c) in (("vector", 0, 800), ("gpsimd", 800, 2 * GW)):
                                    ak2 = akey + eng_
                                    if units[0][0] == 0:
                                        P.op(eng_, I("tensor_copy", out=ac[:, lo_:hi_c], in_=Pt2[st_][:, lo_:hi_c]), ["Pt2_%d" % st_], [ak2])
                                    else:
                                        P.op(eng_, I("tensor_tensor", out=ac[:, lo_:hi_c], in0=ac[:, lo_:hi_c], in1=Pt2[st_][:, lo_:hi_c], op=ALU.add),
                                             ["Pt2_%d" % st_, ak2], [ak2])
                        for bi in range(min(2, nb)):
                            emit_S(bi)
                        for bi in range(nb):
                            if bi + 2 < nb:
                                emit_S(bi + 2)
                            emit_PV(bi)
                        e_ = ep[hi_ % 2]
                        ek = "ae%d" % (hi_ % 2)
                        if regular:
                            P.op("vector", I("tensor_copy", out=daccb[:], in_=dacc[hi_ % 2][:]),
                                 ["dacc%dvector" % (hi_ % 2), "dacc%dgpsimd" % (hi_ % 2)], ["daccb"])
                            mm(B(6, 0, n), ones, daccb[:, 0:GW], True, True, ["cmat", "daccb"], [pk[6]])
                            mm(B(7, 0, n), ones, daccb[:, GW:2 * GW], True, True, ["cmat", "daccb"], [pk[7]])
                        hi_ += 1
                        P.op("scalar", I("activation", out=e_["l0"][:, 0:n], in_=B(6, 0, n), func=AF.Ln), [pk[6]], [ek + "l0"])
                        P.op("scalar", I("activation", out=e_["l1"][:, 0:n], in_=B(7, 0, n), func=AF.Ln), [pk[7]], [ek + "l1"])
                        P.op("vector", I("tensor_copy", out=e_["o0"][:, 0:n], in_=B(4, 0, n)), [pk[4]], [ek + "o0"])
                        P.op("vector", I("tensor_copy", out=e_["o1"][:, 0:n], in_=B(5, 0, n)), [pk[5]], [ek + "o1"])
                        P.op("scalar", I("activation", out=e_["l0"][:, 0:n], in_=e_["l0"][:, 0:n], func=AF.Exp, scale=-1.0), [ek + "l0"], [ek + "l0"])
                        P.op("scalar", I("activation", out=e_["l1"][:, 0:n], in_=e_["l1"][:, 0:n], func=AF.Exp, scale=-1.0), [ek + "l1"], [ek + "l1"])
                        P.op("vector", I("tensor_tensor", out=e_["o0"][:, 0:n], in0=e_["o0"][:, 0:n], in1=e_["l0"][:, 0:n], op=ALU.mult),
                             [ek + "o0", ek + "l0"], [ek + "o0"])
                        P.op("vector", I("tensor_tensor", out=e_["o1"][:, 0:n], in0=e_["o1"][:, 0:n], in1=e_["l1"][:, 0:n], op=ALU.mult),
                             [ek + "o1", ek + "l1"], [ek + "o1"])
                        P.op("vector", I("scalar_tensor_tensor", out=e_["o0"][:, 0:n], in0=e_["o1"][:, 0:n], scalar=neglam[:], in1=e_["o0"][:, 0:n],
                                         op0=ALU.mult, op1=ALU.add), [ek + "o0", ek + "o1", "neglam"], [ek + "o0"])
                        dma(STQ, yT_v[:, h, c0:c0 + n], e_["o0"][:, 0:n], [ek + "o0"], ["yT_d%d" % ci])
        P.barrier()

        if STOP == "2a":
            P.finalize()
            P.emit()
            return nc
        for hp in range(2):
            with ExitStack() as so:
                KT = [sb(so, "KTm%d" % h, [80, S], BF16) for h in range(4)]
                Vm = sb(so, "Vm", [128, 32, 4, 128], BF16)
                QT = [sb(so, "QTm%d" % h, [80, NQ], BF16) for h in range(4)]
                kms = sb(so, "kms", [64, 4, 16], F32)
                kmT = sb(so, "kmT", [64, 4, 16], BF16)
                with ExitStack() as st:
                    wk = sb(st, "wkm", [128, 8, 256], BF16)
                    wv = sb(st, "wvm", [128, 8, 256], BF16)
                    dma("gpsimd", wk[:].rearrange("p c n -> p (c n)"), wsrc("wkm%d" % hp), [], ["wk"])
                    dma("gpsimd", wv[:].rearrange("p c n -> p (c n)"), wsrc("wvm%d" % hp), [], ["wv"])
                    for h in range(4):
                        dma("sync", KT[h][64:80, :], oh_d, [], ["KToh%d" % h])
                    P.op("gpsimd", I("memset", Vm[:, :, :, 64:128], 1.0), [], ["Vones"])
                    hTg = [sb(st, "hTgm%d" % i, [128, 8, GW], BF16) for i in range(2)]
                    nr = NR(st, "nrkm")
                    dma("sync", hTg[0][:], hT_v[:, :, 0:GW], [], ["hTg0"])
                    cosT = sb(st, "cosqm", [128, NQ], F32)
                    sinT = sb(st, "sinqm", [128, NQ], F32)
                    dma("sync", cosT[:], cosq_d, [], ["cos"])
                    dma("sync", sinT[:], sinq_d, [], ["sin"])
                    wq = sb(st, "wqm", [128, 8, 256], BF16)
                    dma("gpsimd", wq[:].rearrange("p c n -> p (c n)"), wsrc("wqm%d" % hp), [], ["wq"])
                    gmask = sb(st, "gmask", [128, 17, 16], F32)
                    valid = sb(st, "valid", [128, 17, 16], F32)
                    dma("sync", gmask[:].rearrange("p a b -> p (a b)"), gmask_d, [], ["gmask"])
                    dma("sync", valid[:].rearrange("p a b -> p (a b)"), valid_d, [], ["valid"])
                    for g in range(NG):
                        b = g % 2
                        c0 = g * GW
                        if g + 1 < NG:
                            dma("sync", hTg[1 - b][:], hT_v[:, :, c0 + GW:c0 + 2 * GW], [], ["hTg%d" % (1 - b)])
                        for bl in range(2):
                            bank = bl
                            for c in range(8):
                                mm(B(bank), wk[:, c, 128 * bl:128 * bl + 128], hTg[b][:, c, :], c == 0, c == 7, ["wk", "hTg%d" % b], [pk[bank]])
                            nr.block(bank, GW, gn[:, 3:4], coskT[:, c0:c0 + GW], sinkT[:, c0:c0 + GW],
                                     [(KT[2 * bl][0:64, c0:c0 + GW], 0, 64, "KT%d_%d" % (2 * bl, g)),
                                      (KT[2 * bl + 1][0:64, c0:c0 + GW], 64, 128, "KT%d_%d" % (2 * bl + 1, g))], ckey="cosk", skey="sink", pool_out=True)
                            if g > 0:
                                for h_ in (2 * bl, 2 * bl + 1):
                                    P.op("vector", I("reduce_sum", out=kms[:, h_, 2 * (g - 1):2 * g],
                                                     in_=KT[h_][0:64, c0 - GW:c0].rearrange("p (n k) -> p n k", k=256), axis=AX.X),
                                         ["KT%d_%d" % (h_, g - 1)], ["kms%d_%d" % (h_, g - 1)])
                        for j in range(4):
                            bank = 6 + j % 2
                            for c in range(8):
                                mm(B(bank, 0, 256), hTg[b][:, c, 128 * j:128 * j + 128], wv[:, c, :], c == 0, c == 7, ["wv", "hTg%d" % b], [pk[bank]])
                            if j == 0:
                                nr.flush()
                            P.op("scalar", I("activation", out=Vm[:, 4 * g + j, :, 0:64], in_=B(bank, 0, 256).rearrange("p (h d) -> p h d", h=4),
                                             func=AF.Copy), [pk[bank]], ["V_%d" % (4 * g + j)])
                    nr.flush()
                    for h in range(4):
                        P.op("vector", I("reduce_sum", out=kms[:, h, 2 * (NG - 1):2 * NG],
                                         in_=KT[h][0:64, (NG - 1) * GW:NG * GW].rearrange("p (n k) -> p n k", k=256), axis=AX.X),
                             ["KT%d_%d" % (h, NG - 1)], ["kms%d_%d" % (h, NG - 1)])
                        P.op("vector", I("tensor_scalar", out=kmT[:, h, :], in0=kms[:, h, :], scalar1=1.0 / 256.0, scalar2=None, op0=ALU.mult),
                             ["kms%d_%d" % (h, g_) for g_ in range(NG)], ["kmT%d" % h])
                    gb = sb(st, "gb", [128, 16, 16], F32)
                    sel = sb(st, "sel", [128, 16, 16], F32)
                    mx = sb(st, "mx", [128, 16, 8], F32)
                    MBs = [sb(st, "MB%d" % i, [128, 4, 4, 80], BF16) for i in range(2)]
                    P.op("gpsimd", I("memset", MBs[0][:], 0.0), [], ["MB0"])
                    P.op("gpsimd", I("memset", MBs[1][:], 0.0), [], ["MB1"])
                    def gate(ci):
                        c0, n, _ = CHUNKS[ci]
                        gbk = 6 + ci % 2
                        MB = MBs[ci % 2]
                        mbk = "MB%d" % (ci % 2)
                        regular = n == GW
                        nt = 4 if regular else 1
                        rows = 128 if regular else 8
                        qt0 = 4 * ci if regular else 16
                        for h in range(4):
                            for t in range(nt):
                                wcol = 128 if regular else 8
                                mm(B(gbk, (h * nt + t) * 16, (h * nt + t) * 16 + 16, 0, rows), QT[h][0:64, c0 + wcol * t:c0 + wcol * t + wcol], kmT[:, h, :],
                                   True, True, ["QT%d_%d" % (h, ci), "kmT%d" % h], [pk[gbk]])
                        nn_ = 4 * nt
                        gbv = gb[0:rows, 0:nn_, :].rearrange("p (h t) n -> p h t n", h=4)
                        selv = sel[0:rows, 0:nn_, :].rearrange("p (h t) n -> p h t n", h=4)
                        P.op("vector", I("tensor_tensor", out=gbv, in0=B(gbk, 0, nn_ * 16, 0, rows).rearrange("p (h t n) -> p h t n", h=4, t=nt),
                                         in1=gmask[0:rows, qt0:qt0 + nt, :].unsqueeze(1).to_broadcast([rows, 4, nt, 16]), op=ALU.add),
                             [pk[gbk], "gmask"], ["gb"])
                        for i in range(nn_):
                            P.op("vector", I("max", out=mx[0:rows, i, :], in_=gb[0:rows, i, :]), ["gb"], ["mx"])
                        P.op("vector", I("tensor_tensor", out=sel[0:rows, 0:nn_, :], in0=gb[0:rows, 0:nn_, :],
                                         in1=mx[0:rows, 0:nn_, 2:3].to_broadcast([rows, nn_, 16]), op=ALU.is_ge), ["gb", "mx"], ["sel"])
                        P.op("vector", I("scalar_tensor_tensor", out=MB[0:rows, :, 0:nt, 64:80], in0=selv, scalar=1.0,
                                         in1=valid[0:rows, qt0:qt0 + nt, :].unsqueeze(1).to_broadcast([rows, 4, nt, 16]),
                                         op0=ALU.subtract, op1=ALU.mult), ["sel", "valid", mbk], [mbk])
                        for h in range(4):
                            tbk = 2 + h
                            for t in range(nt):
                                wcol = 128 if regular else 8
                                mm(B(tbk, wcol * t, wcol * t + wcol, 0, 80), MB[0:rows, h, t, :], ident[0:rows, 0:rows], True, True, [mbk, "cmat"], [pk[tbk]])
                            P.op("vector", I("tensor_copy", out=QT[h][64:80, c0:c0 + n], in_=B(tbk, 0, n, 64, 80)), [pk[tbk]], ["QTa%d_%d" % (h, ci)])
                    dma("sync", hTg[0][:, :, 0:GW], hq_v[:, :, 0:GW], [], ["hTg0"])
                    for ci, (c0, n, _) in enumerate(CHUNKS):
                        b = ci % 2
                        if ci + 1 < len(CHUNKS):
                            c0n, nn, _ = CHUNKS[ci + 1]
                            dma("sync", hTg[1 - b][:, :, 0:nn], hq_v[:, :, c0n:c0n + nn], [], ["hTg%d" % (1 - b)])
                        for bl in range(2):
                            bank = bl
                            for c in range(8):
                                mm(B(bank, 0, n), wq[:, c, 128 * bl:128 * bl + 128], hTg[b][:, c, 0:n], c == 0, c == 7, ["wq", "hTg%d" % b], [pk[bank]])
                            nr.block(bank, n, gn[:, 2:3], cosT[:, c0:c0 + n], sinT[:, c0:c0 + n],
                                     [(QT[2 * bl][0:64, c0:c0 + n], 0, 64, "QT%d_%d" % (2 * bl, ci)),
                                      (QT[2 * bl + 1][0:64, c0:c0 + n], 64, 128, "QT%d_%d" % (2 * bl + 1, ci))])
                    nr.flush()
                    for ci in range(len(CHUNKS)):
                        gate(ci)
                P.barrier()
                with ExitStack() as st:
                    amask, hmask = load_amasks(st)
                    Pt2 = [sb(st, "Pt2m_%d" % i, [128, 2 * GW], BF16) for i in range(3)]
                    rl = [sb(st, "m_rl%d" % i, [64, GW], F32) for i in range(2)]
                    obt = [sb(st, "m_ob%d" % i, [64, GW], BF16) for i in range(2)]
                    hi_ = 0
                    for ci, (c0, n, nkt) in enumerate(CHUNKS):
                        regular = n == GW
                        for h in range(4):
                            hh = 4 * hp + h
                            if regular:
                                batches = [[j, j + 1] for j in range(0, nkt, 2)]
                            else:
                                batches = [list(range(nkt))]
                            nb = len(batches)
                            od = 4 + hi_ % 2

                            def emit_S(bi):
                                st_ = bi % 3
                                units = batches[bi]
                                for ui, j in enumerate(units):
                                    off = ui * n
                                    if regular:
                                        masked = j >= nkt - 8
                                        mk = amask[:, ci % 2, j - (nkt - 8), :] if masked else None
                                    else:
                                        masked = True
                                        mk = hmask[:, j, :]
                                    oap = pp[STP[st_]][:, off:off + n]
                                    mm(oap, KT[h][0:80, 128 * j:128 * j + 128], QT[h][0:80, c0:c0 + n], True, not masked,
                                       ["KT%d_%d" % (h, j // 4), "KToh%d" % h, "QT%d_%d" % (h, ci), "QTa%d_%d" % (h, ci)], STK[st_])
                                    if masked:
                                        mm(oap, ident, mk, False, True, ["cmat", "amask", "hmask"], STK[st_])
                                w = len(units) * n
                                P.op("scalar", I("activation", out=Pt2[st_][:, 0:w], in_=pp[STP[st_]][:, 0:w], func=AF.Exp, scale=0.125),
                                     STK[st_], ["Pt2_%d" % st_])

                            def emit_PV(bi):
                                st_ = bi % 3
                                units = batches[bi]
                                for ui, j in enumerate(units):
                                    off = ui * n
                                    mm(B(od, 0, n), Vm[:, j, h, :], Pt2[st_][:, off:off + n], j == 0, j == nkt - 1,
                                       ["V_%d" % j, "Vones", "Pt2_%d" % st_], [pk[od]])
                            for bi in range(min(2, nb)):
                                emit_S(bi)
                            for bi in range(nb):
                                if bi + 2 < nb:
                                    emit_S(bi + 2)
                                emit_PV(bi)
                            r_ = rl[hi_ % 2]
                            o_ = obt[hi_ % 2]
                            ek = "me%d" % (hi_ % 2)
                            hi_ += 1
                            P.op("vector", I("reciprocal", out=r_[:, 0:n], in_=B(od, 0, n, 64, 128)), [pk[od]], [ek + "r"])
                            P.op("vector", I("tensor_tensor", out=o_[:, 0:n], in0=B(od, 0, n, 0, 64), in1=r_[:, 0:n], op=ALU.mult), [pk[od], ek + "r"], [ek + "o"])
                            dma(STQ, obT_d[64 * hh:64 * hh + 64, c0:c0 + n], o_[:, 0:n], [ek + "o"], ["obT_d%d" % ci])
            P.barrier()

        if STOP == "2b":
            P.finalize()
            P.emit()
            return nc
        stab.close()
        sw = es.enter_context(ExitStack())
        wupA = sb(sw, "wupA", [128, 2, 8, 1408], BF16)
        with ExitStack() as st:
            wg = sb(st, "wg", [128, 4, 2, 8, 256], BF16)
            wa = sb(st, "wa", [128, 4, D], BF16)
            wb = sb(st, "wb", [128, 4, D], BF16)
            wo = sb(st, "wo", [128, 8, D], BF16)
            def wg_piece(p_):
                dma("gpsimd", wg[:, p_, 0, :, :].rearrange("p c n -> p (c n)"), wsrc("wga%d" % p_), [], ["wga%d" % p_])
                dma("gpsimd", wg[:, p_, 1, :, :].rearrange("p c n -> p (c n)"), wsrc("wgb%d" % p_), [], ["wgb%d" % p_])
            wg_piece(0)
            dma("gpsimd", wa[:].rearrange("p c n -> p (c n)"), wsrc("wa"), [], ["wa"])
            dma("gpsimd", wb[:].rearrange("p c n -> p (c n)"), wsrc("wb"), [], ["wb"])
            for p_ in range(1, 4):
                wg_piece(p_)
            dma("gpsimd", wo[:].rearrange("p c n -> p (c n)"), wsrc("wo"), [], ["wo"])
            dma("gpsimd", wupA[:, 0, :, :].rearrange("p c n -> p (c n)"), wsrc("wupAu"), [], ["wupAu"])
            dma("gpsimd", wupA[:, 1, :, :].rearrange("p c n -> p (c n)"), wsrc("wupAg"), [], ["wupAg"])
            hTg = [sb(st, "t_hT%d" % i, [128, 8, GW], BF16) for i in range(2)]
            yg1 = sb(st, "t_y", [128, 4, GW], F32)
            yg = [yg1, yg1]
            obg = [sb(st, "t_ob%d" % i, [128, 4, GW], BF16) for i in range(2)]
            xg1 = sb(st, "t_xg", [128, 8, GW], F32)
            xg = [xg1, xg1]
            ysq = sb(st, "t_ysq", [128, 4, GW], BF16)
            lnr = [sb(st, "t_ln%d" % i, [128, GW], F32) for i in range(2)]
            rs = [sb(st, "t_rs%d" % i, [128, GW], F32) for i in range(2)]
            oag = sb(st, "t_oa", [128, 4, GW], BF16)
            sga = sb(st, "t_sga", [128, GW], F32)
            sgb = sb(st, "t_sgb", [128, GW], F32)
            m1 = sb(st, "t_m1", [128, GW], F32)
            m2 = sb(st, "t_m2", [128, GW], F32)
            mg = sb(st, "t_mg", [128, 8, GW], BF16)
            x1t = [sb(st, "t_x1%d" % i, [128, GW], F32) for i in range(2)]

            def t_load(ci):
                c0, n, _ = CHUNKS[ci]
                b = ci % 2
                dma("sync", hTg[b][:, :, 0:n], hq_v[:, :, c0:c0 + n], [], ["hTg%d" % b])
                dma("sync", obg[b][:, :, 0:n], obT_v[:, :, c0:c0 + n], [], ["obg%d" % b])

            def t_load_y(ci):
                c0, n, _ = CHUNKS[ci]
                dma("sync", yg1[:, :, 0:n], yT_v[:, :, c0:c0 + n], [], ["yg"])

            def t_load_x(ci):
                c0, n, _ = CHUNKS[ci]
                dma("sync", xg1[:, :, 0:n], xq_v[:, :, c0:c0 + n], [], ["xgT"])
            t_load(0)
            t_load_y(0)
            t_load_x(0)
            for ci, (c0, n, _) in enumerate(CHUNKS):
                b = ci % 2
                if ci + 1 < len(CHUNKS):
                    t_load(ci + 1)
                P.op("scalar", I("activation", out=ysq[:, :, 0:n], in_=yg[b][:, :, 0:n], func=AF.Square), ["yg"], ["ysq"])
                for h in range(4):
                    bank = 6 + h % 2
                    mm(B(bank, 0, n), on128, ysq[:, h, 0:n], True, True, ["ysq", "cmat"], [pk[bank]])
                    rstd_from(bank, n, lnr[h % 2], rs[h % 2], "trs%d" % (h % 2))
                    P.op("vector", I("scalar_tensor_tensor", out=oag[:, h, 0:n], in0=yg[b][:, h, 0:n], scalar=gsub[:], in1=rs[h % 2][:, 0:n],
                                     op0=ALU.mult, op1=ALU.mult), ["yg", "gsub", "trs%d" % (h % 2)], ["oag%d" % h])
                if ci + 1 < len(CHUNKS):
                    t_load_y(ci + 1)
                for d in range(8):
                    for c in range(8):
                        mm(B(2, 0, n), wg[:, d // 2, 0, c, 128 * (d % 2):128 * (d % 2) + 128], hTg[b][:, c, 0:n], c == 0, c == 7, ["wga%d" % (d // 2), "hTg%d" % b], [pk[2]])
                    for c in range(8):
                        mm(B(3, 0, n), wg[:, d // 2, 1, c, 128 * (d % 2):128 * (d % 2) + 128], hTg[b][:, c, 0:n], c == 0, c == 7, ["wgb%d" % (d // 2), "hTg%d" % b], [pk[3]])
                    for c in range(4):
                        mm(B(0, 0, n), wa[:, c, 128 * d:128 * d + 128], oag[:, c, 0:n], c == 0, c == 3, ["wa", "oag%d" % c], [pk[0]])
                    for c in range(4):
                        mm(B(1, 0, n), wb[:, c, 128 * d:128 * d + 128], obg[b][:, c, 0:n], c == 0, c == 3, ["wb", "obg%d" % b], [pk[1]])
                    P.op("scalar", I("activation", out=sga[:, 0:n], in_=B(2, 0, n), func=AF.Sigmoid, bias=bg[:, d:d + 1]), [pk[2], "bg"], ["sga"])
                    P.op("scalar", I("activation", out=sgb[:, 0:n], in_=B(3, 0, n), func=AF.Sigmoid, bias=bg[:, 8 + d:9 + d]), [pk[3], "bg"], ["sgb"])
                    P.op("vector", I("tensor_tensor", out=m1[:, 0:n], in0=B(0, 0, n), in1=sga[:, 0:n], op=ALU.mult), [pk[0], "sga"], ["m1"])
                    P.op("vector", I("tensor_tensor", out=m2[:, 0:n], in0=B(1, 0, n), in1=sgb[:, 0:n], op=ALU.mult), [pk[1], "sgb"], ["m2"])
                    P.op("gpsimd", I("tensor_tensor", out=mg[:, d, 0:n], in0=m1[:, 0:n], in1=m2[:, 0:n], op=ALU.add), ["m1", "m2"], ["mg%d" % d])
                for d in range(8):
                    bank = 4 + d % 2
                    for c in range(8):
                        mm(B(bank, 0, n), wo[:, c, 128 * d:128 * d + 128], mg[:, c, 0:n], c == 0, c == 7, ["wo", "mg%d" % c], [pk[bank]])
                    xo = x1t[d % 2]
                    P.op("vector", I("tensor_tensor", out=xo[:, 0:n], in0=B(bank, 0, n), in1=xg[b][:, d, 0:n], op=ALU.add),
                         [pk[bank], "xgT"], ["x1t%d" % (d % 2)])
                    dma(STQ, x1_v[:, d, c0:c0 + n], xo[:, 0:n], ["x1t%d" % (d % 2)], ["x1_d%d" % ci])
                if ci + 1 < len(CHUNKS):
                    t_load_x(ci + 1)
        P.barrier()

        if STOP == "3a":
            P.finalize()
            P.emit()
            return nc
        with ExitStack() as st:
            wupB = sb(st, "wupB", [128, 2, 8, 1408], BF16)
            dma("gpsimd", wupB[:, 0, :, :].rearrange("p c n -> p (c n)"), wsrc("wupBu"), [], ["wupBu"])
            dma("gpsimd", wupB[:, 1, :, :].rearrange("p c n -> p (c n)"), wsrc("wupBg"), [], ["wupBg"])

            def wu(fb, c):
                W = wupA if fb < 11 else wupB
                return W[:, 0, c, 128 * (fb % 11):128 * (fb % 11) + 128], ("wupAu" if fb < 11 else "wupBu")

            def wgt(fb, c):
                W = wupA if fb < 11 else wupB
                return W[:, 1, c, 128 * (fb % 11):128 * (fb % 11) + 128], ("wupAg" if fb < 11 else "wupBg")
            wd = [sb(st, "wd%d" % i, [128, NFB, 128], BF16) for i in range(3)]
            x1g = [sb(st, "f_x1g%d" % i, [128, 8, GW], F32) for i in range(2)]
            sq = sb(st, "f_sq", [128, 8, GW], BF16)
            lnr = sb(st, "f_ln", [128, GW], F32)
            rstd = sb(st, "f_rstd", [128, GW], F32)
            h2 = sb(st, "f_h2", [128, 8, GW], BF16)
            U = [sb(st, "f_U%d" % i, [128, GW + 2], F32) for i in range(2)]
            Uh = sb(st, "f_Uh", [128, NFB, 8], F32)
            cc_ = [sb(st, "f_c%d" % i, [128, GW], F32) for i in range(2)]
            ge = [sb(st, "f_ge%d" % i, [128, GW], F32) for i in range(2)]
            mT = sb(st, "f_mT", [128, NFB, GW], BF16)
            ot = [sb(st, "f_ot%d" % i, [128, GW], F32) for i in range(2)]
            order = [4, 0, 1, 2, 3]

            def f_load(oi):
                ci = order[oi]
                c0, n, _ = CHUNKS[ci]
                dma("sync", x1g[oi % 2][:, :, 0:n], x1_v[:, :, c0:c0 + n], [], ["x1g%d" % (oi % 2)])
            f_load(0)

            def f_rms_parts(oi_):
                ci_ = order[oi_]
                _, n_, _ = CHUNKS[ci_]
                b_ = oi_ % 2

                def p0():
                    P.op("scalar", I("activation", out=sq[:, :, 0:n_], in_=x1g[b_][:, :, 0:n_], func=AF.Square), ["x1g%d" % b_], ["fsq"])

                def p1():
                    for c in range(8):
                        mm(B(0, 0, n_), on1024, sq[:, c, 0:n_], c == 0, c == 7, ["fsq", "cmat"], [pk[0]])
                    rstd_from(0, n_, lnr, rstd, "frstd")

                def p2():
                    for c in range(8):
                        P.op("vector", I("scalar_tensor_tensor", out=h2[:, c, 0:n_], in0=x1g[b_][:, c, 0:n_], scalar=g2[:, c:c + 1],
                                         in1=rstd[:, 0:n_], op0=ALU.mult, op1=ALU.mult), ["x1g%d" % b_, "g2", "frstd"], ["h2"])
                return [p0, p1, p2]
            wdi = 0
            for oi, ci in enumerate(order):
                c0, n, _ = CHUNKS[ci]
                b = oi % 2
                if oi + 1 < len(order):
                    f_load(oi + 1)
                if oi == 0:
                    for f_ in f_rms_parts(0):
                        f_()
                if ci == 4:
                    for fb in range(NFB):
                        bank = 1 + fb % 2
                        for c in range(8):
                            mm(B(bank, 0, n), wu(fb, c)[0], h2[:, c, 0:n], c == 0, c == 7, [wu(fb, c)[1], "h2"], [pk[bank]])
                        P.op("scalar", I("activation", out=Uh[:, fb, :], in_=B(bank, 0, n), func=AF.Copy), [pk[bank]], ["Uh"])
                    if oi + 1 < len(order):
                        for f_ in f_rms_parts(oi + 1):
                            f_()
                    continue
                s = ci
                wl = []
                for d in range(3):
                    wl.append(dma("gpsimd", wd[(wdi + d) % 3][:].rearrange("p c n -> p (c n)"), wsrc("wd%d" % d), [], ["wd%d" % ((wdi + d) % 3)]))
                for fb in range(NFB):
                    bu, bgk = 1 + fb % 2, 3 + fb % 2
                    Ut, uk = U[fb % 2], "U%d" % (fb % 2)
                    ct, ck = cc_[fb % 2], "c%d" % (fb % 2)
                    gt, gk = ge[fb % 2], "ge%d" % (fb % 2)
                    for c in range(8):
                        mm(B(bu), wu(fb, c)[0], h2[:, c, :], c == 0, c == 7, [wu(fb, c)[1], "h2"], [pk[bu]])
                    for c in range(8):
                        mm(B(bgk), wgt(fb, c)[0], h2[:, c, :], c == 0, c == 7, [wgt(fb, c)[1], "h2"], [pk[bgk]])
                    P.op("scalar", I("activation", out=Ut[:, 2:GW + 2], in_=B(bu), func=AF.Copy), [pk[bu]], [uk])
                    P.op("vector", I("tensor_scalar", out=Ut[:, 0:2], in0=Uh[:, fb, 2 * s:2 * s + 2], scalar1=hflag[:, s:s + 1], scalar2=None, op0=ALU.mult),
                         ["Uh", "hflag"], [uk])
                    P.op("vector", I("tensor_scalar", out=ct[:], in0=Ut[:, 0:GW], scalar1=cw[:, fb, 0:1], scalar2=cb[:, fb:fb + 1],
                                     op0=ALU.mult, op1=ALU.add), [uk, "cw", "cb"], [ck])
                    P.op("vector", I("scalar_tensor_tensor", out=ct[:], in0=Ut[:, 1:GW + 1], scalar=cw[:, fb, 1:2], in1=ct[:],
                                     op0=ALU.mult, op1=ALU.add), [uk, "cw", ck], [ck])
                    P.op("vector", I("scalar_tensor_tensor", out=ct[:], in0=Ut[:, 2:GW + 2], scalar=cw[:, fb, 2:3], in1=ct[:],
                                     op0=ALU.mult, op1=ALU.add), [uk, "cw", ck], [ck])
                    P.op("scalar", I("activation", out=gt[:], in_=ct[:], func=AF.Gelu_apprx_tanh), [ck], [gk])
                    P.op("vector", I("tensor_tensor", out=mT[:, fb, :], in0=B(bgk), in1=gt[:], op=ALU.mult), [pk[bgk], gk], ["mT%d" % fb])
                nxt = f_rms_parts(oi + 1) if oi + 1 < len(order) else []
                if nxt:
                    nxt[0]()
                for d in range(8):
                    wt = wd[wdi % 3]
                    wkey = "wd%d" % (wdi % 3)
                    bank = 5 + d % 2
                    for fb in range(NFB):
                        mm(B(bank), wt[:, fb, :], mT[:, fb, :], fb == 0, fb == NFB - 1, [wkey, "mT%d" % fb], [pk[bank]])
                    if nxt and d == 0:
                        nxt[1]()
                    if nxt and d == 1:
                        nxt[2]()
                    o_ = ot[d % 2]
                    P.op("vector", I("tensor_tensor", out=o_[:], in0=B(bank), in1=x1g[b][:, d, :], op=ALU.add),
                         [pk[bank], "x1g%d" % b], ["ot%d" % (d % 2)])
                    dma(STQ, outT_v[:, d, c0:c0 + GW], o_[:], ["ot%d" % (d % 2)], ["out%d_%d" % (s, d)])
                    if d + 3 < 8:
                        dma("gpsimd", wd[wdi % 3][:].rearrange("p c n -> p (c n)"), wsrc("wd%d" % (d + 3)), [], [wkey])
                    wdi += 1

        P.finalize()
        P.emit()
    return nc


def own_positions(r):
    pos = np.zeros(NQ, np.int64)
    flag = np.ones(4, np.float32)
    for s in range(4):
        g = GROUPS[r][s]
        pos[512 * s:512 * s + 512] = 512 * g + np.arange(512)
        for i in range(2):
            p = 512 * g - 2 + i
            if p < 0:
                p = i
                flag[s] = 0.0
            pos[2048 + 2 * s + i] = p
    return pos, flag


def _consts(r):
    bf = ml_dtypes.bfloat16
    inv = (1.0 / (np.float32(10000.0) ** (np.arange(0, 64, 2, dtype=np.float32) / np.float32(64)))).astype(np.float32)
    ang = (np.arange(S, dtype=np.float32)[:, None] * inv[None, :]).astype(np.float32)
    cos = np.cos(ang).astype(np.float32)
    sin = np.sin(ang).astype(np.float32)
    p = np.arange(128)
    sgn = np.where((p % 64) < 32, -1.0, 1.0).astype(np.float32)
    cosk = np.ascontiguousarray(cos[:, p % 32].T)
    sink = np.ascontiguousarray((sin[:, p % 32] * sgn[None, :]).T)
    pos, flag = own_positions(r)
    cosq = np.ascontiguousarray(cosk[:, pos])
    sinq = np.ascontiguousarray(sink[:, pos])
    ident = np.eye(128, dtype=np.float32)
    sw = (p // 64) * 64 + ((p % 64) + 32) % 64
    perm = np.zeros((128, 128), np.float32)
    perm[sw, p] = 1.0
    bd64 = np.zeros((128, 128), np.float32)
    bd64[:64, :64] = 1.0 / 64
    bd64[64:, 64:] = 1.0 / 64
    cmat = np.concatenate([ident, perm, bd64, np.full((128, 128), 1.0 / 1024, np.float32),
                           np.full((128, 128), 1.0 / 128, np.float32), np.ones((128, 128), np.float32)], axis=1).astype(bf)
    k = np.arange(128)[:, None]
    am = np.zeros((4, 8, 128, GW), np.float32)
    for s, (c0, n, nkt) in enumerate(SLOTS):
        qpos = pos[c0:c0 + n][None, :]
        for jj in range(8):
            j = nkt - 8 + jj
            am[s, jj] = np.where(128 * j + k <= qpos, 0.0, -BIGM)
    assert np.array_equal(am[0], am[2]) and np.array_equal(am[1], am[3])
    amask = np.ascontiguousarray(am[0:2].transpose(2, 0, 1, 3).reshape(128, 2 * 8 * GW)).astype(bf)
    hpos = pos[2048:2056][None, :]
    hm = np.stack([np.where(128 * j + k <= hpos, 0.0, -BIGM) for j in range(32)], axis=1)
    hmask = np.ascontiguousarray(hm.reshape(128, 32 * 8)).astype(bf)
    gmask = np.zeros((128, 17, 16), np.float32)
    valid = np.zeros((128, 17, 16), np.float32)
    nidx = np.arange(16)[None, :]
    for qt in range(16):
        own = (pos[128 * qt:128 * qt + 128] // 256)[:, None]
        valid[:, qt, :] = (nidx < own)
    own = (pos[2048:2056] // 256)[:, None]
    valid[0:8, 16, :] = (nidx < own)
    gmask = np.where(valid > 0, 0.0, -1e30).astype(np.float32)
    oh = np.zeros((16, S), np.float32)
    for n_ in range(16):
        oh[n_, 256 * n_:256 * n_ + 256] = BIGB
    hflag = np.tile(flag[None, :], (128, 1)).astype(np.float32)
    return dict(cosk=cosk, sink=sink, cosq=cosq, sinq=sinq, cmat=cmat, amask=amask, hmask=hmask,
                gmask=np.ascontiguousarray(gmask.reshape(128, 17 * 16)), valid=np.ascontiguousarray(valid.reshape(128, 17 * 16)),
                hflag=hflag, oh16=oh.astype(bf)), pos


_NC_CACHE = {}


def kernel(x, norm1_g, w_in, b_gate, qn_a, kn_a, lam_q1, lam_k1, lam_q2, lam_k2, subln_g,
           qn_b, kn_b, w_a_proj, w_b_proj, w_out, norm2_g, w_up, conv_w, conv_b, w_down):
    f = np.float32
    x = np.asarray(x, f)
    Bn = x.shape[0]
    col = lambda v: np.ascontiguousarray(np.asarray(v, f).reshape(-1, 128).T)
    gn = np.stack([np.tile(np.asarray(v, f).reshape(64), 2) for v in (qn_a, kn_a, qn_b, kn_b)], axis=1)
    lamv = np.concatenate([np.tile(np.asarray(v, f).reshape(1, 64), (128, 1)) for v in (lam_q1, lam_k1, lam_q2, lam_k2)], axis=1)
    cwl = np.ascontiguousarray(np.asarray(conv_w, f)[0].T.reshape(NFB, 128, 3).transpose(1, 0, 2).reshape(128, NFB * 3))
    def pc(w, a, b_):
        w = np.asarray(w, f)
        C = w.shape[0] // 128
        return w[:, a:b_].reshape(C, 128, b_ - a).transpose(1, 0, 2).reshape(128, C * (b_ - a))
    Win, Wup, Wdn = np.asarray(w_in, f)[0], np.asarray(w_up, f)[0], np.asarray(w_down, f)[0]
    pieces = {"wk": pc(Win, 512, 1024), "wv": pc(Win, 1024, 1536), "wq": pc(Win, 0, 512),
              "wa": pc(np.asarray(w_a_proj, f)[0], 0, D), "wb": pc(np.asarray(w_b_proj, f)[0], 0, D), "wo": pc(np.asarray(w_out, f)[0], 0, D),
              "wupAu": pc(Wup, 0, 1408), "wupAg": pc(Wup, DFF, DFF + 1408), "wupBu": pc(Wup, 1408, DFF), "wupBg": pc(Wup, DFF + 1408, 2 * DFF)}
    for hp in range(2):
        pieces["wkm%d" % hp] = pc(Win, 2048 + 256 * hp, 2048 + 256 * hp + 256)
        pieces["wvm%d" % hp] = pc(Win, 2560 + 256 * hp, 2560 + 256 * hp + 256)
        pieces["wqm%d" % hp] = pc(Win, 1536 + 256 * hp, 1536 + 256 * hp + 256)
    for p_ in range(4):
        pieces["wga%d" % p_] = pc(Win, 3072 + 256 * p_, 3072 + 256 * p_ + 256)
        pieces["wgb%d" % p_] = pc(Win, 4096 + 256 * p_, 4096 + 256 * p_ + 256)
    for d_ in range(8):
        pieces["wd%d" % d_] = pc(Wdn, 128 * d_, 128 * d_ + 128)
    wpack = np.empty((128, WTOT), f)
    for nm, (o_, n_) in WOFF.items():
        assert pieces[nm].shape == (128, n_), (nm, pieces[nm].shape, n_)
        wpack[:, o_:o_ + n_] = pieces[nm]
    shared = {
        "wpack": wpack,
        "g1": col(norm1_g), "g2": col(norm2_g), "bg": col(b_gate),
        "cw": cwl, "cb": col(conv_b),
        "gn": np.ascontiguousarray(gn.astype(f)), "gsub": np.ascontiguousarray(np.asarray(subln_g, f).reshape(128, 1)),
        "lamv": np.ascontiguousarray(lamv.astype(f)),
    }
    cst = [_consts(r) for r in range(2)]
    in_maps = []
    for b in range(Bn):
        xTb = np.ascontiguousarray(x[b].T)
        for r in range(2):
            m = dict(shared)
            m.update(cst[r][0])
            m["xT"] = xTb
            m["xq"] = np.ascontiguousarray(xTb[:, cst[r][1]])
            in_maps.append(m)
    if "nc" not in _NC_CACHE:
        _NC_CACHE["nc"] = build_program()
    nc = _NC_CACHE["nc"]
    res = run_bass_kernel_spmd(nc, in_maps, core_ids=list(range(2 * Bn)))
    out = np.empty((Bn, S, D), f)
    for b in range(Bn):
        for r in range(2):
            o = np.asarray(res.results[2 * b + r]["outT"], f)
            out[b, cst[r][1][0:2048], :] = o.T
    if DEBUG:
        kernel.debug = res.results
    return out
```
